# Optimizing a Trainium2 kernel written in Bass

```python
import math
import jax, jax.numpy as jnp
from jax import lax
import numpy as np

D_MODEL = 1024
BATCH = 32
SEQ = 256
DEPTH = 2
DEC_BATCH = 2
DEC_SEQ = 1024
PAST_LEN = 256

GRID_W = 64
MLA_HEADS = 6
MLA_NOPE = 64
MLA_ROPE = 32
MLA_QK = MLA_NOPE + MLA_ROPE
MLA_V = 64
Q_LORA = 256
KV_LORA = 128
NA_HEADS = 6
NA_HD = 64
NA_KR = 8
NA_KW = 16
DF_HEADS = 4
DF_HD = 64
DF_QK = 32

MLA_W = MLA_HEADS * MLA_V
NA_W = NA_HEADS * NA_HD
DF_W = DF_HEADS * DF_HD
MIX_W = MLA_W + NA_W + DF_W
SPLIT_SIZES = (Q_LORA, KV_LORA, MLA_ROPE, NA_W, NA_W, NA_W, DF_W, DF_W, DF_W)
IN_COLS = Q_LORA + KV_LORA + MLA_ROPE + 3 * NA_W + 3 * DF_W
D_FF = -(-8 * D_MODEL // (3 * 256)) * 256
ROPE_BASE = 10000.0
EPS = 1e-6
Q_BLOCK = 128
NEG = -1e30

kernel_name = 'hybrid_mla_natten_diff_dit_step'


def _rms(x, g):
    xf = x.astype(jnp.float32)
    y = xf * lax.rsqrt(jnp.mean(xf * xf, axis=-1, keepdims=True) + EPS)
    return (y * g.astype(jnp.float32)).astype(x.dtype)


def _rms_pairs(x, g):
    s = x.shape
    return _rms(x.reshape(s[:-1] + (2, DF_QK)), g).reshape(s)


def _heads(t, n):
    b, l, _ = t.shape
    return t.reshape(b, l, n, -1).transpose(0, 2, 1, 3)


def _merge(t):
    b, h, l, d = t.shape
    return t.transpose(0, 2, 1, 3).reshape(b, l, h * d)


def _axial_rope(length, rot_dim):
    t = jnp.arange(length)
    row = (t // GRID_W).astype(jnp.float32)
    col = (t % GRID_W).astype(jnp.float32)
    n = rot_dim // 4
    inv = 1.0 / (ROPE_BASE ** (jnp.arange(n, dtype=jnp.float32) * 2.0 / (rot_dim // 2)))
    ar = row[:, None] * inv
    ac = col[:, None] * inv
    ang = jnp.concatenate([ar, ar, ac, ac], axis=-1)
    return jnp.cos(ang), jnp.sin(ang)


def _rotate_axial(x):
    xs = x.reshape(x.shape[:-1] + (2, 2, -1))
    x1, x2 = xs[..., 0, :], xs[..., 1, :]
    return jnp.stack([-x2, x1], axis=-2).reshape(x.shape)


def _apply_rope(x, cos, sin):
    xf = x.astype(jnp.float32)
    return (xf * cos + _rotate_axial(xf) * sin).astype(x.dtype)


def _rope_tail(x, cos, sin):
    return jnp.concatenate([x[..., :MLA_NOPE], _apply_rope(x[..., MLA_NOPE:], cos, sin)], axis=-1)


def _rope_pairs(x, cos, sin):
    s = x.shape
    xr = x.reshape(s[:-1] + (2, DF_QK))
    return _apply_rope(xr, cos[:, None], sin[:, None]).reshape(s)


def _over_query_blocks(fn, q):
    b, h, l, d = q.shape
    qb = math.gcd(l, Q_BLOCK)
    nb = l // qb
    blocks = q.reshape(b, h, nb, qb, d).transpose(2, 0, 1, 3, 4)
    out = lax.map(fn, blocks)
    return out.transpose(1, 2, 0, 3, 4).reshape(b, h, l, out.shape[-1])


def _softmax_attend(q, k, v, scale):
    def blk(qb):
        s = jnp.einsum('bhqd,bhkd->bhqk', qb, k).astype(jnp.float32) * scale
        p = jax.nn.softmax(s, axis=-1).astype(v.dtype)
        return jnp.einsum('bhqk,bhkd->bhqd', p, v)
    return _over_query_blocks(blk, q)


def _diff_lambda(lp, lam_init):
    f = lambda a: a.astype(jnp.float32)
    return (jnp.exp(jnp.sum(f(lp['df_lq1']) * f(lp['df_lk1'])))
            - jnp.exp(jnp.sum(f(lp['df_lq2']) * f(lp['df_lk2']))) + lam_init)


def _diff_attend(q, k, v, lp, lam_init):
    scale = DF_QK ** -0.5
    lam = _diff_lambda(lp, lam_init)
    k1, k2 = k[..., :DF_QK], k[..., DF_QK:]

    def blk(qb):
        s1 = jnp.einsum('bhqd,bhkd->bhqk', qb[..., :DF_QK], k1).astype(jnp.float32) * scale
        s2 = jnp.einsum('bhqd,bhkd->bhqk', qb[..., DF_QK:], k2).astype(jnp.float32) * scale
        p = jax.nn.softmax(s1, axis=-1) - lam * jax.nn.softmax(s2, axis=-1)
        return jnp.einsum('bhqk,bhkd->bhqd', p.astype(v.dtype), v)
    o = _over_query_blocks(blk, q)
    return _rms(o, lp['g_df_sub']) * (1.0 - lam_init)


def _neighbourhood_attend(q, k, v, k_ctx, v_ctx, rpb):
    b, h, L, d = q.shape
    rows = L // GRID_W
    kr = min(NA_KR, rows)
    kw = NA_KW
    r = jnp.arange(rows)
    cidx = jnp.arange(GRID_W)
    row_idx = jnp.clip(r - kr // 2, 0, rows - kr)[:, None] + jnp.arange(kr)[None, :]
    col_start = jnp.clip(cidx - kw // 2, 0, GRID_W - kw)
    in_win = (cidx[None, :] >= col_start[:, None]) & (cidx[None, :] < col_start[:, None] + kw)
    rel_r = row_idx - r[:, None] + (NA_KR - 1)
    rel_c = jnp.clip(cidx[None, :] - cidx[:, None], -(kw - 1), kw - 1) + (kw - 1)
    bias = rpb[:, rel_r[:, None, :, None], rel_c[None, :, None, :]]
    qg = q.reshape(b, h, rows, GRID_W, d)
    kg = k.reshape(b, h, rows, GRID_W, d)[:, :, row_idx]
    vg = v.reshape(b, h, rows, GRID_W, d)[:, :, row_idx]
    scale = d ** -0.5
    s_win = jnp.einsum('bhrqd,bhrjkd->bhrqjk', qg, kg).astype(jnp.float32) * scale + bias[None].astype(jnp.float32)
    s_win = jnp.where(in_win[:, None, :], s_win, NEG)
    s_ctx = jnp.einsum('bhrqd,bhcd->bhrqc', qg, k_ctx).astype(jnp.float32) * scale
    n_win = kr * GRID_W
    p = jax.nn.softmax(jnp.concatenate([s_win.reshape(b, h, rows, GRID_W, n_win), s_ctx], axis=-1), axis=-1).astype(v.dtype)
    p_win = p[..., :n_win].reshape(b, h, rows, GRID_W, kr, GRID_W)
    o = (jnp.einsum('bhrqjk,bhrjkd->bhrqd', p_win, vg)
         + jnp.einsum('bhrqc,bhcd->bhrqd', p[..., n_win:], v_ctx))
    return o.reshape(b, h, L, d)


def _modulation(cvec, w_mod, b_mod):
    m = jax.nn.silu(cvec) @ w_mod + b_mod
    return jnp.split(m[:, None, :], 6, axis=-1)


def _split_in(z):
    idx = np.cumsum(SPLIT_SIZES)[:-1].tolist()
    return jnp.split(z, idx, axis=-1)


def _mixer_front(x, mod, lp):
    sh, sc = mod[0], mod[1]
    h = _rms(x, lp['g_mix']) * (1.0 + sc) + sh
    cq, ckv, krope, na_q, na_k, na_v, df_q, df_k, df_v = _split_in(h @ lp['w_in'])
    q_mla = _rms(_heads(_rms(cq, lp['g_qa']) @ lp['w_uq'], MLA_HEADS), lp['g_mla_q'])
    ckv_n = _rms(ckv, lp['g_kva'])
    q_na = _rms(_heads(na_q, NA_HEADS), lp['g_na_q'])
    k_na = _rms(_heads(na_k, NA_HEADS), lp['g_na_k'])
    v_na = _heads(na_v, NA_HEADS)
    q_df = _rms_pairs(_heads(df_q, DF_HEADS), lp['g_df_q'])
    k_df = _rms_pairs(_heads(df_k, DF_HEADS), lp['g_df_k'])
    v_df = _heads(df_v, DF_HEADS)
    return q_mla, ckv_n, krope, q_na, k_na, v_na, q_df, k_df, v_df


def _mla_keys(ckv_n, krope, lp):
    kv = _heads(ckv_n @ lp['w_ukv'], MLA_HEADS)
    b, h, l, _ = kv.shape
    kr = jnp.broadcast_to(krope[:, None], (b, h, l, MLA_ROPE))
    k = _rms(jnp.concatenate([kv[..., :MLA_NOPE], kr], axis=-1), lp['g_mla_k'])
    return k, kv[..., MLA_NOPE:]


def _finish(x, o_mla, o_na, o_df, mod, lp):
    mix = jnp.concatenate([_merge(o_mla), _merge(o_na), _merge(o_df)], axis=-1) @ lp['w_out']
    x = x + mod[2] * mix
    h = _rms(x, lp['g_ffn']) * (1.0 + mod[4]) + mod[3]
    ffn = (jax.nn.silu(h @ lp['w_gate']) * (h @ lp['w_up'])) @ lp['w_down']
    return x + mod[5] * ffn


def _context_layer(x, mod, lp, lam_init):
    q_mla, ckv_n, krope, q_na, k_na, v_na, q_df, k_df, v_df = _mixer_front(x, mod, lp)
    k_mla, v_mla = _mla_keys(ckv_n, krope, lp)
    o_mla = _softmax_attend(q_mla, k_mla, v_mla, MLA_QK ** -0.5)
    o_na = _softmax_attend(q_na, k_na, v_na, NA_HD ** -0.5)
    o_df = _diff_attend(q_df, k_df, v_df, lp, lam_init)
    return _finish(x, o_mla, o_na, o_df, mod, lp), (ckv_n, krope, k_na, v_na, k_df, v_df)


def _latent_layer(x, mod, lp, lam_init, ctx):
    ckv_c, krope_c, k_na_c, v_na_c, k_df_c, v_df_c = ctx
    L = x.shape[1]
    cos_m, sin_m = _axial_rope(L, MLA_ROPE)
    cos_d, sin_d = _axial_rope(L, DF_QK)
    q_mla, ckv_n, krope, q_na, k_na, v_na, q_df, k_df, v_df = _mixer_front(x, mod, lp)
    q_mla = _rope_tail(q_mla, cos_m, sin_m)
    k_lat, v_lat = _mla_keys(ckv_n, krope, lp)
    k_lat = _rope_tail(k_lat, cos_m, sin_m)
    k_ctx, v_ctx = _mla_keys(ckv_c, krope_c, lp)
    o_mla = _softmax_attend(q_mla, jnp.concatenate([k_ctx, k_lat], axis=2),
                            jnp.concatenate([v_ctx, v_lat], axis=2), MLA_QK ** -0.5)
    o_na = _neighbourhood_attend(q_na, k_na, v_na, k_na_c, v_na_c, lp['na_rpb'])
    q_df = _rope_pairs(q_df, cos_d, sin_d)
    k_df = _rope_pairs(k_df, cos_d, sin_d)
    o_df = _diff_attend(q_df, jnp.concatenate([k_df_c, k_df], axis=2),
                        jnp.concatenate([v_df_c, v_df], axis=2), lp, lam_init)
    return _finish(x, o_mla, o_na, o_df, mod, lp)


def setup_inputs(seed: int = 0) -> dict:
    key = jax.random.key(seed)
    ks = jax.random.split(key, 36)
    f32 = jnp.float32
    nrm = lambda k, shape, s: jax.random.normal(k, shape, f32) * s
    gain = lambda k, shape: 1.0 + 0.05 * jax.random.normal(k, shape, f32)
    return {
        'x_prompt': nrm(ks[0], (BATCH, SEQ, D_MODEL), 1.0),
        'x_sample': nrm(ks[1], (DEC_BATCH, DEC_SEQ, D_MODEL), 1.0),
        'cache_mla_ckv': nrm(ks[2], (DEC_BATCH, DEPTH, PAST_LEN, KV_LORA), 1.0),
        'cache_mla_krope': nrm(ks[3], (DEC_BATCH, DEPTH, PAST_LEN, MLA_ROPE), 1.0),
        'cache_na_k': nrm(ks[4], (DEC_BATCH, DEPTH, NA_HEADS, PAST_LEN, NA_HD), 1.0),
        'cache_na_v': nrm(ks[5], (DEC_BATCH, DEPTH, NA_HEADS, PAST_LEN, NA_HD), 1.0),
        'cache_df_k': nrm(ks[6], (DEC_BATCH, DEPTH, DF_HEADS, PAST_LEN, 2 * DF_QK), 1.0),
        'cache_df_v': nrm(ks[7], (DEC_BATCH, DEPTH, DF_HEADS, PAST_LEN, DF_HD), 1.0),
        'c': nrm(ks[8], (DEC_BATCH, D_MODEL), 1.0),
        'c_ctx': nrm(ks[9], (D_MODEL,), 1.0),
        'w_mod': nrm(ks[10], (DEPTH, D_MODEL, 6 * D_MODEL), 0.5 * D_MODEL ** -0.5),
        'b_mod': nrm(ks[11], (DEPTH, 6 * D_MODEL), 0.01),
        'g_mix': gain(ks[12], (DEPTH, D_MODEL)),
        'w_in': nrm(ks[13], (DEPTH, D_MODEL, IN_COLS), D_MODEL ** -0.5),
        'g_qa': gain(ks[14], (DEPTH, Q_LORA)),
        'w_uq': nrm(ks[15], (DEPTH, Q_LORA, MLA_HEADS * MLA_QK), Q_LORA ** -0.5),
        'g_kva': gain(ks[16], (DEPTH, KV_LORA)),
        'w_ukv': nrm(ks[17], (DEPTH, KV_LORA, MLA_HEADS * (MLA_NOPE + MLA_V)), KV_LORA ** -0.5),
        'g_mla_q': gain(ks[18], (DEPTH, MLA_QK)),
        'g_mla_k': gain(ks[19], (DEPTH, MLA_QK)),
        'g_na_q': gain(ks[20], (DEPTH, NA_HD)),
        'g_na_k': gain(ks[21], (DEPTH, NA_HD)),
        'na_rpb': nrm(ks[22], (DEPTH, NA_HEADS, 2 * NA_KR - 1, 2 * NA_KW - 1), 0.1),
        'g_df_q': gain(ks[23], (DEPTH, DF_QK)),
        'g_df_k': gain(ks[24], (DEPTH, DF_QK)),
        'df_lq1': nrm(ks[25], (DEPTH, DF_QK), 0.1),
        'df_lk1': nrm(ks[26], (DEPTH, DF_QK), 0.1),
        'df_lq2': nrm(ks[27], (DEPTH, DF_QK), 0.1),
        'df_lk2': nrm(ks[28], (DEPTH, DF_QK), 0.1),
        'g_df_sub': gain(ks[29], (DEPTH, DF_HD)),
        'w_out': nrm(ks[30], (DEPTH, MIX_W, D_MODEL), MIX_W ** -0.5),
        'g_ffn': gain(ks[31], (DEPTH, D_MODEL)),
        'w_gate': nrm(ks[32], (DEPTH, D_MODEL, D_FF), D_MODEL ** -0.5),
        'w_up': nrm(ks[33], (DEPTH, D_MODEL, D_FF), D_MODEL ** -0.5),
        'w_down': nrm(ks[34], (DEPTH, D_FF, D_MODEL), D_FF ** -0.5),
    }


def reference(x_prompt, x_sample, cache_mla_ckv, cache_mla_krope, cache_na_k, cache_na_v,
              cache_df_k, cache_df_v, c, c_ctx, w_mod, b_mod, g_mix, w_in, g_qa, w_uq, g_kva,
              w_ukv, g_mla_q, g_mla_k, g_na_q, g_na_k, na_rpb, g_df_q, g_df_k, df_lq1, df_lk1,
              df_lq2, df_lk2, g_df_sub, w_out, g_ffn, w_gate, w_up, w_down):
    xp = x_prompt
    xs = x_sample
    new = [[], [], [], [], [], []]
    for l in range(DEPTH):
        lp = {'g_mix': g_mix[l], 'w_in': w_in[l], 'g_qa': g_qa[l], 'w_uq': w_uq[l],
              'g_kva': g_kva[l], 'w_ukv': w_ukv[l], 'g_mla_q': g_mla_q[l], 'g_mla_k': g_mla_k[l],
              'g_na_q': g_na_q[l], 'g_na_k': g_na_k[l], 'na_rpb': na_rpb[l],
              'g_df_q': g_df_q[l], 'g_df_k': g_df_k[l], 'df_lq1': df_lq1[l], 'df_lk1': df_lk1[l],
              'df_lq2': df_lq2[l], 'df_lk2': df_lk2[l], 'g_df_sub': g_df_sub[l],
              'w_out': w_out[l], 'g_ffn': g_ffn[l], 'w_gate': w_gate[l], 'w_up': w_up[l],
              'w_down': w_down[l]}
        lam_init = 0.8 - 0.6 * math.exp(-0.3 * l)
        mod_ctx = _modulation(c_ctx[None, :], w_mod[l], b_mod[l])
        mod_lat = _modulation(c, w_mod[l], b_mod[l])
        xp, ctx_new = _context_layer(xp, mod_ctx, lp, lam_init)
        for lst, t in zip(new, ctx_new):
            lst.append(t)
        ctx_cached = (cache_mla_ckv[:, l], cache_mla_krope[:, l], cache_na_k[:, l],
                      cache_na_v[:, l], cache_df_k[:, l], cache_df_v[:, l])
        xs = _latent_layer(xs, mod_lat, lp, lam_init, ctx_cached)
    new_mla_ckv = jnp.stack(new[0], axis=1)
    new_mla_krope = jnp.stack(new[1], axis=1)
    new_na_k = jnp.stack(new[2], axis=1)
    new_na_v = jnp.stack(new[3], axis=1)
    new_df_k = jnp.stack(new[4], axis=1)
    new_df_v = jnp.stack(new[5], axis=1)
    return (xp, xs, new_mla_ckv, new_mla_krope, new_na_k, new_na_v, new_df_k, new_df_v)
```

```python
import math
import os
import numpy as np
import ml_dtypes
import concourse.bass as bass
import concourse.mybir as mybir
from concourse.bass_utils import run_bass_kernel_spmd

F32 = mybir.dt.float32
BF16 = mybir.dt.bfloat16
ALU = mybir.AluOpType
AF = mybir.ActivationFunctionType

D = 1024
DEPTH = 2
NGRP = 5
N = 256
NTOK = NGRP * N
EPS = 1e-6
GRID_W = 64
IN_COLS = 2336
DFF = 2816
NFF = DFF // 128
EXR = 11 * 128 + 1024
NEGB = -30000.0
LAM_INIT = [0.8 - 0.6 * math.exp(-0.3 * l) for l in range(DEPTH)]
FFN_BLOCKS = [(0, 4), (4, 4), (8, 4), (12, 4), (16, 4), (20, 2)]
NSLOT = 8
C_CQ0, C_CQ1, C_CKV, C_KR = 0, 128, 256, 384
C_NAQ, C_NAK, C_NAV = 416, 800, 1184
C_DFQ, C_DFK, C_DFV = 1568, 1824, 2080
WIN_SLOTS = [[(0, 416)], [(416, 512)], [(928, 256), (2080, 256)], [(1184, 512)], [(1696, 384)]]
GCOLS = 26


class KB:
    def __init__(self, nc):
        self.nc = nc
        self.ops = {e: [] for e in ("pe", "act", "dve", "pool", "sp")}
        self.cnt = {e: 0 for e in ("pe", "act", "dve", "pool")}
        self.engsem = {}
        self.dsem = {}
        self.dcnt = {}
        self.lastw = {}
        self.readers = {}
        self.waited = {e: {} for e in self.ops}
        self.sem_objs = []
        self.final = {}
        self.barrier_ev = None
        self.enabled = True

    def barrier(self, fn):
        if not self.enabled:
            return
        need = {}
        for e_, c in self.cnt.items():
            if c > 0:
                need["E:" + e_] = c
        for s_, v in self.dcnt.items():
            if s_.startswith("D:ring"):
                continue
            need[s_] = v
        waits = []
        for s_, v in need.items():
            if self.waited["dve"].get(s_, 0) < v:
                self.waited["dve"][s_] = v
                waits.append((s_, v))
        self.cnt["dve"] += 1
        self.barrier_ev = ("E:dve", self.cnt["dve"])
        self.ops["dve"].append((waits, fn, "E:dve", 1))

    def _deps(self, eng, reads, writes):
        need = {}
        if self.barrier_ev is not None:
            need[self.barrier_ev[0]] = self.barrier_ev[1]
        for k in reads:
            ev = self.lastw.get(k)
            if ev is not None:
                need[ev[0]] = max(need.get(ev[0], 0), ev[1])
        for k in writes:
            ev = self.lastw.get(k)
            if ev is not None:
                need[ev[0]] = max(need.get(ev[0], 0), ev[1])
            for ev in self.readers.get(k, ()):
                need[ev[0]] = max(need.get(ev[0], 0), ev[1])
        waits = []
        for s, v in need.items():
            if eng == "pe" and s == "E:pe":
                continue
            if self.waited[eng].get(s, 0) < v:
                self.waited[eng][s] = v
                waits.append((s, v))
        return waits

    def _commit(self, ev, reads, writes):
        for k in reads:
            self.readers.setdefault(k, []).append(ev)
        for k in writes:
            self.lastw[k] = ev
            self.readers[k] = []

    def op(self, eng, fn, reads=(), writes=()):
        if not self.enabled:
            return
        waits = self._deps(eng, reads, writes)
        self.cnt[eng] += 1
        ev = ("E:" + eng, self.cnt[eng])
        self.ops[eng].append((waits, fn, ev[0], 1))
        self._commit(ev, reads, writes)

    def dma(self, q, semname, out, in_, reads=(), writes=(), final=False):
        if not self.enabled:
            return
        waits = self._deps(q, reads, writes)
        s = "D:" + semname
        self.dcnt[s] = self.dcnt.get(s, 0) + 16
        ev = (s, self.dcnt[s])
        self.ops[q].append((waits, (lambda e, o=out, i=in_: e.dma_start(out=o, in_=i)), s, 16))
        self._commit(ev, reads, writes)
        if final:
            self.final[s] = self.dcnt[s]

    def coll(self, semname, fn, reads=(), writes=()):
        if not self.enabled:
            return
        waits = self._deps("pool", reads, writes)
        s = "C:" + semname
        self.dcnt[s] = self.dcnt.get(s, 0) + 1
        ev = (s, self.dcnt[s])
        self.ops["pool"].append((waits, fn, s, 1))
        self._commit(ev, reads, writes)

    def all_sems(self):
        names = set()
        for e, lst in self.ops.items():
            for waits, fn, s, amt in lst:
                names.add(s)
                for (ws, v) in waits:
                    names.add(ws)
        return sorted(names)


class _Stop(Exception):
    pass


def build_program(debug_taps=None):
    nc = bass.Bass("TRN2", target_bir_lowering=False)
    kb = KB(nc)
    KSTOP = int(os.environ.get("KSTOP", "1000"))
    KSKIP = os.environ.get("KSKIP", "").split(",")
    KSUB = int(os.environ.get("KSUB", "1000"))

    def sub(i, g):
        if g == 0 and i > KSUB:
            kb.enabled = False
    stg = {"i": 0}

    marks = []

    def stage(name=""):
        marks.append((name, kb.cnt["pe"], kb.cnt["act"], kb.cnt["dve"]))
        stg["i"] += 1
        if stg["i"] > KSTOP:
            kb.enabled = False

    def din(name, shape, dt=F32):
        return nc.dram_tensor(name, list(shape), dt, kind="ExternalInput").ap()

    def dout(name, shape, dt=F32):
        return nc.dram_tensor(name, list(shape), dt, kind="ExternalOutput").ap()

    xin = din("xin", [NGRP, N, D])
    cT_d = din("cT", [128, 16])
    wmod_d = din("wmod", [DEPTH, D, 1536])
    bmod_d = din("bmod", [DEPTH, 1, 1536])
    w_in_d = din("w_in", [DEPTH, D, IN_COLS])
    w_uq_d = din("w_uq", [DEPTH, 256, 576])
    w_ukv_d = din("w_ukv", [DEPTH, 128, 768])
    w_out_d = din("w_out", [DEPTH, D, D])
    w_gate_d = din("w_gate", [DEPTH, D, DFF])
    w_up_d = din("w_up", [DEPTH, D, DFF])
    w_down_d = din("w_down", [DEPTH, DFF, D])
    gains_d = din("gains", [128, DEPTH * GCOLS + 2])
    lamp_d = din("lamp", [1, DEPTH * 4 * 32])
    ident_d = din("ident", [128, 128])
    rmat_d = din("rmat", [128, 2 * 128])
    rope_d = din("rope", [128, 4 * N])
    bones_d = din("bones", [128, 3 * 128])
    biasm_d = din("biasm", [DEPTH, 6, 1024, N])
    c_ckv_d = din("c_ckv", [DEPTH, 256, 128])
    c_kr_d = din("c_kr", [DEPTH, 256, 32])
    c_nak_d = din("c_nak", [DEPTH, 6, 256, 64])
    c_nav_d = din("c_nav", [DEPTH, 6, 256, 64])
    c_dfk_d = din("c_dfk", [DEPTH, 4, 256, 64])
    c_dfv_d = din("c_dfv", [DEPTH, 4, 256, 64])

    y_d = dout("y", [NGRP, N, D])
    o_ckv = dout("o_ckv", [4, DEPTH, 256, 128])
    o_kr = dout("o_kr", [4, DEPTH, 256, 32])
    o_nak = dout("o_nak", [4, DEPTH, 6, 256, 64])
    o_nav = dout("o_nav", [4, DEPTH, 6, 256, 64])
    o_dfk = dout("o_dfk", [4, DEPTH, 4, 256, 64])
    o_dfv = dout("o_dfv", [4, DEPTH, 4, 256, 64])

    mx_in = nc.dram_tensor("mx_in", [128, 48], F32)
    mx_out = nc.dram_tensor("mx_out", [512, 48], F32)
    exk_in = [nc.dram_tensor(f"exk_in{l}", [1408, N], BF16) for l in range(DEPTH)]
    exk_out = [nc.dram_tensor(f"exk_out{l}", [4 * 1408, N], BF16) for l in range(DEPTH)]
    exv_in = [nc.dram_tensor(f"exv_in{l}", [1024, N], BF16) for l in range(DEPTH)]
    exv_out = [nc.dram_tensor(f"exv_out{l}", [4 * 1024, N], BF16) for l in range(DEPTH)]
    RG = [[0, 1, 2, 3], [4, 5, 6, 7]]

    dbg_outs = []

    from contextlib import ExitStack
    es = ExitStack()

    def sb(name, shape, dt=F32):
        return es.enter_context(nc.sbuf_tensor("s_" + name, list(shape), dt))

    with es:
        xT = sb("xT", [128, 8, NTOK])
        ring = [sb(f"ring{i}", [128, 4096], BF16) for i in range(NSLOT)]
        ident = sb("ident", [128, 128])
        rmat = sb("rmat", [128, 256])
        ropet = sb("ropet", [128, 4 * N])
        bones = sb("bones", [128, 3 * 128], BF16)
        ones_b = sb("ones_b", [128, 128], BF16)
        ones_f = sb("ones_f", [128, 2])
        gains = sb("gains", [128, DEPTH * GCOLS + 2])
        lamt = sb("lamt", [128, 16])
        nlam = sb("nlam", [128, DEPTH])
        cT = sb("cT", [128, 16])
        scT = sb("scT", [128, 16])
        modS = sb("modS", [128, 48])
        modT = sb("modT", [128, 4, 48])
        mv = sb("mv", [128, DEPTH, 2, 8, 8])
        epsb = sb("epsb", [128, 1])
        UB = 64 * 1024
        U = sb("U", [128, UB // 2], BF16)
        carve = {"o": 0}

        def uview(shape, dt):
            n = 1
            for d_ in shape[1:]:
                n *= d_
            nb = n * (4 if dt == F32 else 2)
            o = carve["o"]
            assert o % 4 == 0 and o + nb <= UB, (o, nb)
            carve["o"] = o + nb
            v = U[:, o // 2:(o + nb) // 2]
            if dt == F32:
                v = v.bitcast(F32)
            if len(shape) == 3:
                v = v.rearrange("p (a b) -> p a b", a=shape[1])
            elif len(shape) == 4:
                v = v.rearrange("p (a b c) -> p a b c", a=shape[1], b=shape[2])
            return v
        hT = uview([128, 8, N], BF16)
        mixT = hT
        QT = uview([128, 11, N], BF16)
        KT = uview([128, 11, N], BF16)
        VP = uview([128, 16, 192], BF16)
        KTc = uview([128, 11, N], BF16)
        VPc = uview([128, 16, 192], BF16)
        KTst = [uview([128, 1024], BF16) for i in range(2)]
        QTm = uview([128, 4, N], BF16)
        VPst = [uview([128, 8, 192], BF16) for i in range(2)]
        VP4 = VP.rearrange("p b (s c) -> p b s c", s=3)
        VPc4 = VPc.rearrange("p b (s c) -> p b s c", s=3)
        VPst4 = [v_.rearrange("p b (s c) -> p b s c", s=3) for v_ in VPst]
        Ebuf = [uview([128, 8, N], BF16) for i in range(2)]
        Est = [uview([128, 2, N], F32) for i in range(2)]
        cst2 = uview([128, 2, 128], F32)
        _o = carve["o"]
        vstage = uview([128, 640], F32)
        kstage = uview([128, 7, 128], F32)
        _o2 = carve["o"]
        carve["o"] = _o
        cst = uview([128, 2, 128], F32)
        cstk = uview([128, 2, 6, 64], F32)
        cstd = uview([128, 2, 4, 64], F32)
        carve["o"] = max(_o2, carve["o"])
        mixer_bytes = carve["o"]
        carve["o"] = 0
        h2T = uview([128, 8, NTOK], BF16)
        aT = [uview([128, 4, 2 * N], BF16) for i in range(2)]
        silt = [uview([128, 2 * N], F32) for i in range(2)]
        carve["o"] = 0
        xstage = [uview([128, 2, D], F32)]
        wst = [uview([128, 1536], F32) for i in range(2)]
        mrow = uview([128, 1536], F32)
        bmrow = uview([128, 1536], F32)
        lamp = uview([128, DEPTH * 4 * 32], F32)
        NT = 3
        tsq = [sb(f"tsq{i}", [128, N], BF16) for i in range(NT)]
        tf = [sb(f"tf{i}", [128, N]) for i in range(NT)]
        tr = [sb(f"tr{i}", [128, N]) for i in range(NT)]
        tg = [sb(f"tg{i}", [128, N]) for i in range(NT)]
        rstd_x = sb("rstd_x", [128, N])
        ckvn_f = sb("ckvn_f", [128, N])
        ckvn_b = sb("ckvn_b", [128, N], BF16)
        krT = sb("krT", [128, N])
        cqn = sb("cqn", [128, 2, N], BF16)
        knf = sb("knf", [128, 5, N])
        odf = sb("odf", [128, N])
        PT = [sb(f"PT{i}", [128, 512], BF16) for i in range(3)]
        rc = [sb(f"rc{i}", [128, N]) for i in range(2)]
        ps = [es.enter_context(nc.psum_tensor(f"ps{i}", [128, 512], F32)) for i in range(8)]

        cnt = {"t": 0, "S": 0, "O": 0, "A": 0, "B": 0, "pt": 0, "rc": 0, "ring": 0}
        POOLS = {"S": (0, 1), "O": (2, 3), "A": (4, 5), "B": (6, 7)}

        def bank(pool):
            b = POOLS[pool][cnt[pool] % 2]
            cnt[pool] += 1
            return b

        def tmp(lst, nm):
            i = cnt["t"] % NT
            return lst[i], (nm, i)

        def nexttmp():
            cnt["t"] += 1

        def mm(out, lhsT, rhs, start, stop, reads, writes):
            kb.op("pe", lambda e: e.matmul(out, lhsT=lhsT, rhs=rhs, start=start, stop=stop), reads, writes)

        def tp(out, in_, idn, reads, writes):
            kb.op("pe", lambda e: e.transpose(out, in_, idn), reads, writes)

        def act(out, in_, func, reads, writes, scale=1.0, bias=None, accum=None):
            def f(e):
                kw = {}
                if bias is not None:
                    kw["bias"] = bias
                if accum is not None:
                    kw["accum_out"] = accum
                return e.activation(out=out, in_=in_, func=func, scale=scale, **kw)
            kb.op("act", f, reads, writes)

        def stt(out, in0, scalar, in1, op0, op1, reads, writes, eng="dve", accum=None):
            def f(e):
                if accum is not None:
                    return e.scalar_tensor_tensor(out=out, in0=in0, scalar=scalar, in1=in1, op0=op0, op1=op1, accum_out=accum)
                return e.scalar_tensor_tensor(out=out, in0=in0, scalar=scalar, in1=in1, op0=op0, op1=op1)
            kb.op(eng, f, reads, writes)

        def tt(out, in0, in1, op, reads, writes, eng="dve"):
            kb.op(eng, lambda e: e.tensor_tensor(out=out, in0=in0, in1=in1, op=op), reads, writes)

        def ts(out, in0, s1, s2, op0, op1, reads, writes, eng="dve"):
            if op1 is None:
                kb.op(eng, lambda e: e.tensor_scalar(out=out, in0=in0, scalar1=s1, scalar2=None, op0=op0), reads, writes)
            else:
                kb.op(eng, lambda e: e.tensor_scalar(out=out, in0=in0, scalar1=s1, scalar2=s2, op0=op0, op1=op1), reads, writes)

        def cp(out, in_, reads, writes, eng="dve"):
            kb.op(eng, lambda e: e.tensor_copy(out=out, in_=in_), reads, writes)

        def recip(out, in_, reads, writes):
            kb.op("dve", lambda e: e.reciprocal(out=out, in_=in_), reads, writes)

        def memset(ap, val, writes, eng="dve"):
            kb.op(eng, lambda e: e.memset(ap, val), (), writes)

        def dbg(name, ap, shape, reads, dt=F32):
            if debug_taps is None or name not in debug_taps:
                return
            t = nc.dram_tensor("dbg_" + name, list(shape), dt, kind="ExternalOutput").ap()
            kb.dma("sp", "dbg_" + name, t, ap, reads=reads, writes=[("dbgout", name)], final=True)
            dbg_outs.append(name)

        kb.dma("sp", "c_ident", ident[:], ident_d, writes=["ident"])
        kb.dma("sp", "c_gains", gains[:], gains_d, writes=["gains"])
        kb.dma("sp", "c_cT", cT[:], cT_d, writes=["cT"])
        kb.dma("sp", "c_rmat", rmat[:], rmat_d, writes=["rmat"])
        kb.dma("sp", "c_rope", ropet[:], rope_d, writes=["ropet"])
        kb.dma("pool", "c_bones", bones[:], bones_d, writes=["bones"])
        kb.dma("sp", "c_lamp", lamp[:], lamp_d[0].partition_broadcast(128), writes=["lamp"])
        memset(ones_b[:], 1.0, ["ones_b"])
        memset(ones_f[:], 1.0, ["ones_f"])
        memset(epsb[:], EPS, ["epsb"])
        memset(krT[:], 0.0, ["krT"])
        dummy = sb("dummy", [128, 2])

        def barrier():
            kb.barrier(lambda e: e.memset(dummy[:], 0.0))

        def mixer_init():
            memset(VP[:, :, 64:128], 1.0, ["VPones"])
            memset(VPc[:, :, 64:128], 1.0, ["VPcones"])
            for i in range(2):
                memset(VPst[i][:, :, 64:128], 1.0, [("VPstones", i)])
            memset(cst2[:], 0.0, ["cst2"])

        BO2 = bones[:, 0:128]
        BO4 = bones[:, 128:256]
        BO96 = bones[:, 256:384]

        def gcol(l, j, w=1):
            return gains[:, l * GCOLS + j: l * GCOLS + j + w]
        G_MIX, G_FFN, G_QA, G_KVA, G_MQ, G_MK, G_NQ, G_NK, G_DQ, G_DK, G_DS = 0, 8, 16, 18, 19, 20, 21, 22, 23, 24, 25

        ring_last = [0] * NSLOT
        prog = {"i": 0}

        def ring_free():
            return sum(1 for v in ring_last if v < 10 ** 6)

        def ring_alloc():
            i = min(range(NSLOT), key=lambda s: ring_last[s])
            assert ring_last[i] < 10 ** 6, "weight ring exhausted"
            prog["i"] += 1
            ring_last[i] = prog["i"] + 10 ** 6
            return i

        def ring_touch(i):
            prog["i"] += 1
            ring_last[i] = prog["i"]

        def load_w(slot, pieces):
            for (dst, src) in pieces:
                kb.dma("pool", f"ring{slot}", dst, src, writes=[("ring", slot)])

        class WS:
            pass

        def mixer_loader(l):
            w = WS()
            w.win = []
            w.wout = []
            w.colmap = {}
            steps = []

            def st_small():
                s = ring_alloc()
                w.small = s
                r = ring[s]
                load_w(s, [(r[:, 0:1152].rearrange("p (k c) -> p k c", k=2), w_uq_d[l].rearrange("(k p) c -> p k c", p=128)),
                           (r[:, 1152:1536].rearrange("p (h c) -> p h c", h=6), w_ukv_d[l].rearrange("p (h c) -> p h c", h=6)[:, :, 0:64]),
                           (r[:, 1536:1920].rearrange("p (h c) -> p h c", h=6), w_ukv_d[l].rearrange("p (h c) -> p h c", h=6)[:, :, 64:128])])
            steps.append(st_small)
            wv = w_in_d[l].rearrange("(k p) c -> p k c", p=128)

            def mk_win(pieces):
                def f():
                    s = ring_alloc()
                    w.win.append(s)
                    tot = sum(nc_ for (_, nc_) in pieces)
                    view = ring[s][:, 0:8 * tot].rearrange("p (k c) -> p k c", k=8)
                    off = 0
                    pl = []
                    for (c0, ncol) in pieces:
                        pl.append((view[:, :, off:off + ncol], wv[:, :, c0:c0 + ncol]))
                        w.colmap[c0] = (s, view, off, ncol)
                        off += ncol
                    load_w(s, pl)
                return f
            for pieces in WIN_SLOTS:
                steps.append(mk_win(pieces))
            wo = w_out_d[l].rearrange("(k p) c -> p k c", p=128)

            def mk_wout(j):
                def f():
                    s = ring_alloc()
                    w.wout.append(s)
                    view = ring[s][:, :].rearrange("p (k c) -> p k c", k=8)
                    load_w(s, [(view, wo[:, :, j * 512:(j + 1) * 512])])
                return f
            for j in range(2):
                steps.append(mk_wout(j))
            return w, steps

        def win_ap(w, col, width, k):
            for c0, (s, view, off, ncol) in w.colmap.items():
                if c0 <= col and col + width <= c0 + ncol:
                    return view[:, k, off + col - c0: off + col - c0 + width], ("ring", s), s
            raise KeyError(col)

        def load_ffn_block(l, bi):
            c0, ncnk = FFN_BLOCKS[bi]
            w = WS()
            w.n = ncnk
            cols = ncnk * 128
            w.g = ring_alloc()
            vg = ring[w.g][:, 0:8 * cols].rearrange("p (k c) -> p k c", k=8)
            load_w(w.g, [(vg, w_gate_d[l].rearrange("(k p) c -> p k c", p=128)[:, :, c0 * 128:c0 * 128 + cols])])
            w.u = ring_alloc()
            vu = ring[w.u][:, 0:8 * cols].rearrange("p (k c) -> p k c", k=8)
            load_w(w.u, [(vu, w_up_d[l].rearrange("(k p) c -> p k c", p=128)[:, :, c0 * 128:c0 * 128 + cols])])
            w.d = ring_alloc()
            vd = ring[w.d][:, 0:ncnk * 1024].rearrange("p (c f) -> p c f", c=ncnk)
            load_w(w.d, [(vd, w_down_d[l][c0 * 128:c0 * 128 + cols, :].rearrange("(c p) f -> p c f", p=128))])
            w.vg, w.vu, w.vd = vg, vu, vd
            return w

        mixw, _steps = mixer_loader(0)
        for _f in _steps:
            _f()

        stage("lambda")
        for l in range(DEPTH):
            for j in range(2):
                a = lamp[:, (l * 4 + 2 * j) * 32:(l * 4 + 2 * j + 1) * 32]
                b = lamp[:, (l * 4 + 2 * j + 1) * 32:(l * 4 + 2 * j + 2) * 32]
                stt(tf[0][:, 0:32], a, 1.0, b, ALU.mult, ALU.mult,
                    ["lamp"], [("tf", 0), ("lamt", l, j)], accum=lamt[:, l * 2 + j:l * 2 + j + 1])
            act(lamt[:, 4 + l * 2:4 + l * 2 + 2], lamt[:, l * 2:l * 2 + 2], AF.Exp, [("lamt", l, 0), ("lamt", l, 1)], [("lame", l)])
            tt(lamt[:, 8 + l:9 + l], lamt[:, 5 + l * 2:6 + l * 2], lamt[:, 4 + l * 2:5 + l * 2], ALU.subtract, [("lame", l)], [("lamd", l)])
            ts(nlam[:, l:l + 1], lamt[:, 8 + l:9 + l], -LAM_INIT[l], None, ALU.add, None, [("lamd", l)], [("nlam", l)])

        stage("xT")
        def emit_xT(g):
            st_ = 0
            kb.dma("sp", f"xstage{st_}", xstage[st_][:], xin[g].rearrange("(t p) d -> p t d", p=128), writes=[("xstage", st_)])
            for k in range(8):
                b = bank("O")
                for t in range(2):
                    tp(ps[b][:, t * 128:(t + 1) * 128], xstage[st_][:, t, k * 128:(k + 1) * 128], ident[:],
                       [("xstage", st_), "ident"], [("ps", b)])
                if k % 2:
                    cp(xT[:, k, g * N:(g + 1) * N], ps[b][:, 0:N], [("ps", b)], [("xT", g, k)])
                else:
                    act(xT[:, k, g * N:(g + 1) * N], ps[b][:, 0:N], AF.Copy, [("ps", b)], [("xT", g, k)])

        stage("mod")
        act(scT[:], cT[:], AF.Silu, ["cT"], ["scT"])
        XT_AT = {(0, 0): 0, (0, 3): 1, (0, 6): 2, (1, 1): 3, (1, 4): 4}
        for l in range(DEPTH):
            banks = [bank("A"), bank("B"), bank("S")]
            kb.dma("sp", "c_bm", bmrow[0:1, :], bmod_d[l], writes=["bmrow"])
            for k in range(8):
                if (l, k) in XT_AT:
                    emit_xT(XT_AT[(l, k)])
                wsl = (l * 8 + k) % 2
                kb.dma("sp", f"wst{wsl}", wst[wsl][:], wmod_d[l, k * 128:(k + 1) * 128, :], writes=[("wst", wsl)])
                for j in range(3):
                    mm(ps[banks[j]][0:2, :], scT[:, 2 * k:2 * k + 2], wst[wsl][:, j * 512:(j + 1) * 512], k == 0, False,
                       ["scT", ("wst", wsl)], [("ps", banks[j])])
            for j in range(3):
                mm(ps[banks[j]][0:2, :], ones_f[0:1, 0:2], bmrow[0:1, j * 512:(j + 1) * 512], False, True,
                   ["ones_f", "bmrow"], [("ps", banks[j])])
                cp(mrow[0:2, j * 512:(j + 1) * 512], ps[banks[j]][0:2, :], [("ps", banks[j])], [("mrow", j)])
            bt = bank("B")
            for jb in range(12):
                tp(ps[bt][:, jb * 2:jb * 2 + 2], mrow[0:2, jb * 128:(jb + 1) * 128], ident[0:2, 0:2],
                   [("mrow", jb // 4), "ident"], [("ps", bt)])
            cp(modS[:, l * 24:(l + 1) * 24], ps[bt][:, 0:24], [("ps", bt)], ["modS"])
        stage("modgather")
        kb.dma("sp", "mx", mx_in.ap(), modS[:], reads=["modS"], writes=["mx_in"])
        kb.coll("mx", lambda e: e.collective_compute("AllGather", ALU.bypass, replica_groups=RG,
                                                     ins=[mx_in.ap().opt()], outs=[mx_out.ap().opt()]),
                reads=["mx_in"], writes=["mx_out"])
        kb.dma("sp", "mxb", modT[:], mx_out.ap().rearrange("(r p) c -> p r c", p=128), reads=["mx_out"], writes=["modT"])
        for l in range(DEPTH):
            for cnd in range(2):
                for r in range(4):
                    cp(mv[:, l, cnd, 0:6, :].rearrange("p a b -> p (a b)")[:, r * 12:(r + 1) * 12],
                       modT[:, r, l * 24 + cnd:l * 24 + 24:2], ["modT"], [("mvraw", l, cnd)])
                stt(mv[:, l, cnd, 6, :], mv[:, l, cnd, 1, :], 1.0, gcol(l, G_MIX, 8), ALU.add, ALU.mult,
                    [("mvraw", l, cnd), "gains"], [("mv", l, cnd)])
                stt(mv[:, l, cnd, 7, :], mv[:, l, cnd, 4, :], 1.0, gcol(l, G_FFN, 8), ALU.add, ALU.mult,
                    [("mvraw", l, cnd), "gains"], [("mv", l, cnd)])

        def MV(l, cnd, kind, k):
            return mv[:, l, cnd, kind, k:k + 1]
        K_SH1, K_GATE1, K_SH2, K_GATE2, K_G1, K_G2 = 0, 2, 3, 5, 6, 7

        def xkeys(g):
            return [("xT", g, k) for k in range(8)]

        def norm_mod(l, g, kG, kSH, dst, dst_key):
            cnd = 1 if g == 4 else 0
            T = slice(g * N, (g + 1) * N)
            b = bank("B")
            for k in range(8):
                sq_, ksq = tmp(tsq, "tsq")
                act(sq_[:], xT[:, k, T], AF.Square, [("xT", g, k)], [ksq])
                mm(ps[b][:, 0:N], ones_b[:], sq_[:], k == 0, k == 7, [ksq, "ones_b"], [("ps", b)])
                nexttmp()
            t1, k1 = tmp(tf, "tf")
            act(t1[:], ps[b][:, 0:N], AF.Ln, [("ps", b), "epsb"], [k1], scale=1.0 / D, bias=epsb[:, 0:1])
            act(rstd_x[:], t1[:], AF.Exp, [k1], ["rstd_x"], scale=-0.5)
            nexttmp()
            for k in range(8):
                t2, k2 = tmp(tg, "tg")
                stt(t2[:], xT[:, k, T], MV(l, cnd, kG, k), rstd_x[:], ALU.mult, ALU.mult,
                    [("xT", g, k), ("mv", l, cnd), "rstd_x"], [k2])
                act(dst(k), t2[:], AF.Identity, [k2, ("mvraw", l, cnd)], [dst_key(k)], bias=MV(l, cnd, kSH, k))
                nexttmp()

        def headnorm(pb, M, d, bo, gain, outs, extra_reads=(), src=None, src_key=None):
            srcap = ps[pb][0:M, 0:N] if src is None else src
            skey = ("ps", pb) if src_key is None else src_key
            sq_, ksq = tmp(tsq, "tsq")
            act(sq_[0:M, :], srcap, AF.Square, [skey], [ksq])
            b2 = bank("B")
            mm(ps[b2][0:M, 0:N], bo[0:M, 0:M], sq_[0:M, :], True, True, [ksq, "bones"], [("ps", b2)])
            t1, k1 = tmp(tf, "tf")
            act(t1[0:M, :], ps[b2][0:M, 0:N], AF.Ln, [("ps", b2), "epsb"], [k1], scale=1.0 / d, bias=epsb[0:M, 0:1])
            r_, kr_ = tmp(tr, "tr")
            act(r_[0:M, :], t1[0:M, :], AF.Exp, [k1], [kr_], scale=-0.5)
            for (oap, okey) in outs:
                stt(oap, srcap, gain, r_[0:M, :], ALU.mult, ALU.mult, [skey, kr_, "gains"] + list(extra_reads), [okey])
            nexttmp()

        def rope(src_f, M, which, dst, reads, writes):
            ro = 0 if which == "mla" else 2
            R = rmat[0:M, 0:M] if which == "mla" else rmat[0:M, 128:128 + M]
            b = bank("B")
            mm(ps[b][0:M, 0:N], R, src_f, True, True, list(reads) + ["rmat"], [("ps", b)])
            t1, k1 = tmp(tf, "tf")
            tt(t1[0:M, :], ps[b][0:M, 0:N], ropet[0:M, (ro + 1) * N:(ro + 2) * N], ALU.mult, [("ps", b), "ropet"], [k1])
            t2, k2 = tmp(tg, "tg")
            tt(t2[0:M, :], src_f, ropet[0:M, ro * N:(ro + 1) * N], ALU.mult, list(reads) + ["ropet"], [k2])
            tt(dst, t1[0:M, :], t2[0:M, :], ALU.add, [k1, k2], writes)
            nexttmp()

        def mla_k_from(l, w, ckvT_b, ckv_key, kr_f, kr_key, sample_rope, dstKT, dst_key):
            rsm = ring[w.small]
            for h in range(6):
                ba = bank("A")
                mm(ps[ba][0:64, 0:N], rsm[:, 1152 + h * 64:1152 + h * 64 + 64], ckvT_b, True, True,
                   [("ring", w.small), ckv_key], [("ps", ba)])
                ring_touch(w.small)
                sq_, ksq = tmp(tsq, "tsq")
                act(sq_[0:64, :], ps[ba][0:64, 0:N], AF.Square, [("ps", ba)], [ksq])
                act(sq_[64:96, :], kr_f[64:96, :], AF.Square, [kr_key], [ksq])
                b2 = bank("B")
                mm(ps[b2][0:96, 0:N], BO96[0:96, 0:96], sq_[0:96, :], True, True, [ksq, "bones"], [("ps", b2)])
                t1, k1 = tmp(tf, "tf")
                act(t1[0:96, :], ps[b2][0:96, 0:N], AF.Ln, [("ps", b2), "epsb"], [k1], scale=1.0 / 96, bias=epsb[0:96, 0:1])
                r_, kr_ = tmp(tr, "tr")
                act(r_[0:96, :], t1[0:96, :], AF.Exp, [k1], [kr_], scale=-0.5)
                if not sample_rope:
                    stt(dstKT(h)[0:64, :], ps[ba][0:64, 0:N], gcol(l, G_MK)[0:64, :], r_[0:64, :], ALU.mult, ALU.mult,
                        [("ps", ba), kr_, "gains"], [dst_key(h)])
                    stt(dstKT(h)[64:96, :], kr_f[64:96, :], gcol(l, G_MK)[64:96, :], r_[64:96, :], ALU.mult, ALU.mult,
                        [kr_key, kr_, "gains"], [dst_key(h)])
                    nexttmp()
                else:
                    nexttmp()
                    kf, kfk = knf[:, 4, :], ("knf", 4)
                    stt(kf[0:64, :], ps[ba][0:64, 0:N], gcol(l, G_MK)[0:64, :], r_[0:64, :], ALU.mult, ALU.mult,
                        [("ps", ba), kr_, "gains"], [kfk])
                    stt(kf[64:96, :], kr_f[64:96, :], gcol(l, G_MK)[64:96, :], r_[64:96, :], ALU.mult, ALU.mult,
                        [kr_key, kr_, "gains"], [kfk])
                    rope(kf[0:96, :], 96, "mla", dstKT(h)[0:96, :], [kfk], [dst_key(h)])

        tsq2 = sb("tsqx", [128, N], BF16)
        P4 = (4, 5, 0, 1)
        pcnt = {"p": 0, "j": 0}

        def bank4():
            b_ = P4[pcnt["p"] % 4]
            pcnt["p"] += 1
            return b_

        def run_norm_pipeline(jobs):
            n = len(jobs)
            for i in range(min(2, n)):
                jobs[i]["P"]()
                jobs[i]["Q"]()
            for i in range(n):
                jobs[i]["R"]()
                jobs[i]["T"]()
                if i + 2 < n:
                    jobs[i + 2]["P"]()
                    jobs[i + 2]["Q"]()
                jobs[i]["U"]()

        def mk_job(P, M, d, bo, U, sq_extra=None, nop_norm=False, R_custom=None):
            st = {}
            j = pcnt["j"]
            pcnt["j"] += 1
            ti = j % NT
            sq_, ksq = tsq[ti], ("tsq", ti)
            t1, k1 = tf[ti], ("tf", ti)
            r_, kr_ = tr[ti], ("tr", ti)
            st.update(sq=sq_, ksq=ksq, r=r_, kr=kr_, ti=ti)

            def P_():
                P(st)

            def Q_():
                if nop_norm:
                    return
                pb = st["pb"]
                rows = st.get("rows", M)
                act(sq_[0:rows, :], ps[pb][0:rows, 0:N], AF.Square, [("ps", pb)], [ksq])
                if sq_extra is not None:
                    sq_extra(st)

            def R_():
                if nop_norm:
                    return
                if R_custom is not None:
                    R_custom(st)
                    return
                b2 = bank("B")
                st["b2"] = b2
                mm(ps[b2][0:M, 0:N], bo[0:M, 0:M], sq_[0:M, :], True, True, [ksq, "bones", "ones_b"], [("ps", b2)])

            def T_():
                if nop_norm:
                    return
                b2 = st["b2"]
                act(t1[0:M, :], ps[b2][0:M, 0:N], AF.Ln, [("ps", b2), "epsb"], [k1], scale=1.0 / d, bias=epsb[0:M, 0:1])
                act(r_[0:M, :], t1[0:M, :], AF.Exp, [k1], [kr_], scale=-0.5)

            def U_():
                U(st)
            return {"P": P_, "Q": Q_, "R": R_, "T": T_, "U": U_}

        def mk_mla_k_job(l, w, h, ckvT_b, ckv_key, kr_f, kr_key, sample_rope, dstKT, dst_key):
            rsm_ = ring[w.small]

            def P(st):
                pb = bank4()
                st["pb"] = pb
                st["rows"] = 64
                mm(ps[pb][0:64, 0:N], rsm_[:, 1152 + h * 64:1152 + h * 64 + 64], ckvT_b, True, True,
                   [("ring", w.small), ckv_key], [("ps", pb)])
                ring_touch(w.small)

            def sqx(st):
                act(st["sq"][64:96, :], kr_f[64:96, :], AF.Square, [kr_key], [st["ksq"]])

            def U(st):
                pb, r_, kr_ = st["pb"], st["r"], st["kr"]
                if not sample_rope:
                    d0, dk = dstKT(h), dst_key(h)
                else:
                    d0, dk = knf[:, 4, :], ("knf", 4)
                stt(d0[0:64, :], ps[pb][0:64, 0:N], gcol(l, G_MK)[0:64, :], r_[0:64, :], ALU.mult, ALU.mult,
                    [("ps", pb), kr_, "gains"], [dk])
                stt(d0[64:96, :], kr_f[64:96, :], gcol(l, G_MK)[64:96, :], r_[64:96, :], ALU.mult, ALU.mult,
                    [kr_key, kr_, "gains"], [dk])
                if sample_rope:
                    rope(d0[0:96, :], 96, "mla", dstKT(h)[0:96, :], [dk], [dst_key(h)])
            return mk_job(P, 96, 96, BO96, U, sq_extra=sqx)

        def front(l, g, w, part="all"):
            do_q = part in ("all", "q")
            do_kv = part in ("all", "kv")
            sample = (g == 4)
            norm_mod(l, g, K_G1, K_SH1, lambda k: hT[:, k, :], lambda k: ("hT", k))
            rsm = ring[w.small]

            def proj(col, M, pb, po=0):
                for k in range(8):
                    wap, wkey, s_ = win_ap(w, col, M, k)
                    mm(ps[pb][po:po + M, 0:N], wap, hT[:, k, :], k == 0, k == 7, [wkey, ("hT", k)], [("ps", pb)])
                    ring_touch(s_)

            def chunk_job(col, M, d, bo, gain, outs, post=None):
                def P(st):
                    st["pb"] = bank4()
                    proj(col, M, st["pb"])

                def U(st):
                    pb, r_, kr_ = st["pb"], st["r"], st["kr"]
                    for (oap, okey) in outs:
                        stt(oap, ps[pb][0:M, 0:N], gain, r_[0:M, :], ALU.mult, ALU.mult, [("ps", pb), kr_, "gains"], [okey])
                    if post is not None:
                        post()
                return mk_job(P, M, d, bo, U)

            jobs = []
            if do_q:
                def P_cq(st):
                    st["pb"] = bank4()
                    st["pb1"] = bank4()
                    proj(C_CQ0, 128, st["pb"])
                    proj(C_CQ1, 128, st["pb1"])

                def sq_cq(st):
                    act(tsq2[:], ps[st["pb1"]][:, 0:N], AF.Square, [("ps", st["pb1"])], ["tsq2"])

                def U_cq(st):
                    r_, kr_ = st["r"], st["kr"]
                    stt(cqn[:, 0, :], ps[st["pb"]][:, 0:N], gcol(l, G_QA), r_[:], ALU.mult, ALU.mult, [("ps", st["pb"]), kr_, "gains"], [("cqn", 0)])
                    stt(cqn[:, 1, :], ps[st["pb1"]][:, 0:N], gcol(l, G_QA + 1), r_[:], ALU.mult, ALU.mult, [("ps", st["pb1"]), kr_, "gains"], [("cqn", 1)])
                def R_cq(st):
                    b2 = bank("B")
                    st["b2"] = b2
                    mm(ps[b2][:, 0:N], ones_b[:], st["sq"][:], True, False, [st["ksq"], "ones_b"], [("ps", b2)])
                    mm(ps[b2][:, 0:N], ones_b[:], tsq2[:], False, True, ["tsq2", "ones_b"], [("ps", b2)])
                jobs.append(mk_job(P_cq, 128, 256, ones_b, U_cq, sq_extra=sq_cq, R_custom=R_cq))
            if do_kv:
                jobs.append(chunk_job(C_CKV, 128, 128, ones_b, gcol(l, G_KVA), [(ckvn_f[:], "ckvn_f"), (ckvn_b[:], "ckvn_b")]))

                def P_kr(st):
                    st["pb"] = bank4()
                    proj(C_KR, 32, st["pb"], po=64)

                def U_kr(st):
                    cp(krT[64:96, :], ps[st["pb"]][64:96, 0:N], [("ps", st["pb"])], ["krT"])
                jobs.append(mk_job(P_kr, 32, 32, ones_b, U_kr, nop_norm=True))
            if do_q:
                for c in range(3):
                    jobs.append(chunk_job(C_NAQ + c * 128, 128, 64, BO2, gcol(l, G_NQ), [(QT[:, 6 + c, :], ("QT", 6 + c))]))
                uq = rsm[:, 0:1152].rearrange("p (k c) -> p k c", k=2)
                for h in range(6):
                    def P_mq(st, h=h):
                        st["pb"] = bank4()
                        for k2 in range(2):
                            mm(ps[st["pb"]][0:96, 0:N], uq[:, k2, h * 96:(h + 1) * 96], cqn[:, k2, :], k2 == 0, k2 == 1,
                               [("ring", w.small), ("cqn", k2)], [("ps", st["pb"])])
                        ring_touch(w.small)

                    def U_mq(st, h=h):
                        pb, r_, kr_ = st["pb"], st["r"], st["kr"]
                        if not sample:
                            stt(QT[0:96, h, :], ps[pb][0:96, 0:N], gcol(l, G_MQ)[0:96, :], r_[0:96, :], ALU.mult, ALU.mult,
                                [("ps", pb), kr_, "gains"], [("QT", h)])
                        else:
                            stt(knf[0:96, 3, :], ps[pb][0:96, 0:N], gcol(l, G_MQ)[0:96, :], r_[0:96, :], ALU.mult, ALU.mult,
                                [("ps", pb), kr_, "gains"], [("knf", 3)])
                            rope(knf[0:96, 3, :], 96, "mla", QT[0:96, h, :], [("knf", 3)], [("QT", h)])
                    jobs.append(mk_job(P_mq, 96, 96, BO96, U_mq))
            if do_kv:
                for c in range(3):
                    outs = [(KT[:, 6 + c, :], ("KT", 6 + c))]
                    if not sample:
                        outs.append((knf[:, c, :], ("knf", c)))
                    jobs.append(chunk_job(C_NAK + c * 128, 128, 64, BO2, gcol(l, G_NK), outs))
                for h in range(6):
                    jobs.append(mk_mla_k_job(l, w, h, ckvn_b[:], "ckvn_b", krT, "krT", sample,
                                             lambda h_: KT[:, h_, :], lambda h_: ("KT", h_)))
            if do_q:
                for c in range(2):
                    def post_q(c=c):
                        if sample:
                            rope(knf[:, 3, :], 128, "df", QT[:, 9 + c, :], [("knf", 3)], [("QT", 9 + c)])
                        for m_ in range(2):
                            ts(QTm[:, 2 * c + m_, :], QT[:, 9 + c, :], gains[:, DEPTH * GCOLS + m_:DEPTH * GCOLS + m_ + 1], None, ALU.mult, None,
                               [("QT", 9 + c), "gains"], [("QTm", 2 * c + m_)])
                    outs = [(QT[:, 9 + c, :], ("QT", 9 + c))] if not sample else [(knf[:, 3, :], ("knf", 3))]
                    jobs.append(chunk_job(C_DFQ + c * 128, 128, 32, BO4, gcol(l, G_DQ), outs, post=post_q))
            if do_kv:
                for c in range(2):
                    if not sample:
                        jobs.append(chunk_job(C_DFK + c * 128, 128, 32, BO4, gcol(l, G_DK),
                                              [(KT[:, 9 + c, :], ("KT", 9 + c)), (knf[:, 3 + c, :], ("knf", 3 + c))]))
                    else:
                        def post_k(c=c):
                            rope(knf[:, 3, :], 128, "df", KT[:, 9 + c, :], [("knf", 3)], [("KT", 9 + c)])
                        jobs.append(chunk_job(C_DFK + c * 128, 128, 32, BO4, gcol(l, G_DK), [(knf[:, 3, :], ("knf", 3))], post=post_k))
            run_norm_pipeline(jobs)
            if do_kv:
                vsrc = rsm[:, 1536:1920]
                for t in range(2):
                    bv = bank("A")
                    mm(ps[bv][:, 0:384], ckvn_b[:, t * 128:(t + 1) * 128], vsrc, True, True,
                       [("ring", w.small), "ckvn_b"], [("ps", bv)])
                    ring_touch(w.small)
                    cp(VP4[:, t * 8:t * 8 + 3, 0:3:2, :], ps[bv][:, 0:384].rearrange("p (q a c) -> p q a c", q=3, a=2),
                       [("ps", bv)], [("VP", t, 0)])
            sub(8, g)
            for t in range(2 if do_kv else 0):
                b1_, b2_ = bank("A"), bank("A")
                for k in range(8):
                    wap, wkey, s = win_ap(w, C_NAV, 384, k)
                    mm(ps[b1_][:, 0:384], hT[:, k, t * 128:(t + 1) * 128], wap, k == 0, k == 7, [wkey, ("hT", k)], [("ps", b1_)])
                    ring_touch(s)
                for k in range(8):
                    wap, wkey, s = win_ap(w, C_DFV, 256, k)
                    mm(ps[b2_][:, 0:256], hT[:, k, t * 128:(t + 1) * 128], wap, k == 0, k == 7, [wkey, ("hT", k)], [("ps", b2_)])
                    ring_touch(s)
                if not sample:
                    act(vstage[:, 0:384], ps[b1_][:, 0:384], AF.Copy, [("ps", b1_)], [("vstage", 0)])
                    act(vstage[:, 384:640], ps[b2_][:, 0:256], AF.Copy, [("ps", b2_)], [("vstage", 1)])
                if sample:
                    cp(VP4[:, t * 8 + 3:t * 8 + 6, 0:3:2, :], ps[b1_][:, 0:384].rearrange("p (q a c) -> p q a c", q=3, a=2),
                       [("ps", b1_)], [("VP", t, 1)])
                    cp(VP4[:, t * 8 + 6:t * 8 + 8, 0:3:2, :], ps[b2_][:, 0:256].rearrange("p (q a c) -> p q a c", q=2, a=2),
                       [("ps", b2_)], [("VP", t, 2)])
                else:
                    cp(VP4[:, t * 8 + 3:t * 8 + 6, 0:3:2, :], vstage[:, 0:384].rearrange("p (q a c) -> p q a c", q=3, a=2),
                       [("vstage", 0)], [("VP", t, 1)])
                    cp(VP4[:, t * 8 + 6:t * 8 + 8, 0:3:2, :], vstage[:, 384:640].rearrange("p (q a c) -> p q a c", q=2, a=2),
                       [("vstage", 1)], [("VP", t, 2)])
                if not sample and "vo" not in KSKIP:
                    rv = [("vstage", 0), ("vstage", 1)]
                    kb.dma("sp", "o_nav", o_nav[g, l][:, t * 128:(t + 1) * 128, :].rearrange("h p c -> p h c"),
                           vstage[:, 0:384].rearrange("p (h c) -> p h c", h=6), reads=rv, writes=[("out", "nav", g, l, t)], final=True)
                    kb.dma("sp", "o_dfv", o_dfv[g, l][:, t * 128:(t + 1) * 128, :].rearrange("h p c -> p h c"),
                           vstage[:, 384:640].rearrange("p (h c) -> p h c", h=4), reads=rv, writes=[("out", "dfv", g, l, t)], final=True)

        def prompt_outputs(l, g):
            srcs = [(ckvn_f, None, "ckvn_f"), (krT, None, "krT")] + [(knf, c, ("knf", c)) for c in range(5)]
            for t in range(2):
                for half in range(2):
                    b = bank("B")
                    lst = srcs[0:4] if half == 0 else srcs[4:7]
                    for j, (tile_, c, key) in enumerate(lst):
                        src = tile_[:, t * 128:(t + 1) * 128] if c is None else tile_[:, c, t * 128:(t + 1) * 128]
                        tp(ps[b][:, j * 128:(j + 1) * 128], src, ident[:], [key, "ident"], [("ps", b)])
                    n = len(lst)
                    j0 = 0 if half == 0 else 4
                    if half == 0:
                        act(kstage[:, j0:j0 + n, :], ps[b][:, 0:n * 128].rearrange("p (j c) -> p j c", j=n), AF.Copy,
                            [("ps", b)], [("kstage", half)])
                    else:
                        cp(kstage[:, j0:j0 + n, :], ps[b][:, 0:n * 128].rearrange("p (j c) -> p j c", j=n),
                           [("ps", b)], [("kstage", half)])
                rk = [("kstage", 0), ("kstage", 1)]
                kb.dma("sp", "o_ckv", o_ckv[g, l][t * 128:(t + 1) * 128, :], kstage[:, 0, :], reads=rk, writes=[("out", "ckv", g, l, t)], final=True)
                kb.dma("sp", "o_kr", o_kr[g, l][t * 128:(t + 1) * 128, :], kstage[:, 1, 64:96], reads=rk, writes=[("out", "kr", g, l, t)], final=True)
                kb.dma("sp", "o_nak", o_nak[g, l][:, t * 128:(t + 1) * 128, :].rearrange("h p c -> p h c"),
                       kstage[:, 2:5, :].rearrange("p j (a c) -> p (j a) c", a=2), reads=rk, writes=[("out", "nak", g, l, t)], final=True)
                kb.dma("sp", "o_dfk", o_dfk[g, l][:, t * 128:(t + 1) * 128, :].rearrange("h p c -> p h c"),
                       kstage[:, 5:7, :].rearrange("p j (a c) -> p (j a) c", a=2), reads=rk, writes=[("out", "dfk", g, l, t)], final=True)

        def head_maps(hh):
            if hh < 6:
                return [(hh, 0, 96, 96 ** -0.5, None)]
            if hh < 12:
                j = hh - 6
                return [(6 + j // 2, (j % 2) * 64, 64, 0.125, None)]
            j = hh - 12
            return [(9 + j // 2, (j % 2) * 64, 64, 32 ** -0.5, 2 * (j // 2) + m) for m in range(2)]

        def vblock(tile_, base, c, hh, nh):
            blk = c * 8 + hh // 2 if nh == 16 else c
            if hh % 2 == 0:
                return tile_[:, blk, 0:128]
            return tile_[:, blk, 64:192]

        def attn_norm(hh, ob, dst, dkey, rdkeys):
            i = cnt["rc"] % 2
            cnt["rc"] += 1
            lo, hi = (0, 64) if hh % 2 == 0 else (64, 128)
            dl, dh = (64, 128) if hh % 2 == 0 else (0, 64)
            act(rc[i][lo:hi, :], ps[ob][dl:dh, 0:N], AF.Ln, [("ps", ob)], [("rc", i)])
            act(rc[i][lo:hi, :], rc[i][lo:hi, :], AF.Exp, [("rc", i)], [("rc", i)], scale=-1.0)
            tt(dst[lo:hi, :], ps[ob][lo:hi, 0:N], rc[i][lo:hi, :], ALU.mult, [("ps", ob), ("rc", i)] + rdkeys, [dkey])

        def df_finish(l, pi):
            sq_, ksq = tmp(tsq, "tsq")
            act(sq_[:], odf[:], AF.Square, ["odf0", "odf1"], [ksq])
            b2 = bank("B")
            mm(ps[b2][:, 0:N], BO2, sq_[:], True, True, [ksq, "bones"], [("ps", b2)])
            t1, k1 = tmp(tf, "tf")
            act(t1[:], ps[b2][:, 0:N], AF.Ln, [("ps", b2), "epsb"], [k1], scale=1.0 / 64, bias=epsb[:, 0:1])
            r_, kr_ = tmp(tr, "tr")
            act(r_[:], t1[:], AF.Exp, [k1], [kr_], scale=-0.5)
            t2, k2 = tmp(tg, "tg")
            stt(t2[:], odf[:], gcol(l, G_DS), r_[:], ALU.mult, ALU.mult, ["odf0", "odf1", kr_, "gains"], [k2])
            ts(mixT[:, pi, :], t2[:], 1.0 - LAM_INIT[l], None, ALU.mult, None, [k2], [("hT", pi)])
            nexttmp()

        def run_pipeline(jobs):
            n = len(jobs)
            if not n:
                return
            jobs[0][0]()
            if n > 1:
                jobs[1][0]()
            jobs[0][1]()
            for i in range(n):
                jobs[i][2]()
                if i + 2 < n:
                    jobs[i + 2][0]()
                if i + 1 < n:
                    jobs[i + 1][1]()
                jobs[i][3]()

        def attention_prompt(l, g):
            jobs = []
            for hh in range(16):
                pi = hh // 2
                maps = head_maps(hh)
                obs = []
                for mi, (qc, pb, kr, scale, qm) in enumerate(maps):
                    st = {}

                    def S_(st=st, qc=qc, pb=pb, kr=kr, qm=qm):
                        sbk = bank("S")
                        st["sbk"] = sbk
                        qap = QT[pb:pb + kr, qc, :] if qm is None else QTm[pb:pb + kr, qm, :]
                        qkey = ("QT", qc) if qm is None else ("QTm", qm)
                        for kc in range(2):
                            mm(ps[sbk][:, kc * N:(kc + 1) * N], KT[pb:pb + kr, qc, kc * 128:(kc + 1) * 128], qap,
                               True, True, [("KT", qc), qkey], [("ps", sbk)])

                    def E_(st=st, scale=scale):
                        pi_ = cnt["pt"] % 3
                        cnt["pt"] += 1
                        st["pt"] = pi_
                        act(PT[pi_][:], ps[st["sbk"]][:, :], AF.Exp, [("ps", st["sbk"])], [("PT", pi_)], scale=scale)

                    def PV_(st=st, hh=hh, obs=obs):
                        ob = bank("O")
                        pi_ = st["pt"]
                        grp = 0 if hh < 6 else (1 if hh < 12 else 2)
                        for kc in range(2):
                            mm(ps[ob][:, 0:N], vblock(VP, 1, kc, hh, 16), PT[pi_][:, kc * N:(kc + 1) * N], kc == 0, kc == 1,
                               [("VP", kc, grp), "VPones", ("PT", pi_)], [("ps", ob)])
                        obs.append(ob)

                    def POST_(hh=hh, pi=pi, obs=obs, last=(mi == len(maps) - 1)):
                        if not last:
                            return
                        if hh < 12:
                            attn_norm(hh, obs[0], mixT[:, pi, :], ("hT", pi), [])
                        else:
                            df_combine(l, hh, obs)
                            if hh % 2 == 1:
                                df_finish(l, pi)
                    jobs.append((S_, E_, PV_, POST_))
            run_pipeline(jobs)

        def df_combine(l, hh, obs):
            lo = 0 if hh % 2 == 0 else 64
            t1, k1 = tmp(tf, "tf")
            attn_norm(hh, obs[0], t1, k1, [])
            t2, k2 = tmp(tg, "tg")
            attn_norm(hh, obs[1], t2, k2, [])
            stt(odf[lo:lo + 64, :], t2[lo:lo + 64, :], nlam[lo:lo + 64, l:l + 1], t1[lo:lo + 64, :], ALU.mult, ALU.add,
                [k1, k2, ("nlam", l)], ["odf%d" % (hh % 2)])
            nexttmp()

        def out_proj(l, g, w):
            cnd = 1 if g == 4 else 0
            T = slice(g * N, (g + 1) * N)
            for f in range(8):
                ba = bank("A")
                s = w.wout[f // 4]
                view = ring[s][:, :].rearrange("p (k c) -> p k c", k=8)
                for k in range(8):
                    mm(ps[ba][:, 0:N], view[:, k, (f % 4) * 128:(f % 4 + 1) * 128], mixT[:, k, :], k == 0, k == 7,
                       [("ring", s), ("hT", k)], [("ps", ba)])
                ring_touch(s)
                stt(xT[:, f, T], ps[ba][:, 0:N], MV(l, cnd, K_GATE1, f), xT[:, f, T], ALU.mult, ALU.add,
                    [("ps", ba), ("mvraw", l, cnd), ("xT", g, f)], [("xT", g, f)])

        def sample_ctx(l, w):
            rsm = ring[w.small]
            kb.dma("sp", "cst", cst[:], c_ckv_d[l].rearrange("(t p) c -> p t c", p=128), writes=["cst"])
            kb.dma("sp", "cst2", cst2[:, :, 64:96], c_kr_d[l].rearrange("(t p) c -> p t c", p=128), writes=["cst2"])
            for t in range(2):
                kb.dma("sp", "cstk", cstk[:, t], c_nak_d[l][:, t * 128:(t + 1) * 128, :].rearrange("h p c -> p h c"), writes=["cstk"])
                kb.dma("sp", "cstd", cstd[:, t], c_dfk_d[l][:, t * 128:(t + 1) * 128, :].rearrange("h p c -> p h c"), writes=["cstd"])
            for t in range(2):
                for a_ in range(2):
                    kb.dma("pool", f"vpc_na{t}{a_}", VPc4[:, t * 8 + 3:t * 8 + 6, 2 * a_, :],
                           c_nav_d[l][a_:6:2, t * 128:(t + 1) * 128, :].rearrange("h p c -> p h c"), writes=[("VPc", t, 1, a_)])
                    kb.dma("pool", f"vpc_df{t}{a_}", VPc4[:, t * 8 + 6:t * 8 + 8, 2 * a_, :],
                           c_dfv_d[l][a_:4:2, t * 128:(t + 1) * 128, :].rearrange("h p c -> p h c"), writes=[("VPc", t, 2, a_)])
            b = bank("B")
            for t in range(2):
                tp(ps[b][:, t * 128:(t + 1) * 128], cst[:, t, :], ident[:], ["cst", "ident"], [("ps", b)])
            cp(Ebuf[0][:, 0, :], ps[b][:, 0:N], [("ps", b)], [("Ebuf", 0, 0)])
            ckvc_b = Ebuf[0][:, 0, :]
            b = bank("B")
            for t in range(2):
                tp(ps[b][:, t * 128:(t + 1) * 128], cst2[:, t, :], ident[:], ["cst2", "ident"], [("ps", b)])
            krc = Est[0][:, 0, :]
            cp(krc, ps[b][:, 0:N], [("ps", b)], [("Est", 0)])
            run_norm_pipeline([mk_mla_k_job(l, w, h, ckvc_b, ("Ebuf", 0, 0), krc, ("Est", 0), False,
                                            lambda h_: KTc[:, h_, :], lambda h_: ("KTc", h_)) for h in range(6)])
            vsrc = rsm[:, 1536:1920]
            for t in range(2):
                bv = bank("A")
                mm(ps[bv][:, 0:384], ckvc_b[:, t * 128:(t + 1) * 128], vsrc, True, True,
                   [("ring", w.small), ("Ebuf", 0, 0)], [("ps", bv)])
                ring_touch(w.small)
                cp(VPc4[:, t * 8:t * 8 + 3, 0:3:2, :], ps[bv][:, 0:384].rearrange("p (q a c) -> p q a c", q=3, a=2),
                   [("ps", bv)], [("VPc", t, 0)])
            for c in range(3):
                b = bank("B")
                for t in range(2):
                    tp(ps[b][:, t * 128:(t + 1) * 128], cstk[:, t, 2 * c:2 * c + 2, :].rearrange("p h c -> p (h c)"), ident[:],
                       ["cstk", "ident"], [("ps", b)])
                cp(KTc[:, 6 + c, :], ps[b][:, 0:N], [("ps", b)], [("KTc", 6 + c)])
            for c in range(2):
                b = bank("B")
                for t in range(2):
                    tp(ps[b][:, t * 128:(t + 1) * 128], cstd[:, t, 2 * c:2 * c + 2, :].rearrange("p h c -> p (h c)"), ident[:],
                       ["cstd", "ident"], [("ps", b)])
                cp(KTc[:, 9 + c, :], ps[b][:, 0:N], [("ps", b)], [("KTc", 9 + c)])

        def sample_exchange(l):
            kb.dma("sp", "exk", exk_in[l].ap().rearrange("(c p) t -> p c t", p=128), KT[:],
                   reads=[("KT", c) for c in range(11)], writes=[("exin", l, "k")])
            kb.coll(f"exk{l}", lambda e, l=l: e.collective_compute("AllGather", ALU.bypass, replica_groups=RG,
                                                                  ins=[exk_in[l].ap().opt()], outs=[exk_out[l].ap().opt()]),
                    reads=[("exin", l, "k")], writes=[("exoutk", l)])
            vreg = exv_in[l].ap().rearrange("(tok a) c -> tok (a c)", a=4).rearrange("(t p) f -> p t f", p=128)
            for t in range(2):
                for a_ in range(2):
                    kb.dma("sp", f"exv{t}{a_}", vreg[:, t, :].rearrange("p (q a c) -> p q a c", q=8, a=2)[:, :, a_, :],
                           VP4[:, t * 8:(t + 1) * 8, 2 * a_, :],
                           reads=[("VP", t, j) for j in range(3)], writes=[("exin", l, "v", t, a_)])
            kb.coll(f"exv{l}", lambda e, l=l: e.collective_compute("AllGather", ALU.bypass, replica_groups=RG,
                                                                  ins=[exv_in[l].ap().opt()], outs=[exv_out[l].ap().opt()]),
                    reads=[("exin", l, "v", t, a_) for t in range(2) for a_ in range(2)], writes=[("exoutv", l)])

        def attention_sample(l):
            exo = exk_out[l].ap().rearrange("(r x) t -> r x t", r=4)
            vall = exv_out[l].ap().rearrange("(r x) t -> r (x t)", r=4)
            kst = {"i": 0}
            kslot = {}
            vdone = set()

            def prestage(hh):
                if hh >= 16:
                    return
                pi = hh // 2
                st_ = pi % 2
                kc_ = head_maps(hh)[0][0]
                if kc_ not in kslot:
                    si = kst["i"] % 2
                    kst["i"] += 1
                    kslot[kc_] = si
                    kb.dma("sp", f"ktst{si}", KTst[si][:, :].rearrange("p (r t) -> p r t", r=4),
                           exo[:, kc_ * 128:(kc_ + 1) * 128, :].rearrange("r p t -> p r t"),
                           reads=[("exoutk", l)], writes=[("KTst", si)])
                for pv in (pi,):
                    if pv in vdone or pv >= 8:
                        continue
                    vdone.add(pv)
                    sv = pv % 2
                    for r in range(4):
                        for a_ in range(2):
                            src = vall[r, :].rearrange("(t p f) -> p t f", t=2, p=128)[:, :, pv * 128 + a_ * 64:pv * 128 + a_ * 64 + 64]
                            kb.dma("pool", f"vpst{sv}_{r}{a_}", VPst4[sv][:, r * 2:r * 2 + 2, 2 * a_, :],
                                   src, reads=[("exoutv", l)], writes=[("VPst", sv, r, a_)])
                if 6 <= hh < 12:
                    hn = hh - 6
                    eb = hn % 2
                    for qd in range(4):
                        es_ = qd % 2
                        kb.dma("sp", f"est{es_}", Est[es_][:], biasm_d[l, hn, qd * 256:(qd + 1) * 256, :].rearrange("(c p) q -> p c q", p=128),
                               writes=[("Est", es_)])
                        act(Ebuf[eb][:, qd * 2:(qd + 1) * 2, :], Est[es_][:], AF.Exp, [("Est", es_)], [("Ebuf", eb, qd)])

            jobs = []
            for hh in range(16):
                pi = hh // 2
                st_ = pi % 2
                maps = head_maps(hh)
                obs = []
                isna = 6 <= hh < 12
                eb = (hh - 6) % 2
                for mi, (qc, pb, kr, scale, qm) in enumerate(maps):
                    hst = {}
                    for cp_ in range(5):
                        st = {}

                        def S_(st=st, hst=hst, qc=qc, pb=pb, kr=kr, qm=qm, cp_=cp_, hh=hh, mi=mi):
                            if cp_ == 0 and mi == 0:
                                prestage(hh + 1)
                            si = kslot[qc]
                            qap = QT[pb:pb + kr, qc, :] if qm is None else QTm[pb:pb + kr, qm, :]
                            qkey = ("QT", qc) if qm is None else ("QTm", qm)
                            sbk = bank("S")
                            st["sbk"] = sbk
                            for kk in range(2):
                                c = cp_ * 2 + kk
                                if c < 2:
                                    lhs = KTc[pb:pb + kr, qc, c * 128:(c + 1) * 128]
                                    rk = [("KTc", qc)]
                                else:
                                    lhs = KTst[si][pb:pb + kr, (c - 2) * 128:(c - 1) * 128]
                                    rk = [("KTst", si)]
                                mm(ps[sbk][:, kk * N:(kk + 1) * N], lhs, qap, True, True, rk + [qkey], [("ps", sbk)])

                        def E_(st=st, scale=scale, cp_=cp_, isna=isna, eb=eb):
                            pi_ = cnt["pt"] % 3
                            cnt["pt"] += 1
                            st["pt"] = pi_
                            act(PT[pi_][:], ps[st["sbk"]][:, :], AF.Exp, [("ps", st["sbk"])], [("PT", pi_)], scale=scale)
                            if isna and cp_ >= 1:
                                e0 = (cp_ - 1) * 2
                                tt(PT[pi_][:], PT[pi_][:], Ebuf[eb][:, e0:e0 + 2, :].rearrange("p c q -> p (c q)"), ALU.mult,
                                   [("PT", pi_), ("Ebuf", eb, cp_ - 1)], [("PT", pi_)])

                        def PV_(st=st, hst=hst, hh=hh, cp_=cp_, st_=st_, obs=obs):
                            if cp_ == 0:
                                hst["ob"] = bank("O")
                                obs.append(hst["ob"])
                            ob = hst["ob"]
                            pi_ = st["pt"]
                            grp = 0 if hh < 6 else (1 if hh < 12 else 2)
                            for kk in range(2):
                                c = cp_ * 2 + kk
                                if c < 2:
                                    lhsv = vblock(VPc, 1, c, hh, 16)
                                    rk = ([("VPc", c, 0)] if grp == 0 else [("VPc", c, grp, hh % 2)]) + ["VPcones"]
                                else:
                                    lhsv = vblock(VPst[st_], 1, c - 2, hh, 2)
                                    rk = [("VPst", st_, (c - 2) // 2, hh % 2), ("VPstones", st_)]
                                mm(ps[ob][:, 0:N], lhsv, PT[pi_][:, kk * N:(kk + 1) * N],
                                   c == 0, c == 9, rk + [("PT", pi_)], [("ps", ob)])

                        def POST_(hh=hh, pi=pi, obs=obs, last=(mi == len(maps) - 1 and cp_ == 4)):
                            if not last:
                                return
                            if hh < 12:
                                attn_norm(hh, obs[0], mixT[:, pi, :], ("hT", pi), [])
                            else:
                                df_combine(l, hh, obs)
                                if hh % 2 == 1:
                                    df_finish(l, pi)
                        jobs.append((S_, E_, PV_, POST_))
            prestage(0)
            run_pipeline(jobs)

        def ffn(l, nxt_steps):
            blocks = [None] * len(FFN_BLOCKS)
            blocks[0] = load_ffn_block(l, 0)
            blocks[1] = load_ffn_block(l, 1)
            barrier()
            def norm2(g):
                norm_mod(l, g, K_G2, K_SH2, lambda k, g=g: h2T[:, k, g * N:(g + 1) * N], lambda k, g=g: ("h2T", g, k))
            norm2(0)
            norm2(1)

            def pump():
                while nxt_steps and ring_free() > 0:
                    nxt_steps.pop(0)()
            pump()
            TB = [(0, 512, 0, (0, 1)), (512, 512, 0, (2, 3)), (1024, 256, 1, (4,))]
            fcnt = {"a": 0, "o": 0}
            OB = (2, 3, 6, 7)
            for bi in range(len(FFN_BLOCKS)):
                w = blocks[bi]
                for ti_, (t0_, W, cnd, grps) in enumerate(TB):
                    if bi == 0 and ti_ == 1:
                        norm2(2)
                        norm2(3)
                    if bi == 0 and ti_ == 2:
                        norm2(4)
                    T = slice(t0_, t0_ + W)
                    ai = fcnt["a"] % 2
                    fcnt["a"] += 1
                    a_ = aT[ai]
                    akey = ("aT", ai)
                    for c in range(w.n):
                        bg, bu = bank4(), bank4()
                        hk = [("h2T", g_, k) for g_ in grps for k in range(8)]
                        for k in range(8):
                            mm(ps[bg][:, 0:W], w.vg[:, k, c * 128:(c + 1) * 128], h2T[:, k, T], k == 0, k == 7,
                               [("ring", w.g)] + [("h2T", g_, k) for g_ in grps], [("ps", bg)])
                        for k in range(8):
                            mm(ps[bu][:, 0:W], w.vu[:, k, c * 128:(c + 1) * 128], h2T[:, k, T], k == 0, k == 7,
                               [("ring", w.u)] + [("h2T", g_, k) for g_ in grps], [("ps", bu)])
                        si_ = (fcnt["a"] + c) % 2
                        act(silt[si_][:, 0:W], ps[bg][:, 0:W], AF.Silu, [("ps", bg)], [("silt", si_)])
                        tt(a_[:, c, 0:W], ps[bu][:, 0:W], silt[si_][:, 0:W], ALU.mult, [("ps", bu), ("silt", si_)], [akey + (c,)])
                    for f in range(8):
                        b = OB[fcnt["o"] % 4]
                        fcnt["o"] += 1
                        for c in range(w.n):
                            mm(ps[b][:, 0:W], w.vd[:, c, f * 128:(f + 1) * 128], a_[:, c, 0:W], c == 0, c == w.n - 1,
                               [("ring", w.d), akey + (c,)], [("ps", b)])
                        xk = [("xT", g_, f) for g_ in grps]
                        stt(xT[:, f, T], ps[b][:, 0:W], MV(l, cnd, K_GATE2, f), xT[:, f, T], ALU.mult, ALU.add,
                            [("ps", b), ("mvraw", l, cnd)] + xk, xk)
                ring_touch(w.g)
                ring_touch(w.u)
                ring_touch(w.d)
                if bi + 2 < len(FFN_BLOCKS):
                    blocks[bi + 2] = load_ffn_block(l, bi + 2)
                pump()
            assert not nxt_steps

        for l in range(DEPTH):
            w = mixw
            stage("layer")
            barrier()
            mixer_init()
            if "s4" not in KSKIP:
                sample_ctx(l, w)
            stage("front4")
            if "s4" not in KSKIP:
                front(l, 4, w, "kv")
            stage("exch")
            if "s4" not in KSKIP and "ex" not in KSKIP:
                sample_exchange(l)
            for g in range(4):
                stage("frontg")
                front(l, g, w)
                if "po" not in KSKIP:
                    prompt_outputs(l, g)
                stage("attng")
                attention_prompt(l, g)
                out_proj(l, g, w)
            stage("attns")
            front(l, 4, w, "q")
            attention_sample(l)
            if l == 0:
                dbg("mix0", hT[:], [128, 8, N], [("hT", k) for k in range(8)], dt=BF16)
                dbg("qt0", QT[:], [128, 11, N], [("QT", k) for k in range(11)], dt=BF16)
                dbg("ktc0", KTc[:], [128, 11, N], [("KTc", k) for k in range(11)], dt=BF16)
            out_proj(l, 4, w)
            stage("ffn")
            if l + 1 < DEPTH:
                nxt_w, nxt_steps = mixer_loader(l + 1)
            else:
                nxt_w, nxt_steps = None, []
            ffn(l, nxt_steps)
            mixw = nxt_w

        kb.enabled = True
        barrier()
        for g in range(NGRP):
            st_ = 0
            for t in range(2):
                for q4 in range(2):
                    b = bank(("A", "B", "S", "O")[(t * 2 + q4) % 4])
                    for kk in range(4):
                        k = q4 * 4 + kk
                        tp(ps[b][:, kk * 128:(kk + 1) * 128], xT[:, k, g * N + t * 128:g * N + (t + 1) * 128], ident[:],
                           [("xT", g, k), "ident"], [("ps", b)])
                    if q4 == 0:
                        act(xstage[st_][:, t, 0:512], ps[b][:, :], AF.Copy, [("ps", b)], [("xstage", st_)])
                    else:
                        cp(xstage[st_][:, t, 512:1024], ps[b][:, :], [("ps", b)], [("xstage", st_)])
            kb.dma("sp", f"y{st_}", y_d[g].rearrange("(t p) d -> p t d", p=128), xstage[st_][:], reads=[("xstage", st_)],
                   writes=[("out", "y", g)], final=True)

        names = kb.all_sems()
        semd = {}
        for nm in names:
            semd[nm] = es.enter_context(nc.semaphore(nm.replace(":", "_")))
        block = es.enter_context(nc.Block())

        def replay(engname, e, drain=False):
            for (waits, fn, s, amt) in kb.ops[engname]:
                for (ws, v) in waits:
                    e.wait_ge(semd[ws], v)
                fn(e).then_inc(semd[s], amt)
            if drain:
                for s, v in kb.final.items():
                    e.wait_ge(semd[s], v)

        @block.tensor
        def _(e):
            replay("pe", e)

        @block.scalar
        def _(e):
            replay("act", e)

        @block.vector
        def _(e):
            replay("dve", e)

        @block.gpsimd
        def _(e):
            replay("pool", e)

        @block.sync
        def _(e):
            replay("sp", e, drain=True)

    build_program.marks = marks
    return nc, dbg_outs


def _rope_tables(pos0):
    t = np.arange(pos0, pos0 + N)
    row = (t // GRID_W).astype(np.float32)
    col = (t % GRID_W).astype(np.float32)

    def tab(rot_dim):
        n = rot_dim // 4
        inv = (1.0 / (np.float32(10000.0) ** (np.arange(n, dtype=np.float32) * np.float32(2.0) / np.float32(rot_dim // 2)))).astype(np.float32)
        ar = row[:, None] * inv
        ac = col[:, None] * inv
        ang = np.concatenate([ar, ar, ac, ac], axis=-1).astype(np.float32)
        return np.cos(ang).astype(np.float32).T, np.sin(ang).astype(np.float32).T
    cm, sm = tab(32)
    cosm = np.ones((128, N), np.float32)
    sinm = np.zeros((128, N), np.float32)
    cosm[64:96] = cm
    sinm[64:96] = sm
    cd, sd = tab(32)
    cosd = np.tile(cd, (4, 1))
    sind = np.tile(sd, (4, 1))
    return np.concatenate([cosm, sinm, cosd, sind], axis=1)


def _rot_block():
    R = np.zeros((32, 32), np.float32)
    for m in range(8):
        R[8 + m, m] = -1.0
        R[m, 8 + m] = 1.0
        R[24 + m, 16 + m] = -1.0
        R[16 + m, 24 + m] = 1.0
    return R


def _consts():
    ident = np.eye(128, dtype=np.float32)
    Rb = _rot_block()
    Rm = np.zeros((128, 128), np.float32)
    Rm[64:96, 64:96] = Rb
    Rd = np.zeros((128, 128), np.float32)
    for j in range(4):
        Rd[32 * j:32 * j + 32, 32 * j:32 * j + 32] = Rb
    bo2 = np.kron(np.eye(2, dtype=np.float32), np.ones((64, 64), np.float32))
    bo4 = np.kron(np.eye(4, dtype=np.float32), np.ones((32, 32), np.float32))
    bo96 = np.zeros((128, 128), np.float32)
    bo96[0:96, 0:96] = 1.0
    return ident, np.concatenate([Rm, Rd], 1), np.concatenate([bo2, bo4, bo96], 1)


def _bias_index(qrank):
    rows, kr, kw = 16, 8, 16
    keys = np.arange(1024)
    kr_ = keys // GRID_W
    kc_ = keys % GRID_W
    q = np.arange(N) + qrank * N
    qr = q // GRID_W
    qc = q % GRID_W
    rstart = np.clip(qr - kr // 2, 0, rows - kr)
    cstart = np.clip(qc - kw // 2, 0, GRID_W - kw)
    inr = (kr_[:, None] >= rstart[None, :]) & (kr_[:, None] < rstart[None, :] + kr)
    inc = (kc_[:, None] >= cstart[None, :]) & (kc_[:, None] < cstart[None, :] + kw)
    mask = inr & inc
    rel_r = np.clip(kr_[:, None] - qr[None, :] + (kr - 1), 0, 2 * kr - 2)
    rel_c = np.clip(kc_[:, None] - qc[None, :], -(kw - 1), kw - 1) + (kw - 1)
    return mask, rel_r, rel_c


_CACHE = {}


def kernel(**inp):
    f32 = lambda a: np.ascontiguousarray(np.asarray(a, dtype=np.float32))
    if "nc" not in _CACHE:
        taps = os.environ.get("KTAPS")
        _CACHE["nc"] = build_program(debug_taps=taps.split(",") if taps else None)
    nc, dbg_outs = _CACHE["nc"]
    x_prompt = f32(inp["x_prompt"])
    x_sample = f32(inp["x_sample"])
    ident, rmat, bones = _consts()
    g = lambda k: f32(inp[k])
    gains = np.zeros((128, DEPTH * GCOLS), np.float32)
    for l in range(DEPTH):
        o = l * GCOLS
        gains[:, o + 0:o + 8] = g("g_mix")[l].reshape(8, 128).T
        gains[:, o + 8:o + 16] = g("g_ffn")[l].reshape(8, 128).T
        gains[:, o + 16:o + 18] = g("g_qa")[l].reshape(2, 128).T
        gains[:, o + 18] = g("g_kva")[l]
        gains[0:96, o + 19] = g("g_mla_q")[l]
        gains[0:96, o + 20] = g("g_mla_k")[l]
        gains[:, o + 21] = np.tile(g("g_na_q")[l], 2)
        gains[:, o + 22] = np.tile(g("g_na_k")[l], 2)
        gains[:, o + 23] = np.tile(g("g_df_q")[l], 4)
        gains[:, o + 24] = np.tile(g("g_df_k")[l], 4)
        gains[:, o + 25] = np.tile(g("g_df_sub")[l], 2)
    gains = np.concatenate([gains, np.zeros((128, 2), np.float32)], axis=1)
    gains[:, DEPTH * GCOLS] = np.tile(np.concatenate([np.ones(32), np.zeros(32)]), 2)
    gains[:, DEPTH * GCOLS + 1] = np.tile(np.concatenate([np.zeros(32), np.ones(32)]), 2)
    lamp = np.stack([np.stack([g("df_lq1")[l], g("df_lk1")[l], g("df_lq2")[l], g("df_lk2")[l]]) for l in range(DEPTH)]).reshape(1, -1)
    rpb = g("na_rpb")
    rpb_ext = np.concatenate([rpb.reshape(DEPTH, 6, -1), np.full((DEPTH, 6, 1), NEGB, np.float32)], axis=-1)
    w_mod = g("w_mod")
    b_mod = g("b_mod")
    shared = {
        "w_in": g("w_in"), "w_uq": g("w_uq"), "w_ukv": g("w_ukv"), "w_out": g("w_out"),
        "w_gate": g("w_gate"), "w_up": g("w_up"), "w_down": g("w_down"),
        "gains": gains, "lamp": np.ascontiguousarray(lamp), "ident": ident, "rmat": rmat, "bones": bones,
    }
    in_maps = []
    for r in range(8):
        b = r // 4
        qr = r % 4
        m = dict(shared)
        m["xin"] = np.ascontiguousarray(np.concatenate([x_prompt[4 * r:4 * r + 4], x_sample[b:b + 1, qr * N:(qr + 1) * N]], axis=0))
        cvec = np.stack([g("c_ctx"), g("c")[b]], axis=-1)
        m["cT"] = np.ascontiguousarray(cvec.reshape(8, 128, 2).transpose(1, 0, 2).reshape(128, 16))
        m["wmod"] = np.ascontiguousarray(w_mod[:, :, qr * 1536:(qr + 1) * 1536])
        m["bmod"] = np.ascontiguousarray(b_mod[:, None, qr * 1536:(qr + 1) * 1536])
        m["rope"] = _rope_tables(qr * N)
        mask, rel_r, rel_c = _bias_index(qr)
        flat = np.where(mask, rel_r * 31 + rel_c, 15 * 31)
        m["biasm"] = np.ascontiguousarray(rpb_ext[:, :, flat])
        m["c_ckv"] = np.ascontiguousarray(g("cache_mla_ckv")[b])
        m["c_kr"] = np.ascontiguousarray(g("cache_mla_krope")[b])
        m["c_nak"] = np.ascontiguousarray(g("cache_na_k")[b])
        m["c_nav"] = np.ascontiguousarray(g("cache_na_v")[b])
        m["c_dfk"] = np.ascontiguousarray(g("cache_df_k")[b])
        m["c_dfv"] = np.ascontiguousarray(g("cache_df_v")[b])
        in_maps.append(m)
    res = run_bass_kernel_spmd(nc, in_maps, core_ids=list(range(8)))
    R = res.results
    _CACHE["last"] = R
    y_prompt = np.concatenate([np.asarray(R[r]["y"])[0:4] for r in range(8)], axis=0)
    y_sample = np.stack([np.concatenate([np.asarray(R[4 * b + q]["y"])[4] for q in range(4)], axis=0) for b in range(2)], axis=0)
    cat = lambda k: np.concatenate([np.asarray(R[r][k]) for r in range(8)], axis=0)
    outs = (y_prompt, y_sample, cat("o_ckv"), cat("o_kr"), cat("o_nak"), cat("o_nav"), cat("o_dfk"), cat("o_dfv"))
    return tuple(np.ascontiguousarray(o, dtype=np.float32) for o in outs)
```

```python
import math
import os
import numpy as np
import ml_dtypes
import concourse.bass as bass
import concourse.mybir as mybir
from concourse.bass_utils import run_bass_kernel_spmd

F32 = mybir.dt.float32
BF16 = mybir.dt.bfloat16
ALU = mybir.AluOpType
AF = mybir.ActivationFunctionType

D = 1024
DEPTH = 2
NGRP = 5
N = 256
NTOK = NGRP * N
EPS = 1e-6
GRID_W = 64
IN_COLS = 2336
DFF = 2816
NFF = DFF // 128
EXR = 11 * 128 + 1024
NEGB = -30000.0
LAM_INIT = [0.8 - 0.6 * math.exp(-0.3 * l) for l in range(DEPTH)]
FFN_BLOCKS = [(0, 4), (4, 4), (8, 4), (12, 4), (16, 4), (20, 2)]
NSLOT = 8
C_CQ0, C_CQ1, C_CKV, C_KR = 0, 128, 256, 384
C_NAQ, C_NAK, C_NAV = 416, 800, 1184
C_DFQ, C_DFK, C_DFV = 1568, 1824, 2080
WIN_SLOTS = [[(0, 416)], [(416, 512)], [(928, 256), (2080, 256)], [(1184, 512)], [(1696, 384)]]
GCOLS = 26


class KB:
    def __init__(self, nc):
        self.nc = nc
        self.ops = {e: [] for e in ("pe", "act", "dve", "pool", "sp")}
        self.cnt = {e: 0 for e in ("pe", "act", "dve", "pool")}
        self.engsem = {}
        self.dsem = {}
        self.dcnt = {}
        self.lastw = {}
        self.readers = {}
        self.waited = {e: {} for e in self.ops}
        self.sem_objs = []
        self.final = {}
        self.barrier_ev = None
        self.enabled = True

    def barrier(self, fn):
        if not self.enabled:
            return
        need = {}
        for e_, c in self.cnt.items():
            if c > 0:
                need["E:" + e_] = c
        for s_, v in self.dcnt.items():
            if s_.startswith("D:ring"):
                continue
            need[s_] = v
        waits = []
        for s_, v in need.items():
            if self.waited["dve"].get(s_, 0) < v:
                self.waited["dve"][s_] = v
                waits.append((s_, v))
        self.cnt["dve"] += 1
        self.barrier_ev = ("E:dve", self.cnt["dve"])
        self.ops["dve"].append((waits, fn, "E:dve", 1))

    def _deps(self, eng, reads, writes):
        need = {}
        if self.barrier_ev is not None:
            need[self.barrier_ev[0]] = self.barrier_ev[1]
        for k in reads:
            ev = self.lastw.get(k)
            if ev is not None:
                need[ev[0]] = max(need.get(ev[0], 0), ev[1])
        for k in writes:
            ev = self.lastw.get(k)
            if ev is not None:
                need[ev[0]] = max(need.get(ev[0], 0), ev[1])
            for ev in self.readers.get(k, ()):
                need[ev[0]] = max(need.get(ev[0], 0), ev[1])
        waits = []
        for s, v in need.items():
            if eng == "pe" and s == "E:pe":
                continue
            if self.waited[eng].get(s, 0) < v:
                self.waited[eng][s] = v
                waits.append((s, v))
        return waits

    def _commit(self, ev, reads, writes):
        for k in reads:
            self.readers.setdefault(k, []).append(ev)
        for k in writes:
            self.lastw[k] = ev
            self.readers[k] = []

    def op(self, eng, fn, reads=(), writes=()):
        if not self.enabled:
            return
        waits = self._deps(eng, reads, writes)
        self.cnt[eng] += 1
        ev = ("E:" + eng, self.cnt[eng])
        self.ops[eng].append((waits, fn, ev[0], 1))
        self._commit(ev, reads, writes)

    def dma(self, q, semname, out, in_, reads=(), writes=(), final=False):
        if not self.enabled:
            return
        waits = self._deps(q, reads, writes)
        s = "D:" + semname
        self.dcnt[s] = self.dcnt.get(s, 0) + 16
        ev = (s, self.dcnt[s])
        self.ops[q].append((waits, (lambda e, o=out, i=in_: e.dma_start(out=o, in_=i)), s, 16))
        self._commit(ev, reads, writes)
        if final:
            self.final[s] = self.dcnt[s]

    def coll(self, semname, fn, reads=(), writes=()):
        if not self.enabled:
            return
        waits = self._deps("pool", reads, writes)
        s = "C:" + semname
        self.dcnt[s] = self.dcnt.get(s, 0) + 1
        ev = (s, self.dcnt[s])
        self.ops["pool"].append((waits, fn, s, 1))
        self._commit(ev, reads, writes)

    def all_sems(self):
        names = set()
        for e, lst in self.ops.items():
            for waits, fn, s, amt in lst:
                names.add(s)
                for (ws, v) in waits:
                    names.add(ws)
        return sorted(names)


class _Stop(Exception):
    pass


def build_program(debug_taps=None):
    nc = bass.Bass("TRN2", target_bir_lowering=False)
    kb = KB(nc)
    KSTOP = int(os.environ.get("KSTOP", "1000"))
    KSKIP = os.environ.get("KSKIP", "").split(",")
    KSUB = int(os.environ.get("KSUB", "1000"))

    def sub(i, g):
        if g == 0 and i > KSUB:
            kb.enabled = False
    stg = {"i": 0}

    marks = []

    def stage(name=""):
        marks.append((name, kb.cnt["pe"], kb.cnt["act"], kb.cnt["dve"]))
        stg["i"] += 1
        if stg["i"] > KSTOP:
            kb.enabled = False

    def din(name, shape, dt=F32):
        return nc.dram_tensor(name, list(shape), dt, kind="ExternalInput").ap()

    def dout(name, shape, dt=F32):
        return nc.dram_tensor(name, list(shape), dt, kind="ExternalOutput").ap()

    xin = din("xin", [NGRP, N, D])
    cT_d = din("cT", [128, 16])
    wmod_d = din("wmod", [DEPTH, D, 1536])
    bmod_d = din("bmod", [DEPTH, 1, 1536])
    w_in_d = din("w_in", [DEPTH, D, IN_COLS])
    w_uq_d = din("w_uq", [DEPTH, 256, 576])
    w_ukv_d = din("w_ukv", [DEPTH, 128, 768])
    w_out_d = din("w_out", [DEPTH, D, D])
    w_gate_d = din("w_gate", [DEPTH, D, DFF])
    w_up_d = din("w_up", [DEPTH, D, DFF])
    w_down_d = din("w_down", [DEPTH, DFF, D])
    gains_d = din("gains", [128, DEPTH * GCOLS + 2])
    lamp_d = din("lamp", [1, DEPTH * 4 * 32])
    ident_d = din("ident", [128, 128])
    rmat_d = din("rmat", [128, 2 * 128])
    rope_d = din("rope", [128, 4 * N])
    bones_d = din("bones", [128, 3 * 128])
    biasm_d = din("biasm", [DEPTH, 6, 1024, N])
    c_ckv_d = din("c_ckv", [DEPTH, 256, 128])
    c_kr_d = din("c_kr", [DEPTH, 256, 32])
    c_nak_d = din("c_nak", [DEPTH, 6, 256, 64])
    c_nav_d = din("c_nav", [DEPTH, 6, 256, 64])
    c_dfk_d = din("c_dfk", [DEPTH, 4, 256, 64])
    c_dfv_d = din("c_dfv", [DEPTH, 4, 256, 64])

    y_d = dout("y", [NGRP, N, D])
    o_ckv = dout("o_ckv", [4, DEPTH, 256, 128])
    o_kr = dout("o_kr", [4, DEPTH, 256, 32])
    o_nak = dout("o_nak", [4, DEPTH, 6, 256, 64])
    o_nav = dout("o_nav", [4, DEPTH, 6, 256, 64])
    o_dfk = dout("o_dfk", [4, DEPTH, 4, 256, 64])
    o_dfv = dout("o_dfv", [4, DEPTH, 4, 256, 64])

    mx_in = nc.dram_tensor("mx_in", [128, 48], F32)
    mx_out = nc.dram_tensor("mx_out", [512, 48], F32)
    exk_in = [nc.dram_tensor(f"exk_in{l}", [1408, N], BF16) for l in range(DEPTH)]
    exk_out = [nc.dram_tensor(f"exk_out{l}", [4 * 1408, N], BF16) for l in range(DEPTH)]
    exv_in = [nc.dram_tensor(f"exv_in{l}", [1024, N], BF16) for l in range(DEPTH)]
    exv_out = [nc.dram_tensor(f"exv_out{l}", [4 * 1024, N], BF16) for l in range(DEPTH)]
    RG = [[0, 1, 2, 3], [4, 5, 6, 7]]

    dbg_outs = []

    from contextlib import ExitStack
    es = ExitStack()

    def sb(name, shape, dt=F32):
        return es.enter_context(nc.sbuf_tensor("s_" + name, list(shape), dt))

    with es:
        xT = sb("xT", [128, 8, NTOK])
        ring = [sb(f"ring{i}", [128, 4096], BF16) for i in range(NSLOT)]
        ident = sb("ident", [128, 128])
        rmat = sb("rmat", [128, 256])
        ropet = sb("ropet", [128, 4 * N])
        bones = sb("bones", [128, 3 * 128], BF16)
        ones_b = sb("ones_b", [128, 128], BF16)
        ones_f = sb("ones_f", [128, 2])
        gains = sb("gains", [128, DEPTH * GCOLS + 2])
        lamt = sb("lamt", [128, 16])
        nlam = sb("nlam", [128, DEPTH])
        cT = sb("cT", [128, 16])
        scT = sb("scT", [128, 16])
        modS = sb("modS", [128, 48])
        modT = sb("modT", [128, 4, 48])
        mv = sb("mv", [128, DEPTH, 2, 8, 8])
        epsb = sb("epsb", [128, 1])
        UB = 64 * 1024
        U = sb("U", [128, UB // 2], BF16)
        carve = {"o": 0}

        def uview(shape, dt):
            n = 1
            for d_ in shape[1:]:
                n *= d_
            nb = n * (4 if dt == F32 else 2)
            o = carve["o"]
            assert o % 4 == 0 and o + nb <= UB, (o, nb)
            carve["o"] = o + nb
            v = U[:, o // 2:(o + nb) // 2]
            if dt == F32:
                v = v.bitcast(F32)
            if len(shape) == 3:
                v = v.rearrange("p (a b) -> p a b", a=shape[1])
            elif len(shape) == 4:
                v = v.rearrange("p (a b c) -> p a b c", a=shape[1], b=shape[2])
            return v
        hT = uview([128, 8, N], BF16)
        mixT = hT
        QT = uview([128, 11, N], BF16)
        KT = uview([128, 11, N], BF16)
        VP = uview([128, 16, 192], BF16)
        KTc = uview([128, 11, N], BF16)
        VPc = uview([128, 16, 192], BF16)
        KTst = [uview([128, 1024], BF16) for i in range(2)]
        QTm = uview([128, 4, N], BF16)
        VPst = [uview([128, 8, 192], BF16) for i in range(2)]
        VP4 = VP.rearrange("p b (s c) -> p b s c", s=3)
        VPc4 = VPc.rearrange("p b (s c) -> p b s c", s=3)
        VPst4 = [v_.rearrange("p b (s c) -> p b s c", s=3) for v_ in VPst]
        Ebuf = [uview([128, 8, N], BF16) for i in range(2)]
        Est = [uview([128, 2, N], F32) for i in range(2)]
        cst2 = uview([128, 2, 128], F32)
        _o = carve["o"]
        vstage = uview([128, 640], F32)
        kstage = uview([128, 7, 128], F32)
        _o2 = carve["o"]
        carve["o"] = _o
        cst = uview([128, 2, 128], F32)
        cstk = uview([128, 2, 6, 64], F32)
        cstd = uview([128, 2, 4, 64], F32)
        carve["o"] = max(_o2, carve["o"])
        mixer_bytes = carve["o"]
        carve["o"] = 0
        h2T = uview([128, 8, NTOK], BF16)
        aT = [uview([128, 4, 2 * N], BF16) for i in range(2)]
        silt = [uview([128, 2 * N], F32) for i in range(2)]
        carve["o"] = 0
        xstage = [uview([128, 2, D], F32)]
        wst = [uview([128, 1536], F32) for i in range(2)]
        mrow = uview([128, 1536], F32)
        bmrow = uview([128, 1536], F32)
        lamp = uview([128, DEPTH * 4 * 32], F32)
        NT = 3
        tsq = [sb(f"tsq{i}", [128, N], BF16) for i in range(NT)]
        tf = [sb(f"tf{i}", [128, N]) for i in range(NT)]
        tr = [sb(f"tr{i}", [128, N]) for i in range(NT)]
        tg = [sb(f"tg{i}", [128, N]) for i in range(NT)]
        rstd_x = sb("rstd_x", [128, N])
        ckvn_f = sb("ckvn_f", [128, N])
        ckvn_b = sb("ckvn_b", [128, N], BF16)
        krT = sb("krT", [128, N])
        cqn = sb("cqn", [128, 2, N], BF16)
        knf = sb("knf", [128, 5, N])
        odf = sb("odf", [128, N])
        PT = [sb(f"PT{i}", [128, 512], BF16) for i in range(3)]
        rc = [sb(f"rc{i}", [128, N]) for i in range(2)]
        kstage2 = sb("kstage2", [128, 7, 128])
        ps = [es.enter_context(nc.psum_tensor(f"ps{i}", [128, 512], F32)) for i in range(8)]

        cnt = {"t": 0, "S": 0, "O": 0, "A": 0, "B": 0, "pt": 0, "rc": 0, "ring": 0}
        POOLS = {"S": (0, 1), "O": (2, 3), "A": (4, 5), "B": (6, 7)}

        def bank(pool):
            b = POOLS[pool][cnt[pool] % 2]
            cnt[pool] += 1
            return b

        def tmp(lst, nm):
            i = cnt["t"] % NT
            return lst[i], (nm, i)

        def nexttmp():
            cnt["t"] += 1

        def mm(out, lhsT, rhs, start, stop, reads, writes):
            kb.op("pe", lambda e: e.matmul(out, lhsT=lhsT, rhs=rhs, start=start, stop=stop), reads, writes)

        def tp(out, in_, idn, reads, writes):
            kb.op("pe", lambda e: e.transpose(out, in_, idn), reads, writes)

        def act(out, in_, func, reads, writes, scale=1.0, bias=None, accum=None):
            def f(e):
                kw = {}
                if bias is not None:
                    kw["bias"] = bias
                if accum is not None:
                    kw["accum_out"] = accum
                return e.activation(out=out, in_=in_, func=func, scale=scale, **kw)
            kb.op("act", f, reads, writes)

        def stt(out, in0, scalar, in1, op0, op1, reads, writes, eng="dve", accum=None):
            def f(e):
                if accum is not None:
                    return e.scalar_tensor_tensor(out=out, in0=in0, scalar=scalar, in1=in1, op0=op0, op1=op1, accum_out=accum)
                return e.scalar_tensor_tensor(out=out, in0=in0, scalar=scalar, in1=in1, op0=op0, op1=op1)
            kb.op(eng, f, reads, writes)

        def tt(out, in0, in1, op, reads, writes, eng="dve"):
            kb.op(eng, lambda e: e.tensor_tensor(out=out, in0=in0, in1=in1, op=op), reads, writes)

        def ts(out, in0, s1, s2, op0, op1, reads, writes, eng="dve"):
            if op1 is None:
                kb.op(eng, lambda e: e.tensor_scalar(out=out, in0=in0, scalar1=s1, scalar2=None, op0=op0), reads, writes)
            else:
                kb.op(eng, lambda e: e.tensor_scalar(out=out, in0=in0, scalar1=s1, scalar2=s2, op0=op0, op1=op1), reads, writes)

        def cp(out, in_, reads, writes, eng="dve"):
            kb.op(eng, lambda e: e.tensor_copy(out=out, in_=in_), reads, writes)

        def recip(out, in_, reads, writes):
            kb.op("dve", lambda e: e.reciprocal(out=out, in_=in_), reads, writes)

        def memset(ap, val, writes, eng="dve"):
            kb.op(eng, lambda e: e.memset(ap, val), (), writes)

        def dbg(name, ap, shape, reads, dt=F32):
            if debug_taps is None or name not in debug_taps:
                return
            t = nc.dram_tensor("dbg_" + name, list(shape), dt, kind="ExternalOutput").ap()
            kb.dma("sp", "dbg_" + name, t, ap, reads=reads, writes=[("dbgout", name)], final=True)
            dbg_outs.append(name)

        kb.dma("sp", "c_ident", ident[:], ident_d, writes=["ident"])
        kb.dma("sp", "c_gains", gains[:], gains_d, writes=["gains"])
        kb.dma("sp", "c_cT", cT[:], cT_d, writes=["cT"])
        kb.dma("sp", "c_rmat", rmat[:], rmat_d, writes=["rmat"])
        kb.dma("sp", "c_rope", ropet[:], rope_d, writes=["ropet"])
        kb.dma("pool", "c_bones", bones[:], bones_d, writes=["bones"])
        kb.dma("sp", "c_lamp", lamp[:], lamp_d[0].partition_broadcast(128), writes=["lamp"])
        memset(ones_b[:], 1.0, ["ones_b"])
        memset(ones_f[:], 1.0, ["ones_f"])
        memset(epsb[:], EPS, ["epsb"])
        memset(krT[:], 0.0, ["krT"])
        dummy = sb("dummy", [128, 2])

        def barrier():
            kb.barrier(lambda e: e.memset(dummy[:], 0.0))

        def mixer_init():
            memset(VP[:, :, 64:128], 1.0, ["VPones"])
            memset(VPc[:, :, 64:128], 1.0, ["VPcones"])
            for i in range(2):
                memset(VPst[i][:, :, 64:128], 1.0, [("VPstones", i)])
            memset(cst2[:], 0.0, ["cst2"])

        BO2 = bones[:, 0:128]
        BO4 = bones[:, 128:256]
        BO96 = bones[:, 256:384]

        def gcol(l, j, w=1):
            return gains[:, l * GCOLS + j: l * GCOLS + j + w]
        G_MIX, G_FFN, G_QA, G_KVA, G_MQ, G_MK, G_NQ, G_NK, G_DQ, G_DK, G_DS = 0, 8, 16, 18, 19, 20, 21, 22, 23, 24, 25

        ring_last = [0] * NSLOT
        prog = {"i": 0}

        def ring_free():
            return sum(1 for v in ring_last if v < 10 ** 6)

        def ring_alloc():
            i = min(range(NSLOT), key=lambda s: ring_last[s])
            assert ring_last[i] < 10 ** 6, "weight ring exhausted"
            prog["i"] += 1
            ring_last[i] = prog["i"] + 10 ** 6
            return i

        def ring_touch(i):
            prog["i"] += 1
            ring_last[i] = prog["i"]

        def load_w(slot, pieces):
            for (dst, src) in pieces:
                kb.dma("pool", f"ring{slot}", dst, src, writes=[("ring", slot)])

        class WS:
            pass

        def mixer_loader(l):
            w = WS()
            w.win = []
            w.wout = []
            w.colmap = {}
            steps = []

            def st_small():
                s = ring_alloc()
                w.small = s
                r = ring[s]
                load_w(s, [(r[:, 0:1152].rearrange("p (k c) -> p k c", k=2), w_uq_d[l].rearrange("(k p) c -> p k c", p=128)),
                           (r[:, 1152:1536].rearrange("p (h c) -> p h c", h=6), w_ukv_d[l].rearrange("p (h c) -> p h c", h=6)[:, :, 0:64]),
                           (r[:, 1536:1920].rearrange("p (h c) -> p h c", h=6), w_ukv_d[l].rearrange("p (h c) -> p h c", h=6)[:, :, 64:128])])
            steps.append(st_small)
            wv = w_in_d[l].rearrange("(k p) c -> p k c", p=128)

            def mk_win(pieces):
                def f():
                    s = ring_alloc()
                    w.win.append(s)
                    tot = sum(nc_ for (_, nc_) in pieces)
                    view = ring[s][:, 0:8 * tot].rearrange("p (k c) -> p k c", k=8)
                    off = 0
                    pl = []
                    for (c0, ncol) in pieces:
                        pl.append((view[:, :, off:off + ncol], wv[:, :, c0:c0 + ncol]))
                        w.colmap[c0] = (s, view, off, ncol)
                        off += ncol
                    load_w(s, pl)
                return f
            for pieces in WIN_SLOTS:
                steps.append(mk_win(pieces))
            wo = w_out_d[l].rearrange("(k p) c -> p k c", p=128)

            def mk_wout(j):
                def f():
                    s = ring_alloc()
                    w.wout.append(s)
                    view = ring[s][:, :].rearrange("p (k c) -> p k c", k=8)
                    load_w(s, [(view, wo[:, :, j * 512:(j + 1) * 512])])
                return f
            for j in range(2):
                steps.append(mk_wout(j))
            return w, steps

        def win_ap(w, col, width, k):
            for c0, (s, view, off, ncol) in w.colmap.items():
                if c0 <= col and col + width <= c0 + ncol:
                    return view[:, k, off + col - c0: off + col - c0 + width], ("ring", s), s
            raise KeyError(col)

        def load_ffn_block(l, bi):
            c0, ncnk = FFN_BLOCKS[bi]
            w = WS()
            w.n = ncnk
            cols = ncnk * 128
            w.g = ring_alloc()
            vg = ring[w.g][:, 0:8 * cols].rearrange("p (k c) -> p k c", k=8)
            load_w(w.g, [(vg, w_gate_d[l].rearrange("(k p) c -> p k c", p=128)[:, :, c0 * 128:c0 * 128 + cols])])
            w.u = ring_alloc()
            vu = ring[w.u][:, 0:8 * cols].rearrange("p (k c) -> p k c", k=8)
            load_w(w.u, [(vu, w_up_d[l].rearrange("(k p) c -> p k c", p=128)[:, :, c0 * 128:c0 * 128 + cols])])
            w.d = ring_alloc()
            vd = ring[w.d][:, 0:ncnk * 1024].rearrange("p (c f) -> p c f", c=ncnk)
            load_w(w.d, [(vd, w_down_d[l][c0 * 128:c0 * 128 + cols, :].rearrange("(c p) f -> p c f", p=128))])
            w.vg, w.vu, w.vd = vg, vu, vd
            return w

        mixw, _steps = mixer_loader(0)
        for _f in _steps:
            _f()

        stage("lambda")
        for l in range(DEPTH):
            for j in range(2):
                a = lamp[:, (l * 4 + 2 * j) * 32:(l * 4 + 2 * j + 1) * 32]
                b = lamp[:, (l * 4 + 2 * j + 1) * 32:(l * 4 + 2 * j + 2) * 32]
                stt(tf[0][:, 0:32], a, 1.0, b, ALU.mult, ALU.mult,
                    ["lamp"], [("tf", 0), ("lamt", l, j)], accum=lamt[:, l * 2 + j:l * 2 + j + 1])
            act(lamt[:, 4 + l * 2:4 + l * 2 + 2], lamt[:, l * 2:l * 2 + 2], AF.Exp, [("lamt", l, 0), ("lamt", l, 1)], [("lame", l)])
            tt(lamt[:, 8 + l:9 + l], lamt[:, 5 + l * 2:6 + l * 2], lamt[:, 4 + l * 2:5 + l * 2], ALU.subtract, [("lame", l)], [("lamd", l)])
            ts(nlam[:, l:l + 1], lamt[:, 8 + l:9 + l], -LAM_INIT[l], None, ALU.add, None, [("lamd", l)], [("nlam", l)])

        stage("xT")
        def emit_xT(g):
            st_ = 0
            kb.dma("sp", f"xstage{st_}", xstage[st_][:], xin[g].rearrange("(t p) d -> p t d", p=128), writes=[("xstage", st_)])
            for k in range(8):
                b = bank("O")
                for t in range(2):
                    tp(ps[b][:, t * 128:(t + 1) * 128], xstage[st_][:, t, k * 128:(k + 1) * 128], ident[:],
                       [("xstage", st_), "ident"], [("ps", b)])
                if k % 2:
                    cp(xT[:, k, g * N:(g + 1) * N], ps[b][:, 0:N], [("ps", b)], [("xT", g, k)])
                else:
                    act(xT[:, k, g * N:(g + 1) * N], ps[b][:, 0:N], AF.Copy, [("ps", b)], [("xT", g, k)])

        stage("mod")
        act(scT[:], cT[:], AF.Silu, ["cT"], ["scT"])
        XT_AT = {(0, 0): 0, (0, 3): 1, (0, 6): 2, (1, 1): 3, (1, 4): 4}
        for l in range(DEPTH):
            banks = [bank("A"), bank("B"), bank("S")]
            kb.dma("sp", "c_bm", bmrow[0:1, :], bmod_d[l], writes=["bmrow"])
            for k in range(8):
                if (l, k) in XT_AT:
                    emit_xT(XT_AT[(l, k)])
                wsl = (l * 8 + k) % 2
                kb.dma("sp", f"wst{wsl}", wst[wsl][:], wmod_d[l, k * 128:(k + 1) * 128, :], writes=[("wst", wsl)])
                for j in range(3):
                    mm(ps[banks[j]][0:2, :], scT[:, 2 * k:2 * k + 2], wst[wsl][:, j * 512:(j + 1) * 512], k == 0, False,
                       ["scT", ("wst", wsl)], [("ps", banks[j])])
            for j in range(3):
                mm(ps[banks[j]][0:2, :], ones_f[0:1, 0:2], bmrow[0:1, j * 512:(j + 1) * 512], False, True,
                   ["ones_f", "bmrow"], [("ps", banks[j])])
                cp(mrow[0:2, j * 512:(j + 1) * 512], ps[banks[j]][0:2, :], [("ps", banks[j])], [("mrow", j)])
            bt = bank("B")
            for jb in range(12):
                tp(ps[bt][:, jb * 2:jb * 2 + 2], mrow[0:2, jb * 128:(jb + 1) * 128], ident[0:2, 0:2],
                   [("mrow", jb // 4), "ident"], [("ps", bt)])
            cp(modS[:, l * 24:(l + 1) * 24], ps[bt][:, 0:24], [("ps", bt)], ["modS"])
        stage("modgather")
        kb.dma("sp", "mx", mx_in.ap(), modS[:], reads=["modS"], writes=["mx_in"])
        kb.coll("mx", lambda e: e.collective_compute("AllGather", ALU.bypass, replica_groups=RG,
                                                     ins=[mx_in.ap().opt()], outs=[mx_out.ap().opt()]),
                reads=["mx_in"], writes=["mx_out"])
        kb.dma("sp", "mxb", modT[:], mx_out.ap().rearrange("(r p) c -> p r c", p=128), reads=["mx_out"], writes=["modT"])
        for l in range(DEPTH):
            for cnd in range(2):
                for r in range(4):
                    cp(mv[:, l, cnd, 0:6, :].rearrange("p a b -> p (a b)")[:, r * 12:(r + 1) * 12],
                       modT[:, r, l * 24 + cnd:l * 24 + 24:2], ["modT"], [("mvraw", l, cnd)])
                stt(mv[:, l, cnd, 6, :], mv[:, l, cnd, 1, :], 1.0, gcol(l, G_MIX, 8), ALU.add, ALU.mult,
                    [("mvraw", l, cnd), "gains"], [("mv", l, cnd)])
                stt(mv[:, l, cnd, 7, :], mv[:, l, cnd, 4, :], 1.0, gcol(l, G_FFN, 8), ALU.add, ALU.mult,
                    [("mvraw", l, cnd), "gains"], [("mv", l, cnd)])

        def MV(l, cnd, kind, k):
            return mv[:, l, cnd, kind, k:k + 1]
        K_SH1, K_GATE1, K_SH2, K_GATE2, K_G1, K_G2 = 0, 2, 3, 5, 6, 7

        def xkeys(g):
            return [("xT", g, k) for k in range(8)]

        def norm_mod(l, g, kG, kSH, dst, dst_key):
            cnd = 1 if g == 4 else 0
            T = slice(g * N, (g + 1) * N)
            b = bank("B")
            for k in range(8):
                sq_, ksq = tmp(tsq, "tsq")
                act(sq_[:], xT[:, k, T], AF.Square, [("xT", g, k)], [ksq])
                mm(ps[b][:, 0:N], ones_b[:], sq_[:], k == 0, k == 7, [ksq, "ones_b"], [("ps", b)])
                nexttmp()
            t1, k1 = tmp(tf, "tf")
            act(t1[:], ps[b][:, 0:N], AF.Ln, [("ps", b), "epsb"], [k1], scale=1.0 / D, bias=epsb[:, 0:1])
            act(rstd_x[:], t1[:], AF.Exp, [k1], ["rstd_x"], scale=-0.5)
            nexttmp()
            for k in range(8):
                t2, k2 = tmp(tg, "tg")
                stt(t2[:], xT[:, k, T], MV(l, cnd, kG, k), rstd_x[:], ALU.mult, ALU.mult,
                    [("xT", g, k), ("mv", l, cnd), "rstd_x"], [k2])
                act(dst(k), t2[:], AF.Identity, [k2, ("mvraw", l, cnd)], [dst_key(k)], bias=MV(l, cnd, kSH, k))
                nexttmp()

        def headnorm(pb, M, d, bo, gain, outs, extra_reads=(), src=None, src_key=None):
            srcap = ps[pb][0:M, 0:N] if src is None else src
            skey = ("ps", pb) if src_key is None else src_key
            sq_, ksq = tmp(tsq, "tsq")
            act(sq_[0:M, :], srcap, AF.Square, [skey], [ksq])
            b2 = bank("B")
            mm(ps[b2][0:M, 0:N], bo[0:M, 0:M], sq_[0:M, :], True, True, [ksq, "bones"], [("ps", b2)])
            t1, k1 = tmp(tf, "tf")
            act(t1[0:M, :], ps[b2][0:M, 0:N], AF.Ln, [("ps", b2), "epsb"], [k1], scale=1.0 / d, bias=epsb[0:M, 0:1])
            r_, kr_ = tmp(tr, "tr")
            act(r_[0:M, :], t1[0:M, :], AF.Exp, [k1], [kr_], scale=-0.5)
            for (oap, okey) in outs:
                stt(oap, srcap, gain, r_[0:M, :], ALU.mult, ALU.mult, [skey, kr_, "gains"] + list(extra_reads), [okey])
            nexttmp()

        def rope(src_f, M, which, dst, reads, writes):
            ro = 0 if which == "mla" else 2
            R = rmat[0:M, 0:M] if which == "mla" else rmat[0:M, 128:128 + M]
            b = bank("B")
            mm(ps[b][0:M, 0:N], R, src_f, True, True, list(reads) + ["rmat"], [("ps", b)])
            t1, k1 = tmp(tf, "tf")
            tt(t1[0:M, :], ps[b][0:M, 0:N], ropet[0:M, (ro + 1) * N:(ro + 2) * N], ALU.mult, [("ps", b), "ropet"], [k1])
            t2, k2 = tmp(tg, "tg")
            tt(t2[0:M, :], src_f, ropet[0:M, ro * N:(ro + 1) * N], ALU.mult, list(reads) + ["ropet"], [k2])
            tt(dst, t1[0:M, :], t2[0:M, :], ALU.add, [k1, k2], writes)
            nexttmp()

        def mla_k_from(l, w, ckvT_b, ckv_key, kr_f, kr_key, sample_rope, dstKT, dst_key):
            rsm = ring[w.small]
            for h in range(6):
                ba = bank("A")
                mm(ps[ba][0:64, 0:N], rsm[:, 1152 + h * 64:1152 + h * 64 + 64], ckvT_b, True, True,
                   [("ring", w.small), ckv_key], [("ps", ba)])
                ring_touch(w.small)
                sq_, ksq = tmp(tsq, "tsq")
                act(sq_[0:64, :], ps[ba][0:64, 0:N], AF.Square, [("ps", ba)], [ksq])
                act(sq_[64:96, :], kr_f[64:96, :], AF.Square, [kr_key], [ksq])
                b2 = bank("B")
                mm(ps[b2][0:96, 0:N], BO96[0:96, 0:96], sq_[0:96, :], True, True, [ksq, "bones"], [("ps", b2)])
                t1, k1 = tmp(tf, "tf")
                act(t1[0:96, :], ps[b2][0:96, 0:N], AF.Ln, [("ps", b2), "epsb"], [k1], scale=1.0 / 96, bias=epsb[0:96, 0:1])
                r_, kr_ = tmp(tr, "tr")
                act(r_[0:96, :], t1[0:96, :], AF.Exp, [k1], [kr_], scale=-0.5)
                if not sample_rope:
                    stt(dstKT(h)[0:64, :], ps[ba][0:64, 0:N], gcol(l, G_MK)[0:64, :], r_[0:64, :], ALU.mult, ALU.mult,
                        [("ps", ba), kr_, "gains"], [dst_key(h)])
                    stt(dstKT(h)[64:96, :], kr_f[64:96, :], gcol(l, G_MK)[64:96, :], r_[64:96, :], ALU.mult, ALU.mult,
                        [kr_key, kr_, "gains"], [dst_key(h)])
                    nexttmp()
                else:
                    nexttmp()
                    kf, kfk = knf[:, 4, :], ("knf", 4)
                    stt(kf[0:64, :], ps[ba][0:64, 0:N], gcol(l, G_MK)[0:64, :], r_[0:64, :], ALU.mult, ALU.mult,
                        [("ps", ba), kr_, "gains"], [kfk])
                    stt(kf[64:96, :], kr_f[64:96, :], gcol(l, G_MK)[64:96, :], r_[64:96, :], ALU.mult, ALU.mult,
                        [kr_key, kr_, "gains"], [kfk])
                    rope(kf[0:96, :], 96, "mla", dstKT(h)[0:96, :], [kfk], [dst_key(h)])

        tsq2 = sb("tsqx", [128, N], BF16)
        P4 = (4, 5, 0, 1)
        pcnt = {"p": 0, "j": 0}

        def bank4():
            b_ = P4[pcnt["p"] % 4]
            pcnt["p"] += 1
            return b_

        def run_norm_pipeline(jobs):
            n = len(jobs)
            for i in range(min(2, n)):
                jobs[i]["P"]()
                jobs[i]["Q"]()
            for i in range(n):
                jobs[i]["R"]()
                jobs[i]["T"]()
                if i + 2 < n:
                    jobs[i + 2]["P"]()
                    jobs[i + 2]["Q"]()
                jobs[i]["U"]()

        def mk_job(P, M, d, bo, U, sq_extra=None, nop_norm=False, R_custom=None):
            st = {}
            j = pcnt["j"]
            pcnt["j"] += 1
            ti = j % NT
            sq_, ksq = tsq[ti], ("tsq", ti)
            t1, k1 = tf[ti], ("tf", ti)
            r_, kr_ = tr[ti], ("tr", ti)
            st.update(sq=sq_, ksq=ksq, r=r_, kr=kr_, ti=ti)

            def P_():
                P(st)

            def Q_():
                if nop_norm:
                    return
                pb = st["pb"]
                rows = st.get("rows", M)
                act(sq_[0:rows, :], ps[pb][0:rows, 0:N], AF.Square, [("ps", pb)], [ksq])
                if sq_extra is not None:
                    sq_extra(st)

            def R_():
                if nop_norm:
                    return
                if R_custom is not None:
                    R_custom(st)
                    return
                b2 = bank("B")
                st["b2"] = b2
                mm(ps[b2][0:M, 0:N], bo[0:M, 0:M], sq_[0:M, :], True, True, [ksq, "bones", "ones_b"], [("ps", b2)])

            def T_():
                if nop_norm:
                    return
                b2 = st["b2"]
                act(t1[0:M, :], ps[b2][0:M, 0:N], AF.Ln, [("ps", b2), "epsb"], [k1], scale=1.0 / d, bias=epsb[0:M, 0:1])
                act(r_[0:M, :], t1[0:M, :], AF.Exp, [k1], [kr_], scale=-0.5)

            def U_():
                U(st)
            return {"P": P_, "Q": Q_, "R": R_, "T": T_, "U": U_}

        def mk_mla_k_job(l, w, h, ckvT_b, ckv_key, kr_f, kr_key, sample_rope, dstKT, dst_key):
            rsm_ = ring[w.small]

            def P(st):
                pb = bank4()
                st["pb"] = pb
                st["rows"] = 64
                mm(ps[pb][0:64, 0:N], rsm_[:, 1152 + h * 64:1152 + h * 64 + 64], ckvT_b, True, True,
                   [("ring", w.small), ckv_key], [("ps", pb)])
                ring_touch(w.small)

            def sqx(st):
                act(st["sq"][64:96, :], kr_f[64:96, :], AF.Square, [kr_key], [st["ksq"]])

            def U(st):
                pb, r_, kr_ = st["pb"], st["r"], st["kr"]
                if not sample_rope:
                    d0, dk = dstKT(h), dst_key(h)
                else:
                    d0, dk = knf[:, 4, :], ("knf", 4)
                stt(d0[0:64, :], ps[pb][0:64, 0:N], gcol(l, G_MK)[0:64, :], r_[0:64, :], ALU.mult, ALU.mult,
                    [("ps", pb), kr_, "gains"], [dk])
                stt(d0[64:96, :], kr_f[64:96, :], gcol(l, G_MK)[64:96, :], r_[64:96, :], ALU.mult, ALU.mult,
                    [kr_key, kr_, "gains"], [dk])
                if sample_rope:
                    rope(d0[0:96, :], 96, "mla", dstKT(h)[0:96, :], [dk], [dst_key(h)])
            return mk_job(P, 96, 96, BO96, U, sq_extra=sqx)

        def front(l, g, w, part="all"):
            do_q = part in ("all", "q")
            do_kv = part in ("all", "kv")
            sample = (g == 4)
            norm_mod(l, g, K_G1, K_SH1, lambda k: hT[:, k, :], lambda k: ("hT", k))
            rsm = ring[w.small]

            def proj(col, M, pb, po=0):
                for k in range(8):
                    wap, wkey, s_ = win_ap(w, col, M, k)
                    mm(ps[pb][po:po + M, 0:N], wap, hT[:, k, :], k == 0, k == 7, [wkey, ("hT", k)], [("ps", pb)])
                    ring_touch(s_)

            def chunk_job(col, M, d, bo, gain, outs, post=None):
                def P(st):
                    st["pb"] = bank4()
                    proj(col, M, st["pb"])

                def U(st):
                    pb, r_, kr_ = st["pb"], st["r"], st["kr"]
                    for (oap, okey) in outs:
                        stt(oap, ps[pb][0:M, 0:N], gain, r_[0:M, :], ALU.mult, ALU.mult, [("ps", pb), kr_, "gains"], [okey])
                    if post is not None:
                        post()
                return mk_job(P, M, d, bo, U)

            jobs = []
            if do_q:
                def P_cq(st):
                    st["pb"] = bank4()
                    st["pb1"] = bank4()
                    proj(C_CQ0, 128, st["pb"])
                    proj(C_CQ1, 128, st["pb1"])

                def sq_cq(st):
                    act(tsq2[:], ps[st["pb1"]][:, 0:N], AF.Square, [("ps", st["pb1"])], ["tsq2"])

                def U_cq(st):
                    r_, kr_ = st["r"], st["kr"]
                    stt(cqn[:, 0, :], ps[st["pb"]][:, 0:N], gcol(l, G_QA), r_[:], ALU.mult, ALU.mult, [("ps", st["pb"]), kr_, "gains"], [("cqn", 0)])
                    stt(cqn[:, 1, :], ps[st["pb1"]][:, 0:N], gcol(l, G_QA + 1), r_[:], ALU.mult, ALU.mult, [("ps", st["pb1"]), kr_, "gains"], [("cqn", 1)])
                def R_cq(st):
                    b2 = bank("B")
                    st["b2"] = b2
                    mm(ps[b2][:, 0:N], ones_b[:], st["sq"][:], True, False, [st["ksq"], "ones_b"], [("ps", b2)])
                    mm(ps[b2][:, 0:N], ones_b[:], tsq2[:], False, True, ["tsq2", "ones_b"], [("ps", b2)])
                jobs.append(mk_job(P_cq, 128, 256, ones_b, U_cq, sq_extra=sq_cq, R_custom=R_cq))
            if do_kv:
                jobs.append(chunk_job(C_CKV, 128, 128, ones_b, gcol(l, G_KVA), [(ckvn_f[:], "ckvn_f"), (ckvn_b[:], "ckvn_b")]))

                def P_kr(st):
                    st["pb"] = bank4()
                    proj(C_KR, 32, st["pb"], po=64)

                def U_kr(st):
                    cp(krT[64:96, :], ps[st["pb"]][64:96, 0:N], [("ps", st["pb"])], ["krT"])
                jobs.append(mk_job(P_kr, 32, 32, ones_b, U_kr, nop_norm=True))
            if do_q:
                for c in range(3):
                    jobs.append(chunk_job(C_NAQ + c * 128, 128, 64, BO2, gcol(l, G_NQ), [(QT[:, 6 + c, :], ("QT", 6 + c))]))
                uq = rsm[:, 0:1152].rearrange("p (k c) -> p k c", k=2)
                for h in range(6):
                    def P_mq(st, h=h):
                        st["pb"] = bank4()
                        for k2 in range(2):
                            mm(ps[st["pb"]][0:96, 0:N], uq[:, k2, h * 96:(h + 1) * 96], cqn[:, k2, :], k2 == 0, k2 == 1,
                               [("ring", w.small), ("cqn", k2)], [("ps", st["pb"])])
                        ring_touch(w.small)

                    def U_mq(st, h=h):
                        pb, r_, kr_ = st["pb"], st["r"], st["kr"]
                        if not sample:
                            stt(QT[0:96, h, :], ps[pb][0:96, 0:N], gcol(l, G_MQ)[0:96, :], r_[0:96, :], ALU.mult, ALU.mult,
                                [("ps", pb), kr_, "gains"], [("QT", h)])
                        else:
                            stt(knf[0:96, 3, :], ps[pb][0:96, 0:N], gcol(l, G_MQ)[0:96, :], r_[0:96, :], ALU.mult, ALU.mult,
                                [("ps", pb), kr_, "gains"], [("knf", 3)])
                            rope(knf[0:96, 3, :], 96, "mla", QT[0:96, h, :], [("knf", 3)], [("QT", h)])
                    jobs.append(mk_job(P_mq, 96, 96, BO96, U_mq))
            if do_kv:
                for c in range(3):
                    outs = [(KT[:, 6 + c, :], ("KT", 6 + c))]
                    if not sample:
                        outs.append((knf[:, c, :], ("knf", c)))
                    jobs.append(chunk_job(C_NAK + c * 128, 128, 64, BO2, gcol(l, G_NK), outs))
                for h in range(6):
                    jobs.append(mk_mla_k_job(l, w, h, ckvn_b[:], "ckvn_b", krT, "krT", sample,
                                             lambda h_: KT[:, h_, :], lambda h_: ("KT", h_)))
            if do_q:
                for c in range(2):
                    def post_q(c=c):
                        if sample:
                            rope(knf[:, 3, :], 128, "df", QT[:, 9 + c, :], [("knf", 3)], [("QT", 9 + c)])
                        for m_ in range(2):
                            ts(QTm[:, 2 * c + m_, :], QT[:, 9 + c, :], gains[:, DEPTH * GCOLS + m_:DEPTH * GCOLS + m_ + 1], None, ALU.mult, None,
                               [("QT", 9 + c), "gains"], [("QTm", 2 * c + m_)])
                    outs = [(QT[:, 9 + c, :], ("QT", 9 + c))] if not sample else [(knf[:, 3, :], ("knf", 3))]
                    jobs.append(chunk_job(C_DFQ + c * 128, 128, 32, BO4, gcol(l, G_DQ), outs, post=post_q))
            if do_kv:
                for c in range(2):
                    if not sample:
                        jobs.append(chunk_job(C_DFK + c * 128, 128, 32, BO4, gcol(l, G_DK),
                                              [(KT[:, 9 + c, :], ("KT", 9 + c)), (knf[:, 3 + c, :], ("knf", 3 + c))]))
                    else:
                        def post_k(c=c):
                            rope(knf[:, 3, :], 128, "df", KT[:, 9 + c, :], [("knf", 3)], [("KT", 9 + c)])
                        jobs.append(chunk_job(C_DFK + c * 128, 128, 32, BO4, gcol(l, G_DK), [(knf[:, 3, :], ("knf", 3))], post=post_k))
            run_norm_pipeline(jobs)
            if do_kv:
                vsrc = rsm[:, 1536:1920]
                for t in range(2):
                    bv = bank("A")
                    mm(ps[bv][:, 0:384], ckvn_b[:, t * 128:(t + 1) * 128], vsrc, True, True,
                       [("ring", w.small), "ckvn_b"], [("ps", bv)])
                    ring_touch(w.small)
                    cp(VP4[:, t * 8:t * 8 + 3, 0:3:2, :], ps[bv][:, 0:384].rearrange("p (q a c) -> p q a c", q=3, a=2),
                       [("ps", bv)], [("VP", t, 0)])
            sub(8, g)
            for t in range(2 if do_kv else 0):
                b1_, b2_ = bank("A"), bank("A")
                for k in range(8):
                    wap, wkey, s = win_ap(w, C_NAV, 384, k)
                    mm(ps[b1_][:, 0:384], hT[:, k, t * 128:(t + 1) * 128], wap, k == 0, k == 7, [wkey, ("hT", k)], [("ps", b1_)])
                    ring_touch(s)
                for k in range(8):
                    wap, wkey, s = win_ap(w, C_DFV, 256, k)
                    mm(ps[b2_][:, 0:256], hT[:, k, t * 128:(t + 1) * 128], wap, k == 0, k == 7, [wkey, ("hT", k)], [("ps", b2_)])
                    ring_touch(s)
                if not sample:
                    act(vstage[:, 0:384], ps[b1_][:, 0:384], AF.Copy, [("ps", b1_)], [("vstage", 0)])
                    act(vstage[:, 384:640], ps[b2_][:, 0:256], AF.Copy, [("ps", b2_)], [("vstage", 1)])
                if sample:
                    cp(VP4[:, t * 8 + 3:t * 8 + 6, 0:3:2, :], ps[b1_][:, 0:384].rearrange("p (q a c) -> p q a c", q=3, a=2),
                       [("ps", b1_)], [("VP", t, 1)])
                    cp(VP4[:, t * 8 + 6:t * 8 + 8, 0:3:2, :], ps[b2_][:, 0:256].rearrange("p (q a c) -> p q a c", q=2, a=2),
                       [("ps", b2_)], [("VP", t, 2)])
                else:
                    cp(VP4[:, t * 8 + 3:t * 8 + 6, 0:3:2, :], vstage[:, 0:384].rearrange("p (q a c) -> p q a c", q=3, a=2),
                       [("vstage", 0)], [("VP", t, 1)])
                    cp(VP4[:, t * 8 + 6:t * 8 + 8, 0:3:2, :], vstage[:, 384:640].rearrange("p (q a c) -> p q a c", q=2, a=2),
                       [("vstage", 1)], [("VP", t, 2)])
                if not sample and "vo" not in KSKIP:
                    rv = [("vstage", 0), ("vstage", 1)]
                    kb.dma("sp", "o_nav", o_nav[g, l][:, t * 128:(t + 1) * 128, :].rearrange("h p c -> p h c"),
                           vstage[:, 0:384].rearrange("p (h c) -> p h c", h=6), reads=rv, writes=[("out", "nav", g, l, t)], final=True)
                    kb.dma("sp", "o_dfv", o_dfv[g, l][:, t * 128:(t + 1) * 128, :].rearrange("h p c -> p h c"),
                           vstage[:, 384:640].rearrange("p (h c) -> p h c", h=4), reads=rv, writes=[("out", "dfv", g, l, t)], final=True)

        def prompt_outputs(l, g):
            srcs = [(ckvn_f, None, "ckvn_f"), (krT, None, "krT")] + [(knf, c, ("knf", c)) for c in range(5)]
            for t in range(2):
                kst_ = kstage if t == 0 else kstage2
                for half in range(2):
                    b = bank("B")
                    lst = srcs[0:4] if half == 0 else srcs[4:7]
                    for j, (tile_, c, key) in enumerate(lst):
                        src = tile_[:, t * 128:(t + 1) * 128] if c is None else tile_[:, c, t * 128:(t + 1) * 128]
                        tp(ps[b][:, j * 128:(j + 1) * 128], src, ident[:], [key, "ident"], [("ps", b)])
                    n = len(lst)
                    j0 = 0 if half == 0 else 4
                    if half == 0:
                        act(kst_[:, j0:j0 + n, :], ps[b][:, 0:n * 128].rearrange("p (j c) -> p j c", j=n), AF.Copy,
                            [("ps", b)], [("kstage", t, half)])
                    else:
                        cp(kst_[:, j0:j0 + n, :], ps[b][:, 0:n * 128].rearrange("p (j c) -> p j c", j=n),
                           [("ps", b)], [("kstage", t, half)])
                rk = [("kstage", t, 0), ("kstage", t, 1)]
                kb.dma("sp", "o_ckv", o_ckv[g, l][t * 128:(t + 1) * 128, :], kst_[:, 0, :], reads=rk, writes=[("out", "ckv", g, l, t)], final=True)
                kb.dma("sp", "o_kr", o_kr[g, l][t * 128:(t + 1) * 128, :], kst_[:, 1, 64:96], reads=rk, writes=[("out", "kr", g, l, t)], final=True)
                kb.dma("sp", "o_nak", o_nak[g, l][:, t * 128:(t + 1) * 128, :].rearrange("h p c -> p h c"),
                       kst_[:, 2:5, :].rearrange("p j (a c) -> p (j a) c", a=2), reads=rk, writes=[("out", "nak", g, l, t)], final=True)
                kb.dma("sp", "o_dfk", o_dfk[g, l][:, t * 128:(t + 1) * 128, :].rearrange("h p c -> p h c"),
                       kst_[:, 5:7, :].rearrange("p j (a c) -> p (j a) c", a=2), reads=rk, writes=[("out", "dfk", g, l, t)], final=True)

        def head_maps(hh):
            if hh < 6:
                return [(hh, 0, 96, 96 ** -0.5, None)]
            if hh < 12:
                j = hh - 6
                return [(6 + j // 2, (j % 2) * 64, 64, 0.125, None)]
            j = hh - 12
            return [(9 + j // 2, (j % 2) * 64, 64, 32 ** -0.5, 2 * (j // 2) + m) for m in range(2)]

        def vblock(tile_, base, c, hh, nh):
            blk = c * 8 + hh // 2 if nh == 16 else c
            if hh % 2 == 0:
                return tile_[:, blk, 0:128]
            return tile_[:, blk, 64:192]

        def attn_norm(hh, ob, dst, dkey, rdkeys):
            i = cnt["rc"] % 2
            cnt["rc"] += 1
            lo, hi = (0, 64) if hh % 2 == 0 else (64, 128)
            dl, dh = (64, 128) if hh % 2 == 0 else (0, 64)
            act(rc[i][lo:hi, :], ps[ob][dl:dh, 0:N], AF.Ln, [("ps", ob)], [("rc", i)])
            act(rc[i][lo:hi, :], rc[i][lo:hi, :], AF.Exp, [("rc", i)], [("rc", i)], scale=-1.0)
            tt(dst[lo:hi, :], ps[ob][lo:hi, 0:N], rc[i][lo:hi, :], ALU.mult, [("ps", ob), ("rc", i)] + rdkeys, [dkey])

        def df_finish(l, pi):
            sq_, ksq = tmp(tsq, "tsq")
            act(sq_[:], odf[:], AF.Square, ["odf0", "odf1"], [ksq])
            b2 = bank("B")
            mm(ps[b2][:, 0:N], BO2, sq_[:], True, True, [ksq, "bones"], [("ps", b2)])
            t1, k1 = tmp(tf, "tf")
            act(t1[:], ps[b2][:, 0:N], AF.Ln, [("ps", b2), "epsb"], [k1], scale=1.0 / 64, bias=epsb[:, 0:1])
            r_, kr_ = tmp(tr, "tr")
            act(r_[:], t1[:], AF.Exp, [k1], [kr_], scale=-0.5)
            t2, k2 = tmp(tg, "tg")
            stt(t2[:], odf[:], gcol(l, G_DS), r_[:], ALU.mult, ALU.mult, ["odf0", "odf1", kr_, "gains"], [k2])
            ts(mixT[:, pi, :], t2[:], 1.0 - LAM_INIT[l], None, ALU.mult, None, [k2], [("hT", pi)])
            nexttmp()

        def run_pipeline(jobs):
            n = len(jobs)
            if not n:
                return
            jobs[0][0]()
            if n > 1:
                jobs[1][0]()
            jobs[0][1]()
            for i in range(n):
                jobs[i][2]()
                if i + 2 < n:
                    jobs[i + 2][0]()
                if i + 1 < n:
                    jobs[i + 1][1]()
                jobs[i][3]()

        def attention_prompt(l, g):
            jobs = []
            for hh in range(16):
                pi = hh // 2
                maps = head_maps(hh)
                obs = []
                for mi, (qc, pb, kr, scale, qm) in enumerate(maps):
                    st = {}

                    def S_(st=st, qc=qc, pb=pb, kr=kr, qm=qm):
                        sbk = bank("S")
                        st["sbk"] = sbk
                        qap = QT[pb:pb + kr, qc, :] if qm is None else QTm[pb:pb + kr, qm, :]
                        qkey = ("QT", qc) if qm is None else ("QTm", qm)
                        for kc in range(2):
                            mm(ps[sbk][:, kc * N:(kc + 1) * N], KT[pb:pb + kr, qc, kc * 128:(kc + 1) * 128], qap,
                               True, True, [("KT", qc), qkey], [("ps", sbk)])

                    def E_(st=st, scale=scale):
                        pi_ = cnt["pt"] % 3
                        cnt["pt"] += 1
                        st["pt"] = pi_
                        act(PT[pi_][:], ps[st["sbk"]][:, :], AF.Exp, [("ps", st["sbk"])], [("PT", pi_)], scale=scale)

                    def PV_(st=st, hh=hh, obs=obs):
                        ob = bank("O")
                        pi_ = st["pt"]
                        grp = 0 if hh < 6 else (1 if hh < 12 else 2)
                        for kc in range(2):
                            mm(ps[ob][:, 0:N], vblock(VP, 1, kc, hh, 16), PT[pi_][:, kc * N:(kc + 1) * N], kc == 0, kc == 1,
                               [("VP", kc, grp), "VPones", ("PT", pi_)], [("ps", ob)])
                        obs.append(ob)

                    def POST_(hh=hh, pi=pi, obs=obs, last=(mi == len(maps) - 1)):
                        if not last:
                            return
                        if hh < 12:
                            attn_norm(hh, obs[0], mixT[:, pi, :], ("hT", pi), [])
                        else:
                            df_combine(l, hh, obs)
                            if hh % 2 == 1:
                                df_finish(l, pi)
                    jobs.append((S_, E_, PV_, POST_))
            run_pipeline(jobs)

        def df_combine(l, hh, obs):
            lo = 0 if hh % 2 == 0 else 64
            t1, k1 = tmp(tf, "tf")
            attn_norm(hh, obs[0], t1, k1, [])
            t2, k2 = tmp(tg, "tg")
            attn_norm(hh, obs[1], t2, k2, [])
            stt(odf[lo:lo + 64, :], t2[lo:lo + 64, :], nlam[lo:lo + 64, l:l + 1], t1[lo:lo + 64, :], ALU.mult, ALU.add,
                [k1, k2, ("nlam", l)], ["odf%d" % (hh % 2)])
            nexttmp()

        def out_proj(l, g, w):
            cnd = 1 if g == 4 else 0
            T = slice(g * N, (g + 1) * N)
            for f in range(8):
                ba = bank("A")
                s = w.wout[f // 4]
                view = ring[s][:, :].rearrange("p (k c) -> p k c", k=8)
                for k in range(8):
                    mm(ps[ba][:, 0:N], view[:, k, (f % 4) * 128:(f % 4 + 1) * 128], mixT[:, k, :], k == 0, k == 7,
                       [("ring", s), ("hT", k)], [("ps", ba)])
                ring_touch(s)
                stt(xT[:, f, T], ps[ba][:, 0:N], MV(l, cnd, K_GATE1, f), xT[:, f, T], ALU.mult, ALU.add,
                    [("ps", ba), ("mvraw", l, cnd), ("xT", g, f)], [("xT", g, f)])

        def sample_ctx(l, w):
            rsm = ring[w.small]
            kb.dma("sp", "cst", cst[:], c_ckv_d[l].rearrange("(t p) c -> p t c", p=128), writes=["cst"])
            kb.dma("sp", "cst2", cst2[:, :, 64:96], c_kr_d[l].rearrange("(t p) c -> p t c", p=128), writes=["cst2"])
            for t in range(2):
                kb.dma("sp", "cstk", cstk[:, t], c_nak_d[l][:, t * 128:(t + 1) * 128, :].rearrange("h p c -> p h c"), writes=["cstk"])
                kb.dma("sp", "cstd", cstd[:, t], c_dfk_d[l][:, t * 128:(t + 1) * 128, :].rearrange("h p c -> p h c"), writes=["cstd"])
            for t in range(2):
                for a_ in range(2):
                    kb.dma("pool", f"vpc_na{t}{a_}", VPc4[:, t * 8 + 3:t * 8 + 6, 2 * a_, :],
                           c_nav_d[l][a_:6:2, t * 128:(t + 1) * 128, :].rearrange("h p c -> p h c"), writes=[("VPc", t, 1, a_)])
                    kb.dma("pool", f"vpc_df{t}{a_}", VPc4[:, t * 8 + 6:t * 8 + 8, 2 * a_, :],
                           c_dfv_d[l][a_:4:2, t * 128:(t + 1) * 128, :].rearrange("h p c -> p h c"), writes=[("VPc", t, 2, a_)])
            b = bank("B")
            for t in range(2):
                tp(ps[b][:, t * 128:(t + 1) * 128], cst[:, t, :], ident[:], ["cst", "ident"], [("ps", b)])
            cp(Ebuf[0][:, 0, :], ps[b][:, 0:N], [("ps", b)], [("Ebuf", 0, 0)])
            ckvc_b = Ebuf[0][:, 0, :]
            b = bank("B")
            for t in range(2):
                tp(ps[b][:, t * 128:(t + 1) * 128], cst2[:, t, :], ident[:], ["cst2", "ident"], [("ps", b)])
            krc = Est[0][:, 0, :]
            cp(krc, ps[b][:, 0:N], [("ps", b)], [("Est", 0)])
            run_norm_pipeline([mk_mla_k_job(l, w, h, ckvc_b, ("Ebuf", 0, 0), krc, ("Est", 0), False,
                                            lambda h_: KTc[:, h_, :], lambda h_: ("KTc", h_)) for h in range(6)])
            vsrc = rsm[:, 1536:1920]
            for t in range(2):
                bv = bank("A")
                mm(ps[bv][:, 0:384], ckvc_b[:, t * 128:(t + 1) * 128], vsrc, True, True,
                   [("ring", w.small), ("Ebuf", 0, 0)], [("ps", bv)])
                ring_touch(w.small)
                cp(VPc4[:, t * 8:t * 8 + 3, 0:3:2, :], ps[bv][:, 0:384].rearrange("p (q a c) -> p q a c", q=3, a=2),
                   [("ps", bv)], [("VPc", t, 0)])
            for c in range(3):
                b = bank("B")
                for t in range(2):
                    tp(ps[b][:, t * 128:(t + 1) * 128], cstk[:, t, 2 * c:2 * c + 2, :].rearrange("p h c -> p (h c)"), ident[:],
                       ["cstk", "ident"], [("ps", b)])
                cp(KTc[:, 6 + c, :], ps[b][:, 0:N], [("ps", b)], [("KTc", 6 + c)])
            for c in range(2):
                b = bank("B")
                for t in range(2):
                    tp(ps[b][:, t * 128:(t + 1) * 128], cstd[:, t, 2 * c:2 * c + 2, :].rearrange("p h c -> p (h c)"), ident[:],
                       ["cstd", "ident"], [("ps", b)])
                cp(KTc[:, 9 + c, :], ps[b][:, 0:N], [("ps", b)], [("KTc", 9 + c)])

        def sample_exchange(l):
            kb.dma("sp", "exk", exk_in[l].ap().rearrange("(c p) t -> p c t", p=128), KT[:],
                   reads=[("KT", c) for c in range(11)], writes=[("exin", l, "k")])
            kb.coll(f"exk{l}", lambda e, l=l: e.collective_compute("AllGather", ALU.bypass, replica_groups=RG,
                                                                  ins=[exk_in[l].ap().opt()], outs=[exk_out[l].ap().opt()]),
                    reads=[("exin", l, "k")], writes=[("exoutk", l)])
            vreg = exv_in[l].ap().rearrange("(tok a) c -> tok (a c)", a=4).rearrange("(t p) f -> p t f", p=128)
            for t in range(2):
                for a_ in range(2):
                    kb.dma("sp", f"exv{t}{a_}", vreg[:, t, :].rearrange("p (q a c) -> p q a c", q=8, a=2)[:, :, a_, :],
                           VP4[:, t * 8:(t + 1) * 8, 2 * a_, :],
                           reads=[("VP", t, j) for j in range(3)], writes=[("exin", l, "v", t, a_)])
            kb.coll(f"exv{l}", lambda e, l=l: e.collective_compute("AllGather", ALU.bypass, replica_groups=RG,
                                                                  ins=[exv_in[l].ap().opt()], outs=[exv_out[l].ap().opt()]),
                    reads=[("exin", l, "v", t, a_) for t in range(2) for a_ in range(2)], writes=[("exoutv", l)])

        def attention_sample(l):
            exo = exk_out[l].ap().rearrange("(r x) t -> r x t", r=4)
            vall = exv_out[l].ap().rearrange("(r x) t -> r (x t)", r=4)
            kst = {"i": 0}
            kslot = {}
            vdone = set()

            def prestage(hh):
                if hh >= 16:
                    return
                pi = hh // 2
                st_ = pi % 2
                kc_ = head_maps(hh)[0][0]
                if kc_ not in kslot:
                    si = kst["i"] % 2
                    kst["i"] += 1
                    kslot[kc_] = si
                    kb.dma("sp", f"ktst{si}", KTst[si][:, :].rearrange("p (r t) -> p r t", r=4),
                           exo[:, kc_ * 128:(kc_ + 1) * 128, :].rearrange("r p t -> p r t"),
                           reads=[("exoutk", l)], writes=[("KTst", si)])
                for pv in (pi,):
                    if pv in vdone or pv >= 8:
                        continue
                    vdone.add(pv)
                    sv = pv % 2
                    for r in range(4):
                        for a_ in range(2):
                            src = vall[r, :].rearrange("(t p f) -> p t f", t=2, p=128)[:, :, pv * 128 + a_ * 64:pv * 128 + a_ * 64 + 64]
                            kb.dma("pool", f"vpst{sv}_{r}{a_}", VPst4[sv][:, r * 2:r * 2 + 2, 2 * a_, :],
                                   src, reads=[("exoutv", l)], writes=[("VPst", sv, r, a_)])
                if 6 <= hh < 12:
                    hn = hh - 6
                    eb = hn % 2
                    for qd in range(4):
                        es_ = qd % 2
                        kb.dma("sp", f"est{es_}", Est[es_][:], biasm_d[l, hn, qd * 256:(qd + 1) * 256, :].rearrange("(c p) q -> p c q", p=128),
                               writes=[("Est", es_)])
                        act(Ebuf[eb][:, qd * 2:(qd + 1) * 2, :], Est[es_][:], AF.Exp, [("Est", es_)], [("Ebuf", eb, qd)])

            jobs = []
            for hh in range(16):
                pi = hh // 2
                st_ = pi % 2
                maps = head_maps(hh)
                obs = []
                isna = 6 <= hh < 12
                eb = (hh - 6) % 2
                for mi, (qc, pb, kr, scale, qm) in enumerate(maps):
                    hst = {}
                    for cp_ in range(5):
                        st = {}

                        def S_(st=st, hst=hst, qc=qc, pb=pb, kr=kr, qm=qm, cp_=cp_, hh=hh, mi=mi):
                            if cp_ == 0 and mi == 0:
                                prestage(hh + 1)
                            si = kslot[qc]
                            qap = QT[pb:pb + kr, qc, :] if qm is None else QTm[pb:pb + kr, qm, :]
                            qkey = ("QT", qc) if qm is None else ("QTm", qm)
                            sbk = bank("S")
                            st["sbk"] = sbk
                            for kk in range(2):
                                c = cp_ * 2 + kk
                                if c < 2:
                                    lhs = KTc[pb:pb + kr, qc, c * 128:(c + 1) * 128]
                                    rk = [("KTc", qc)]
                                else:
                                    lhs = KTst[si][pb:pb + kr, (c - 2) * 128:(c - 1) * 128]
                                    rk = [("KTst", si)]
                                mm(ps[sbk][:, kk * N:(kk + 1) * N], lhs, qap, True, True, rk + [qkey], [("ps", sbk)])

                        def E_(st=st, scale=scale, cp_=cp_, isna=isna, eb=eb):
                            pi_ = cnt["pt"] % 3
                            cnt["pt"] += 1
                            st["pt"] = pi_
                            act(PT[pi_][:], ps[st["sbk"]][:, :], AF.Exp, [("ps", st["sbk"])], [("PT", pi_)], scale=scale)
                            if isna and cp_ >= 1:
                                e0 = (cp_ - 1) * 2
                                tt(PT[pi_][:], PT[pi_][:], Ebuf[eb][:, e0:e0 + 2, :].rearrange("p c q -> p (c q)"), ALU.mult,
                                   [("PT", pi_), ("Ebuf", eb, cp_ - 1)], [("PT", pi_)])

                        def PV_(st=st, hst=hst, hh=hh, cp_=cp_, st_=st_, obs=obs):
                            if cp_ == 0:
                                hst["ob"] = bank("O")
                                obs.append(hst["ob"])
                            ob = hst["ob"]
                            pi_ = st["pt"]
                            grp = 0 if hh < 6 else (1 if hh < 12 else 2)
                            for kk in range(2):
                                c = cp_ * 2 + kk
                                if c < 2:
                                    lhsv = vblock(VPc, 1, c, hh, 16)
                                    rk = ([("VPc", c, 0)] if grp == 0 else [("VPc", c, grp, hh % 2)]) + ["VPcones"]
                                else:
                                    lhsv = vblock(VPst[st_], 1, c - 2, hh, 2)
                                    rk = [("VPst", st_, (c - 2) // 2, hh % 2), ("VPstones", st_)]
                                mm(ps[ob][:, 0:N], lhsv, PT[pi_][:, kk * N:(kk + 1) * N],
                                   c == 0, c == 9, rk + [("PT", pi_)], [("ps", ob)])

                        def POST_(hh=hh, pi=pi, obs=obs, last=(mi == len(maps) - 1 and cp_ == 4)):
                            if not last:
                                return
                            if hh < 12:
                                attn_norm(hh, obs[0], mixT[:, pi, :], ("hT", pi), [])
                            else:
                                df_combine(l, hh, obs)
                                if hh % 2 == 1:
                                    df_finish(l, pi)
                        jobs.append((S_, E_, PV_, POST_))
            prestage(0)
            run_pipeline(jobs)

        def ffn(l, nxt_steps):
            blocks = [None] * len(FFN_BLOCKS)
            blocks[0] = load_ffn_block(l, 0)
            blocks[1] = load_ffn_block(l, 1)
            barrier()
            for g in (0, 1, 2, 3, 4):
                norm_mod(l, g, K_G2, K_SH2, lambda k, g=g: h2T[:, k, g * N:(g + 1) * N], lambda k, g=g: ("h2T", g, k))

            def pump():
                while nxt_steps and ring_free() > 0:
                    nxt_steps.pop(0)()
            pump()
            TB = [(0, 512, 0, (0, 1)), (512, 512, 0, (2, 3)), (1024, 256, 1, (4,))]
            fcnt = {"a": 0, "o": 0}
            OB = (2, 3, 6, 7)
            for bi in range(len(FFN_BLOCKS)):
                w = blocks[bi]
                for (t0_, W, cnd, grps) in TB:
                    T = slice(t0_, t0_ + W)
                    ai = fcnt["a"] % 2
                    fcnt["a"] += 1
                    a_ = aT[ai]
                    akey = ("aT", ai)
                    for c in range(w.n):
                        bg, bu = bank4(), bank4()
                        hk = [("h2T", g_, k) for g_ in grps for k in range(8)]
                        for k in range(8):
                            mm(ps[bg][:, 0:W], w.vg[:, k, c * 128:(c + 1) * 128], h2T[:, k, T], k == 0, k == 7,
                               [("ring", w.g)] + [("h2T", g_, k) for g_ in grps], [("ps", bg)])
                        for k in range(8):
                            mm(ps[bu][:, 0:W], w.vu[:, k, c * 128:(c + 1) * 128], h2T[:, k, T], k == 0, k == 7,
                               [("ring", w.u)] + [("h2T", g_, k) for g_ in grps], [("ps", bu)])
                        si_ = (fcnt["a"] + c) % 2
                        act(silt[si_][:, 0:W], ps[bg][:, 0:W], AF.Silu, [("ps", bg)], [("silt", si_)])
                        tt(a_[:, c, 0:W], ps[bu][:, 0:W], silt[si_][:, 0:W], ALU.mult, [("ps", bu), ("silt", si_)], [akey + (c,)])
                    for f in range(8):
                        b = OB[fcnt["o"] % 4]
                        fcnt["o"] += 1
                        for c in range(w.n):
                            mm(ps[b][:, 0:W], w.vd[:, c, f * 128:(f + 1) * 128], a_[:, c, 0:W], c == 0, c == w.n - 1,
                               [("ring", w.d), akey + (c,)], [("ps", b)])
                        xk = [("xT", g_, f) for g_ in grps]
                        stt(xT[:, f, T], ps[b][:, 0:W], MV(l, cnd, K_GATE2, f), xT[:, f, T], ALU.mult, ALU.add,
                            [("ps", b), ("mvraw", l, cnd)] + xk, xk)
                ring_touch(w.g)
                ring_touch(w.u)
                ring_touch(w.d)
                if bi + 2 < len(FFN_BLOCKS):
                    blocks[bi + 2] = load_ffn_block(l, bi + 2)
                pump()
            assert not nxt_steps

        for l in range(DEPTH):
            w = mixw
            stage("layer")
            barrier()
            mixer_init()
            if "s4" not in KSKIP:
                sample_ctx(l, w)
            stage("front4")
            if "s4" not in KSKIP:
                front(l, 4, w, "kv")
            stage("exch")
            if "s4" not in KSKIP and "ex" not in KSKIP:
                sample_exchange(l)
            for g in range(4):
                stage("frontg")
                front(l, g, w)
                if "po" not in KSKIP:
                    prompt_outputs(l, g)
                stage("attng")
                attention_prompt(l, g)
                out_proj(l, g, w)
            stage("attns")
            front(l, 4, w, "q")
            attention_sample(l)
            if l == 0:
                dbg("mix0", hT[:], [128, 8, N], [("hT", k) for k in range(8)], dt=BF16)
                dbg("qt0", QT[:], [128, 11, N], [("QT", k) for k in range(11)], dt=BF16)
                dbg("ktc0", KTc[:], [128, 11, N], [("KTc", k) for k in range(11)], dt=BF16)
            out_proj(l, 4, w)
            stage("ffn")
            if l + 1 < DEPTH:
                nxt_w, nxt_steps = mixer_loader(l + 1)
            else:
                nxt_w, nxt_steps = None, []
            ffn(l, nxt_steps)
            mixw = nxt_w

        kb.enabled = True
        barrier()
        for g in range(NGRP):
            st_ = 0
            for t in range(2):
                for q4 in range(2):
                    b = bank(("A", "B", "S", "O")[(t * 2 + q4) % 4])
                    for kk in range(4):
                        k = q4 * 4 + kk
                        tp(ps[b][:, kk * 128:(kk + 1) * 128], xT[:, k, g * N + t * 128:g * N + (t + 1) * 128], ident[:],
                           [("xT", g, k), "ident"], [("ps", b)])
                    if q4 == 0:
                        act(xstage[st_][:, t, 0:512], ps[b][:, :], AF.Copy, [("ps", b)], [("xstage", st_)])
                    else:
                        cp(xstage[st_][:, t, 512:1024], ps[b][:, :], [("ps", b)], [("xstage", st_)])
            kb.dma("sp", f"y{st_}", y_d[g].rearrange("(t p) d -> p t d", p=128), xstage[st_][:], reads=[("xstage", st_)],
                   writes=[("out", "y", g)], final=True)

        names = kb.all_sems()
        semd = {}
        for nm in names:
            semd[nm] = es.enter_context(nc.semaphore(nm.replace(":", "_")))
        block = es.enter_context(nc.Block())

        def replay(engname, e, drain=False):
            for (waits, fn, s, amt) in kb.ops[engname]:
                for (ws, v) in waits:
                    e.wait_ge(semd[ws], v)
                fn(e).then_inc(semd[s], amt)
            if drain:
                for s, v in kb.final.items():
                    e.wait_ge(semd[s], v)

        @block.tensor
        def _(e):
            replay("pe", e)

        @block.scalar
        def _(e):
            replay("act", e)

        @block.vector
        def _(e):
            replay("dve", e)

        @block.gpsimd
        def _(e):
            replay("pool", e)

        @block.sync
        def _(e):
            replay("sp", e, drain=True)

    build_program.marks = marks
    return nc, dbg_outs


def _rope_tables(pos0):
    t = np.arange(pos0, pos0 + N)
    row = (t // GRID_W).astype(np.float32)
    col = (t % GRID_W).astype(np.float32)

    def tab(rot_dim):
        n = rot_dim // 4
        inv = (1.0 / (np.float32(10000.0) ** (np.arange(n, dtype=np.float32) * np.float32(2.0) / np.float32(rot_dim // 2)))).astype(np.float32)
        ar = row[:, None] * inv
        ac = col[:, None] * inv
        ang = np.concatenate([ar, ar, ac, ac], axis=-1).astype(np.float32)
        return np.cos(ang).astype(np.float32).T, np.sin(ang).astype(np.float32).T
    cm, sm = tab(32)
    cosm = np.ones((128, N), np.float32)
    sinm = np.zeros((128, N), np.float32)
    cosm[64:96] = cm
    sinm[64:96] = sm
    cd, sd = tab(32)
    cosd = np.tile(cd, (4, 1))
    sind = np.tile(sd, (4, 1))
    return np.concatenate([cosm, sinm, cosd, sind], axis=1)


def _rot_block():
    R = np.zeros((32, 32), np.float32)
    for m in range(8):
        R[8 + m, m] = -1.0
        R[m, 8 + m] = 1.0
        R[24 + m, 16 + m] = -1.0
        R[16 + m, 24 + m] = 1.0
    return R


def _consts():
    ident = np.eye(128, dtype=np.float32)
    Rb = _rot_block()
    Rm = np.zeros((128, 128), np.float32)
    Rm[64:96, 64:96] = Rb
    Rd = np.zeros((128, 128), np.float32)
    for j in range(4):
        Rd[32 * j:32 * j + 32, 32 * j:32 * j + 32] = Rb
    bo2 = np.kron(np.eye(2, dtype=np.float32), np.ones((64, 64), np.float32))
    bo4 = np.kron(np.eye(4, dtype=np.float32), np.ones((32, 32), np.float32))
    bo96 = np.zeros((128, 128), np.float32)
    bo96[0:96, 0:96] = 1.0
    return ident, np.concatenate([Rm, Rd], 1), np.concatenate([bo2, bo4, bo96], 1)


def _bias_index(qrank):
    rows, kr, kw = 16, 8, 16
    keys = np.arange(1024)
    kr_ = keys // GRID_W
    kc_ = keys % GRID_W
    q = np.arange(N) + qrank * N
    qr = q // GRID_W
    qc = q % GRID_W
    rstart = np.clip(qr - kr // 2, 0, rows - kr)
    cstart = np.clip(qc - kw // 2, 0, GRID_W - kw)
    inr = (kr_[:, None] >= rstart[None, :]) & (kr_[:, None] < rstart[None, :] + kr)
    inc = (kc_[:, None] >= cstart[None, :]) & (kc_[:, None] < cstart[None, :] + kw)
    mask = inr & inc
    rel_r = np.clip(kr_[:, None] - qr[None, :] + (kr - 1), 0, 2 * kr - 2)
    rel_c = np.clip(kc_[:, None] - qc[None, :], -(kw - 1), kw - 1) + (kw - 1)
    return mask, rel_r, rel_c


_CACHE = {}


def kernel(**inp):
    f32 = lambda a: np.ascontiguousarray(np.asarray(a, dtype=np.float32))
    if "nc" not in _CACHE:
        taps = os.environ.get("KTAPS")
        _CACHE["nc"] = build_program(debug_taps=taps.split(",") if taps else None)
    nc, dbg_outs = _CACHE["nc"]
    x_prompt = f32(inp["x_prompt"])
    x_sample = f32(inp["x_sample"])
    ident, rmat, bones = _consts()
    g = lambda k: f32(inp[k])
    gains = np.zeros((128, DEPTH * GCOLS), np.float32)
    for l in range(DEPTH):
        o = l * GCOLS
        gains[:, o + 0:o + 8] = g("g_mix")[l].reshape(8, 128).T
        gains[:, o + 8:o + 16] = g("g_ffn")[l].reshape(8, 128).T
        gains[:, o + 16:o + 18] = g("g_qa")[l].reshape(2, 128).T
        gains[:, o + 18] = g("g_kva")[l]
        gains[0:96, o + 19] = g("g_mla_q")[l]
        gains[0:96, o + 20] = g("g_mla_k")[l]
        gains[:, o + 21] = np.tile(g("g_na_q")[l], 2)
        gains[:, o + 22] = np.tile(g("g_na_k")[l], 2)
        gains[:, o + 23] = np.tile(g("g_df_q")[l], 4)
        gains[:, o + 24] = np.tile(g("g_df_k")[l], 4)
        gains[:, o + 25] = np.tile(g("g_df_sub")[l], 2)
    gains = np.concatenate([gains, np.zeros((128, 2), np.float32)], axis=1)
    gains[:, DEPTH * GCOLS] = np.tile(np.concatenate([np.ones(32), np.zeros(32)]), 2)
    gains[:, DEPTH * GCOLS + 1] = np.tile(np.concatenate([np.zeros(32), np.ones(32)]), 2)
    lamp = np.stack([np.stack([g("df_lq1")[l], g("df_lk1")[l], g("df_lq2")[l], g("df_lk2")[l]]) for l in range(DEPTH)]).reshape(1, -1)
    rpb = g("na_rpb")
    rpb_ext = np.concatenate([rpb.reshape(DEPTH, 6, -1), np.full((DEPTH, 6, 1), NEGB, np.float32)], axis=-1)
    w_mod = g("w_mod")
    b_mod = g("b_mod")
    shared = {
        "w_in": g("w_in"), "w_uq": g("w_uq"), "w_ukv": g("w_ukv"), "w_out": g("w_out"),
        "w_gate": g("w_gate"), "w_up": g("w_up"), "w_down": g("w_down"),
        "gains": gains, "lamp": np.ascontiguousarray(lamp), "ident": ident, "rmat": rmat, "bones": bones,
    }
    in_maps = []
    for r in range(8):
        b = r // 4
        qr = r % 4
        m = dict(shared)
        m["xin"] = np.ascontiguousarray(np.concatenate([x_prompt[4 * r:4 * r + 4], x_sample[b:b + 1, qr * N:(qr + 1) * N]], axis=0))
        cvec = np.stack([g("c_ctx"), g("c")[b]], axis=-1)
        m["cT"] = np.ascontiguousarray(cvec.reshape(8, 128, 2).transpose(1, 0, 2).reshape(128, 16))
        m["wmod"] = np.ascontiguousarray(w_mod[:, :, qr * 1536:(qr + 1) * 1536])
        m["bmod"] = np.ascontiguousarray(b_mod[:, None, qr * 1536:(qr + 1) * 1536])
        m["rope"] = _rope_tables(qr * N)
        mask, rel_r, rel_c = _bias_index(qr)
        flat = np.where(mask, rel_r * 31 + rel_c, 15 * 31)
        m["biasm"] = np.ascontiguousarray(rpb_ext[:, :, flat])
        m["c_ckv"] = np.ascontiguousarray(g("cache_mla_ckv")[b])
        m["c_kr"] = np.ascontiguousarray(g("cache_mla_krope")[b])
        m["c_nak"] = np.ascontiguousarray(g("cache_na_k")[b])
        m["c_nav"] = np.ascontiguousarray(g("cache_na_v")[b])
        m["c_dfk"] = np.ascontiguousarray(g("cache_df_k")[b])
        m["c_dfv"] = np.ascontiguousarray(g("cache_df_v")[b])
        in_maps.append(m)
    res = run_bass_kernel_spmd(nc, in_maps, core_ids=list(range(8)))
    R = res.results
    _CACHE["last"] = R
    y_prompt = np.concatenate([np.asarray(R[r]["y"])[0:4] for r in range(8)], axis=0)
    y_sample = np.stack([np.concatenate([np.asarray(R[4 * b + q]["y"])[4] for q in range(4)], axis=0) for b in range(2)], axis=0)
    cat = lambda k: np.concatenate([np.asarray(R[r][k]) for r in range(8)], axis=0)
    outs = (y_prompt, y_sample, cat("o_ckv"), cat("o_kr"), cat("o_nak"), cat("o_nav"), cat("o_dfk"), cat("o_dfv"))
    return tuple(np.ascontiguousarray(o, dtype=np.float32) for o in outs)
```

```python
import math
import os
import numpy as np
import ml_dtypes
import concourse.bass as bass
import concourse.mybir as mybir
from concourse.bass_utils import run_bass_kernel_spmd

F32 = mybir.dt.float32
BF16 = mybir.dt.bfloat16
ALU = mybir.AluOpType
AF = mybir.ActivationFunctionType

D = 1024
DEPTH = 2
NGRP = 5
N = 256
NTOK = NGRP * N
EPS = 1e-6
GRID_W = 64
IN_COLS = 2336
DFF = 2816
NFF = DFF // 128
EXR = 11 * 128 + 1024
NEGB = -30000.0
LAM_INIT = [0.8 - 0.6 * math.exp(-0.3 * l) for l in range(DEPTH)]
FFN_BLOCKS = [(0, 4), (4, 4), (8, 4), (12, 4), (16, 4), (20, 2)]
NSLOT = 8
C_CQ0, C_CQ1, C_CKV, C_KR = 0, 128, 256, 384
C_NAQ, C_NAK, C_NAV = 416, 800, 1184
C_DFQ, C_DFK, C_DFV = 1568, 1824, 2080
WIN_SLOTS = [[(0, 416)], [(416, 512)], [(928, 256), (2080, 256)], [(1184, 512)], [(1696, 384)]]
GCOLS = 26


class KB:
    def __init__(self, nc):
        self.nc = nc
        self.ops = {e: [] for e in ("pe", "act", "dve", "pool", "sp")}
        self.cnt = {e: 0 for e in ("pe", "act", "dve", "pool")}
        self.engsem = {}
        self.dsem = {}
        self.dcnt = {}
        self.lastw = {}
        self.readers = {}
        self.waited = {e: {} for e in self.ops}
        self.sem_objs = []
        self.final = {}
        self.barrier_ev = None
        self.enabled = True

    def barrier(self, fn):
        if not self.enabled:
            return
        need = {}
        for e_, c in self.cnt.items():
            if c > 0:
                need["E:" + e_] = c
        for s_, v in self.dcnt.items():
            if s_.startswith("D:ring"):
                continue
            need[s_] = v
        waits = []
        for s_, v in need.items():
            if self.waited["dve"].get(s_, 0) < v:
                self.waited["dve"][s_] = v
                waits.append((s_, v))
        self.cnt["dve"] += 1
        self.barrier_ev = ("E:dve", self.cnt["dve"])
        self.ops["dve"].append((waits, fn, "E:dve", 1))

    def _deps(self, eng, reads, writes):
        need = {}
        if self.barrier_ev is not None:
            need[self.barrier_ev[0]] = self.barrier_ev[1]
        for k in reads:
            ev = self.lastw.get(k)
            if ev is not None:
                need[ev[0]] = max(need.get(ev[0], 0), ev[1])
        for k in writes:
            ev = self.lastw.get(k)
            if ev is not None:
                need[ev[0]] = max(need.get(ev[0], 0), ev[1])
            for ev in self.readers.get(k, ()):
                need[ev[0]] = max(need.get(ev[0], 0), ev[1])
        waits = []
        for s, v in need.items():
            if eng == "pe" and s == "E:pe":
                continue
            if self.waited[eng].get(s, 0) < v:
                self.waited[eng][s] = v
                waits.append((s, v))
        return waits

    def _commit(self, ev, reads, writes):
        for k in reads:
            self.readers.setdefault(k, []).append(ev)
        for k in writes:
            self.lastw[k] = ev
            self.readers[k] = []

    def op(self, eng, fn, reads=(), writes=()):
        if not self.enabled:
            return
        waits = self._deps(eng, reads, writes)
        self.cnt[eng] += 1
        ev = ("E:" + eng, self.cnt[eng])
        self.ops[eng].append((waits, fn, ev[0], 1))
        self._commit(ev, reads, writes)

    def dma(self, q, semname, out, in_, reads=(), writes=(), final=False):
        if not self.enabled:
            return
        waits = self._deps(q, reads, writes)
        s = "D:" + semname
        self.dcnt[s] = self.dcnt.get(s, 0) + 16
        ev = (s, self.dcnt[s])
        self.ops[q].append((waits, (lambda e, o=out, i=in_: e.dma_start(out=o, in_=i)), s, 16))
        self._commit(ev, reads, writes)
        if final:
            self.final[s] = self.dcnt[s]

    def coll(self, semname, fn, reads=(), writes=()):
        if not self.enabled:
            return
        waits = self._deps("pool", reads, writes)
        s = "C:" + semname
        self.dcnt[s] = self.dcnt.get(s, 0) + 1
        ev = (s, self.dcnt[s])
        self.ops["pool"].append((waits, fn, s, 1))
        self._commit(ev, reads, writes)

    def all_sems(self):
        names = set()
        for e, lst in self.ops.items():
            for waits, fn, s, amt in lst:
                names.add(s)
                for (ws, v) in waits:
                    names.add(ws)
        return sorted(names)


class _Stop(Exception):
    pass


def build_program(debug_taps=None):
    nc = bass.Bass("TRN2", target_bir_lowering=False)
    kb = KB(nc)
    KSTOP = int(os.environ.get("KSTOP", "1000"))
    KSKIP = os.environ.get("KSKIP", "").split(",")
    KSUB = int(os.environ.get("KSUB", "1000"))

    def sub(i, g):
        if g == 0 and i > KSUB:
            kb.enabled = False
    stg = {"i": 0}

    marks = []

    def stage(name=""):
        marks.append((name, kb.cnt["pe"], kb.cnt["act"], kb.cnt["dve"]))
        stg["i"] += 1
        if stg["i"] > KSTOP:
            kb.enabled = False

    def din(name, shape, dt=F32):
        return nc.dram_tensor(name, list(shape), dt, kind="ExternalInput").ap()

    def dout(name, shape, dt=F32):
        return nc.dram_tensor(name, list(shape), dt, kind="ExternalOutput").ap()

    xin = din("xin", [NGRP, N, D])
    cT_d = din("cT", [128, 16])
    wmod_d = din("wmod", [DEPTH, D, 1536])
    bmod_d = din("bmod", [DEPTH, 1, 1536])
    w_in_d = din("w_in", [DEPTH, D, IN_COLS])
    w_uq_d = din("w_uq", [DEPTH, 256, 576])
    w_ukv_d = din("w_ukv", [DEPTH, 128, 768])
    w_out_d = din("w_out", [DEPTH, D, D])
    w_gate_d = din("w_gate", [DEPTH, D, DFF])
    w_up_d = din("w_up", [DEPTH, D, DFF])
    w_down_d = din("w_down", [DEPTH, DFF, D])
    gains_d = din("gains", [128, DEPTH * GCOLS + 2])
    lamp_d = din("lamp", [1, DEPTH * 4 * 32])
    ident_d = din("ident", [128, 128])
    rmat_d = din("rmat", [128, 2 * 128])
    rope_d = din("rope", [128, 4 * N])
    bones_d = din("bones", [128, 3 * 128])
    biasm_d = din("biasm", [DEPTH, 6, 1024, N])
    c_ckv_d = din("c_ckv", [DEPTH, 256, 128])
    c_kr_d = din("c_kr", [DEPTH, 256, 32])
    c_nak_d = din("c_nak", [DEPTH, 6, 256, 64])
    c_nav_d = din("c_nav", [DEPTH, 6, 256, 64])
    c_dfk_d = din("c_dfk", [DEPTH, 4, 256, 64])
    c_dfv_d = din("c_dfv", [DEPTH, 4, 256, 64])

    y_d = dout("y", [NGRP, N, D])
    o_ckv = dout("o_ckv", [4, DEPTH, 256, 128])
    o_kr = dout("o_kr", [4, DEPTH, 256, 32])
    o_nak = dout("o_nak", [4, DEPTH, 6, 256, 64])
    o_nav = dout("o_nav", [4, DEPTH, 6, 256, 64])
    o_dfk = dout("o_dfk", [4, DEPTH, 4, 256, 64])
    o_dfv = dout("o_dfv", [4, DEPTH, 4, 256, 64])

    mx_in = nc.dram_tensor("mx_in", [128, 48], F32)
    mx_out = nc.dram_tensor("mx_out", [512, 48], F32)
    exk_in = [nc.dram_tensor(f"exk_in{l}", [1408, N], BF16) for l in range(DEPTH)]
    exk_out = [nc.dram_tensor(f"exk_out{l}", [4 * 1408, N], BF16) for l in range(DEPTH)]
    exv_in = [nc.dram_tensor(f"exv_in{l}", [1024, N], BF16) for l in range(DEPTH)]
    exv_out = [nc.dram_tensor(f"exv_out{l}", [4 * 1024, N], BF16) for l in range(DEPTH)]
    RG = [[0, 1, 2, 3], [4, 5, 6, 7]]

    dbg_outs = []

    from contextlib import ExitStack
    es = ExitStack()

    def sb(name, shape, dt=F32):
        return es.enter_context(nc.sbuf_tensor("s_" + name, list(shape), dt))

    with es:
        xT = sb("xT", [128, 8, NTOK])
        ring = [sb(f"ring{i}", [128, 4096], BF16) for i in range(NSLOT)]
        ident = sb("ident", [128, 128])
        rmat = sb("rmat", [128, 256])
        ropet = sb("ropet", [128, 4 * N])
        bones = sb("bones", [128, 3 * 128], BF16)
        ones_b = sb("ones_b", [128, 128], BF16)
        ones_f = sb("ones_f", [128, 2])
        gains = sb("gains", [128, DEPTH * GCOLS + 2])
        lamt = sb("lamt", [128, 16])
        nlam = sb("nlam", [128, DEPTH])
        cT = sb("cT", [128, 16])
        scT = sb("scT", [128, 16])
        modS = sb("modS", [128, 48])
        modT = sb("modT", [128, 4, 48])
        mv = sb("mv", [128, DEPTH, 2, 8, 8])
        epsb = sb("epsb", [128, 1])
        UB = 64 * 1024
        U = sb("U", [128, UB // 2], BF16)
        carve = {"o": 0}

        def uview(shape, dt):
            n = 1
            for d_ in shape[1:]:
                n *= d_
            nb = n * (4 if dt == F32 else 2)
            o = carve["o"]
            assert o % 4 == 0 and o + nb <= UB, (o, nb)
            carve["o"] = o + nb
            v = U[:, o // 2:(o + nb) // 2]
            if dt == F32:
                v = v.bitcast(F32)
            if len(shape) == 3:
                v = v.rearrange("p (a b) -> p a b", a=shape[1])
            elif len(shape) == 4:
                v = v.rearrange("p (a b c) -> p a b c", a=shape[1], b=shape[2])
            return v
        hT = uview([128, 8, N], BF16)
        mixT = hT
        QT = uview([128, 11, N], BF16)
        KT = uview([128, 11, N], BF16)
        VP = uview([128, 16, 192], BF16)
        KTc = uview([128, 11, N], BF16)
        VPc = uview([128, 16, 192], BF16)
        KTst = [uview([128, 1024], BF16) for i in range(2)]
        QTm = uview([128, 4, N], BF16)
        VPst = [uview([128, 8, 192], BF16) for i in range(2)]
        VP4 = VP.rearrange("p b (s c) -> p b s c", s=3)
        VPc4 = VPc.rearrange("p b (s c) -> p b s c", s=3)
        VPst4 = [v_.rearrange("p b (s c) -> p b s c", s=3) for v_ in VPst]
        Ebuf = [uview([128, 8, N], BF16) for i in range(2)]
        Est = [uview([128, 2, N], F32) for i in range(2)]
        cst2 = uview([128, 2, 128], F32)
        _o = carve["o"]
        vstage = uview([128, 640], F32)
        kstage = uview([128, 7, 128], F32)
        _o2 = carve["o"]
        carve["o"] = _o
        cst = uview([128, 2, 128], F32)
        cstk = uview([128, 2, 6, 64], F32)
        cstd = uview([128, 2, 4, 64], F32)
        carve["o"] = max(_o2, carve["o"])
        mixer_bytes = carve["o"]
        carve["o"] = 0
        h2T = uview([128, 8, NTOK], BF16)
        aT = [uview([128, 4, 2 * N], BF16) for i in range(2)]
        silt = [uview([128, 2 * N], F32) for i in range(2)]
        carve["o"] = 0
        xstage = [uview([128, 2, D], F32)]
        wst = [uview([128, 1536], F32) for i in range(2)]
        mrow = uview([128, 1536], F32)
        bmrow = uview([128, 1536], F32)
        lamp = uview([128, DEPTH * 4 * 32], F32)
        NT = 3
        tsq = [sb(f"tsq{i}", [128, N], BF16) for i in range(NT)]
        tf = [sb(f"tf{i}", [128, N]) for i in range(NT)]
        tr = [sb(f"tr{i}", [128, N]) for i in range(NT)]
        tg = [sb(f"tg{i}", [128, N]) for i in range(NT)]
        rstd_x = sb("rstd_x", [128, N])
        ckvn_f = sb("ckvn_f", [128, N])
        ckvn_b = sb("ckvn_b", [128, N], BF16)
        krT = sb("krT", [128, N])
        cqn = sb("cqn", [128, 2, N], BF16)
        knf = sb("knf", [128, 5, N])
        odf = sb("odf", [128, N])
        PT = [sb(f"PT{i}", [128, 512], BF16) for i in range(3)]
        rc = [sb(f"rc{i}", [128, N]) for i in range(2)]
        kstage2 = sb("kstage2", [128, 7, 128])
        ps = [es.enter_context(nc.psum_tensor(f"ps{i}", [128, 512], F32)) for i in range(8)]

        cnt = {"t": 0, "S": 0, "O": 0, "A": 0, "B": 0, "pt": 0, "rc": 0, "ring": 0}
        POOLS = {"S": (0, 1), "O": (2, 3), "A": (4, 5), "B": (6, 7)}

        def bank(pool):
            b = POOLS[pool][cnt[pool] % 2]
            cnt[pool] += 1
            return b

        def tmp(lst, nm):
            i = cnt["t"] % NT
            return lst[i], (nm, i)

        def nexttmp():
            cnt["t"] += 1

        def mm(out, lhsT, rhs, start, stop, reads, writes):
            kb.op("pe", lambda e: e.matmul(out, lhsT=lhsT, rhs=rhs, start=start, stop=stop), reads, writes)

        def tp(out, in_, idn, reads, writes):
            kb.op("pe", lambda e: e.transpose(out, in_, idn), reads, writes)

        def act(out, in_, func, reads, writes, scale=1.0, bias=None, accum=None):
            def f(e):
                kw = {}
                if bias is not None:
                    kw["bias"] = bias
                if accum is not None:
                    kw["accum_out"] = accum
                return e.activation(out=out, in_=in_, func=func, scale=scale, **kw)
            kb.op("act", f, reads, writes)

        def stt(out, in0, scalar, in1, op0, op1, reads, writes, eng="dve", accum=None):
            def f(e):
                if accum is not None:
                    return e.scalar_tensor_tensor(out=out, in0=in0, scalar=scalar, in1=in1, op0=op0, op1=op1, accum_out=accum)
                return e.scalar_tensor_tensor(out=out, in0=in0, scalar=scalar, in1=in1, op0=op0, op1=op1)
            kb.op(eng, f, reads, writes)

        def tt(out, in0, in1, op, reads, writes, eng="dve"):
            kb.op(eng, lambda e: e.tensor_tensor(out=out, in0=in0, in1=in1, op=op), reads, writes)

        def ts(out, in0, s1, s2, op0, op1, reads, writes, eng="dve"):
            if op1 is None:
                kb.op(eng, lambda e: e.tensor_scalar(out=out, in0=in0, scalar1=s1, scalar2=None, op0=op0), reads, writes)
            else:
                kb.op(eng, lambda e: e.tensor_scalar(out=out, in0=in0, scalar1=s1, scalar2=s2, op0=op0, op1=op1), reads, writes)

        def cp(out, in_, reads, writes, eng="dve"):
            kb.op(eng, lambda e: e.tensor_copy(out=out, in_=in_), reads, writes)

        def recip(out, in_, reads, writes):
            kb.op("dve", lambda e: e.reciprocal(out=out, in_=in_), reads, writes)

        def memset(ap, val, writes, eng="dve"):
            kb.op(eng, lambda e: e.memset(ap, val), (), writes)

        def dbg(name, ap, shape, reads, dt=F32):
            if debug_taps is None or name not in debug_taps:
                return
            t = nc.dram_tensor("dbg_" + name, list(shape), dt, kind="ExternalOutput").ap()
            kb.dma("sp", "dbg_" + name, t, ap, reads=reads, writes=[("dbgout", name)], final=True)
            dbg_outs.append(name)

        kb.dma("sp", "c_ident", ident[:], ident_d, writes=["ident"])
        kb.dma("sp", "c_gains", gains[:], gains_d, writes=["gains"])
        kb.dma("sp", "c_cT", cT[:], cT_d, writes=["cT"])
        kb.dma("sp", "c_rmat", rmat[:], rmat_d, writes=["rmat"])
        kb.dma("sp", "c_rope", ropet[:], rope_d, writes=["ropet"])
        kb.dma("pool", "c_bones", bones[:], bones_d, writes=["bones"])
        kb.dma("sp", "c_lamp", lamp[:], lamp_d[0].partition_broadcast(128), writes=["lamp"])
        memset(ones_b[:], 1.0, ["ones_b"])
        memset(ones_f[:], 1.0, ["ones_f"])
        memset(epsb[:], EPS, ["epsb"])
        memset(krT[:], 0.0, ["krT"])
        dummy = sb("dummy", [128, 2])

        def barrier():
            kb.barrier(lambda e: e.memset(dummy[:], 0.0))

        def mixer_init():
            memset(VP[:, :, 64:128], 1.0, ["VPones"])
            memset(VPc[:, :, 64:128], 1.0, ["VPcones"])
            for i in range(2):
                memset(VPst[i][:, :, 64:128], 1.0, [("VPstones", i)])
            memset(cst2[:], 0.0, ["cst2"])
            memset(KT[64:128, 0:6, :], 0.0, [("KT", c) for c in range(6)])

        BO2 = bones[:, 0:128]
        BO4 = bones[:, 128:256]
        BO96 = bones[:, 256:384]

        def gcol(l, j, w=1):
            return gains[:, l * GCOLS + j: l * GCOLS + j + w]
        G_MIX, G_FFN, G_QA, G_KVA, G_MQ, G_MK, G_NQ, G_NK, G_DQ, G_DK, G_DS = 0, 8, 16, 18, 19, 20, 21, 22, 23, 24, 25

        ring_last = [0] * NSLOT
        prog = {"i": 0}

        def ring_free():
            return sum(1 for v in ring_last if v < 10 ** 6)

        def ring_alloc():
            i = min(range(NSLOT), key=lambda s: ring_last[s])
            assert ring_last[i] < 10 ** 6, "weight ring exhausted"
            prog["i"] += 1
            ring_last[i] = prog["i"] + 10 ** 6
            return i

        def ring_touch(i):
            prog["i"] += 1
            ring_last[i] = prog["i"]

        def load_w(slot, pieces):
            for (dst, src) in pieces:
                kb.dma("pool", f"ring{slot}", dst, src, writes=[("ring", slot)])

        class WS:
            pass

        def mixer_loader(l):
            w = WS()
            w.win = []
            w.wout = []
            w.colmap = {}
            steps = []

            def st_small():
                s = ring_alloc()
                w.small = s
                r = ring[s]
                load_w(s, [(r[:, 0:1152].rearrange("p (k c) -> p k c", k=2), w_uq_d[l].rearrange("(k p) c -> p k c", p=128)),
                           (r[:, 1152:1536].rearrange("p (h c) -> p h c", h=6), w_ukv_d[l].rearrange("p (h c) -> p h c", h=6)[:, :, 0:64]),
                           (r[:, 1536:1920].rearrange("p (h c) -> p h c", h=6), w_ukv_d[l].rearrange("p (h c) -> p h c", h=6)[:, :, 64:128])])
            steps.append(st_small)
            wv = w_in_d[l].rearrange("(k p) c -> p k c", p=128)

            def mk_win(pieces):
                def f():
                    s = ring_alloc()
                    w.win.append(s)
                    tot = sum(nc_ for (_, nc_) in pieces)
                    view = ring[s][:, 0:8 * tot].rearrange("p (k c) -> p k c", k=8)
                    off = 0
                    pl = []
                    for (c0, ncol) in pieces:
                        pl.append((view[:, :, off:off + ncol], wv[:, :, c0:c0 + ncol]))
                        w.colmap[c0] = (s, view, off, ncol)
                        off += ncol
                    load_w(s, pl)
                return f
            for pieces in WIN_SLOTS:
                steps.append(mk_win(pieces))
            wo = w_out_d[l].rearrange("(k p) c -> p k c", p=128)

            def mk_wout(j):
                def f():
                    s = ring_alloc()
                    w.wout.append(s)
                    view = ring[s][:, :].rearrange("p (k c) -> p k c", k=8)
                    load_w(s, [(view, wo[:, :, j * 512:(j + 1) * 512])])
                return f
            for j in range(2):
                steps.append(mk_wout(j))
            return w, steps

        def win_ap(w, col, width, k):
            for c0, (s, view, off, ncol) in w.colmap.items():
                if c0 <= col and col + width <= c0 + ncol:
                    return view[:, k, off + col - c0: off + col - c0 + width], ("ring", s), s
            raise KeyError(col)

        def load_ffn_block(l, bi):
            c0, ncnk = FFN_BLOCKS[bi]
            w = WS()
            w.n = ncnk
            cols = ncnk * 128
            w.g = ring_alloc()
            vg = ring[w.g][:, 0:8 * cols].rearrange("p (k c) -> p k c", k=8)
            load_w(w.g, [(vg, w_gate_d[l].rearrange("(k p) c -> p k c", p=128)[:, :, c0 * 128:c0 * 128 + cols])])
            w.u = ring_alloc()
            vu = ring[w.u][:, 0:8 * cols].rearrange("p (k c) -> p k c", k=8)
            load_w(w.u, [(vu, w_up_d[l].rearrange("(k p) c -> p k c", p=128)[:, :, c0 * 128:c0 * 128 + cols])])
            w.d = ring_alloc()
            vd = ring[w.d][:, 0:ncnk * 1024].rearrange("p (c f) -> p c f", c=ncnk)
            load_w(w.d, [(vd, w_down_d[l][c0 * 128:c0 * 128 + cols, :].rearrange("(c p) f -> p c f", p=128))])
            w.vg, w.vu, w.vd = vg, vu, vd
            return w

        mixw, _steps = mixer_loader(0)
        for _f in _steps:
            _f()

        stage("lambda")
        for l in range(DEPTH):
            for j in range(2):
                a = lamp[:, (l * 4 + 2 * j) * 32:(l * 4 + 2 * j + 1) * 32]
                b = lamp[:, (l * 4 + 2 * j + 1) * 32:(l * 4 + 2 * j + 2) * 32]
                stt(tf[0][:, 0:32], a, 1.0, b, ALU.mult, ALU.mult,
                    ["lamp"], [("tf", 0), ("lamt", l, j)], accum=lamt[:, l * 2 + j:l * 2 + j + 1])
            act(lamt[:, 4 + l * 2:4 + l * 2 + 2], lamt[:, l * 2:l * 2 + 2], AF.Exp, [("lamt", l, 0), ("lamt", l, 1)], [("lame", l)])
            tt(lamt[:, 8 + l:9 + l], lamt[:, 5 + l * 2:6 + l * 2], lamt[:, 4 + l * 2:5 + l * 2], ALU.subtract, [("lame", l)], [("lamd", l)])
            ts(nlam[:, l:l + 1], lamt[:, 8 + l:9 + l], -LAM_INIT[l], None, ALU.add, None, [("lamd", l)], [("nlam", l)])

        stage("xT")
        def emit_xT(g):
            st_ = 0
            kb.dma("sp", f"xstage{st_}", xstage[st_][:], xin[g].rearrange("(t p) d -> p t d", p=128), writes=[("xstage", st_)])
            for k in range(8):
                b = bank("O")
                for t in range(2):
                    tp(ps[b][:, t * 128:(t + 1) * 128], xstage[st_][:, t, k * 128:(k + 1) * 128], ident[:],
                       [("xstage", st_), "ident"], [("ps", b)])
                if k % 2:
                    cp(xT[:, k, g * N:(g + 1) * N], ps[b][:, 0:N], [("ps", b)], [("xT", g, k)])
                else:
                    act(xT[:, k, g * N:(g + 1) * N], ps[b][:, 0:N], AF.Copy, [("ps", b)], [("xT", g, k)])

        stage("mod")
        act(scT[:], cT[:], AF.Silu, ["cT"], ["scT"])
        XT_AT = {(0, 0): 0, (0, 3): 1, (0, 6): 2, (1, 1): 3, (1, 4): 4}
        for l in range(DEPTH):
            banks = [bank("A"), bank("B"), bank("S")]
            kb.dma("sp", "c_bm", bmrow[0:1, :], bmod_d[l], writes=["bmrow"])
            for k in range(8):
                if (l, k) in XT_AT:
                    emit_xT(XT_AT[(l, k)])
                wsl = (l * 8 + k) % 2
                kb.dma("sp", f"wst{wsl}", wst[wsl][:], wmod_d[l, k * 128:(k + 1) * 128, :], writes=[("wst", wsl)])
                for j in range(3):
                    mm(ps[banks[j]][0:2, :], scT[:, 2 * k:2 * k + 2], wst[wsl][:, j * 512:(j + 1) * 512], k == 0, False,
                       ["scT", ("wst", wsl)], [("ps", banks[j])])
            for j in range(3):
                mm(ps[banks[j]][0:2, :], ones_f[0:1, 0:2], bmrow[0:1, j * 512:(j + 1) * 512], False, True,
                   ["ones_f", "bmrow"], [("ps", banks[j])])
                cp(mrow[0:2, j * 512:(j + 1) * 512], ps[banks[j]][0:2, :], [("ps", banks[j])], [("mrow", j)])
            bt = bank("B")
            for jb in range(12):
                tp(ps[bt][:, jb * 2:jb * 2 + 2], mrow[0:2, jb * 128:(jb + 1) * 128], ident[0:2, 0:2],
                   [("mrow", jb // 4), "ident"], [("ps", bt)])
            cp(modS[:, l * 24:(l + 1) * 24], ps[bt][:, 0:24], [("ps", bt)], ["modS"])
        stage("modgather")
        kb.dma("sp", "mx", mx_in.ap(), modS[:], reads=["modS"], writes=["mx_in"])
        kb.coll("mx", lambda e: e.collective_compute("AllGather", ALU.bypass, replica_groups=RG,
                                                     ins=[mx_in.ap().opt()], outs=[mx_out.ap().opt()]),
                reads=["mx_in"], writes=["mx_out"])
        kb.dma("sp", "mxb", modT[:], mx_out.ap().rearrange("(r p) c -> p r c", p=128), reads=["mx_out"], writes=["modT"])
        for l in range(DEPTH):
            for cnd in range(2):
                for r in range(4):
                    cp(mv[:, l, cnd, 0:6, :].rearrange("p a b -> p (a b)")[:, r * 12:(r + 1) * 12],
                       modT[:, r, l * 24 + cnd:l * 24 + 24:2], ["modT"], [("mvraw", l, cnd)])
                stt(mv[:, l, cnd, 6, :], mv[:, l, cnd, 1, :], 1.0, gcol(l, G_MIX, 8), ALU.add, ALU.mult,
                    [("mvraw", l, cnd), "gains"], [("mv", l, cnd)])
                stt(mv[:, l, cnd, 7, :], mv[:, l, cnd, 4, :], 1.0, gcol(l, G_FFN, 8), ALU.add, ALU.mult,
                    [("mvraw", l, cnd), "gains"], [("mv", l, cnd)])

        def MV(l, cnd, kind, k):
            return mv[:, l, cnd, kind, k:k + 1]
        K_SH1, K_GATE1, K_SH2, K_GATE2, K_G1, K_G2 = 0, 2, 3, 5, 6, 7

        def xkeys(g):
            return [("xT", g, k) for k in range(8)]

        def norm_mod(l, g, kG, kSH, dst, dst_key):
            cnd = 1 if g == 4 else 0
            T = slice(g * N, (g + 1) * N)
            b = bank("B")
            for k in range(8):
                sq_, ksq = tmp(tsq, "tsq")
                act(sq_[:], xT[:, k, T], AF.Square, [("xT", g, k)], [ksq])
                mm(ps[b][:, 0:N], ones_b[:], sq_[:], k == 0, k == 7, [ksq, "ones_b"], [("ps", b)])
                nexttmp()
            t1, k1 = tmp(tf, "tf")
            act(t1[:], ps[b][:, 0:N], AF.Ln, [("ps", b), "epsb"], [k1], scale=1.0 / D, bias=epsb[:, 0:1])
            act(rstd_x[:], t1[:], AF.Exp, [k1], ["rstd_x"], scale=-0.5)
            nexttmp()
            for k in range(8):
                t2, k2 = tmp(tg, "tg")
                stt(t2[:], xT[:, k, T], MV(l, cnd, kG, k), rstd_x[:], ALU.mult, ALU.mult,
                    [("xT", g, k), ("mv", l, cnd), "rstd_x"], [k2])
                act(dst(k), t2[:], AF.Identity, [k2, ("mvraw", l, cnd)], [dst_key(k)], bias=MV(l, cnd, kSH, k))
                nexttmp()

        def headnorm(pb, M, d, bo, gain, outs, extra_reads=(), src=None, src_key=None):
            srcap = ps[pb][0:M, 0:N] if src is None else src
            skey = ("ps", pb) if src_key is None else src_key
            sq_, ksq = tmp(tsq, "tsq")
            act(sq_[0:M, :], srcap, AF.Square, [skey], [ksq])
            b2 = bank("B")
            mm(ps[b2][0:M, 0:N], bo[0:M, 0:M], sq_[0:M, :], True, True, [ksq, "bones"], [("ps", b2)])
            t1, k1 = tmp(tf, "tf")
            act(t1[0:M, :], ps[b2][0:M, 0:N], AF.Ln, [("ps", b2), "epsb"], [k1], scale=1.0 / d, bias=epsb[0:M, 0:1])
            r_, kr_ = tmp(tr, "tr")
            act(r_[0:M, :], t1[0:M, :], AF.Exp, [k1], [kr_], scale=-0.5)
            for (oap, okey) in outs:
                stt(oap, srcap, gain, r_[0:M, :], ALU.mult, ALU.mult, [skey, kr_, "gains"] + list(extra_reads), [okey])
            nexttmp()

        def rope(src_f, M, which, dst, reads, writes):
            ro = 0 if which == "mla" else 2
            R = rmat[0:M, 0:M] if which == "mla" else rmat[0:M, 128:128 + M]
            b = bank("B")
            mm(ps[b][0:M, 0:N], R, src_f, True, True, list(reads) + ["rmat"], [("ps", b)])
            t1, k1 = tmp(tf, "tf")
            tt(t1[0:M, :], ps[b][0:M, 0:N], ropet[0:M, (ro + 1) * N:(ro + 2) * N], ALU.mult, [("ps", b), "ropet"], [k1])
            t2, k2 = tmp(tg, "tg")
            tt(t2[0:M, :], src_f, ropet[0:M, ro * N:(ro + 1) * N], ALU.mult, list(reads) + ["ropet"], [k2])
            tt(dst, t1[0:M, :], t2[0:M, :], ALU.add, [k1, k2], writes)
            nexttmp()

        def mla_k_from(l, w, ckvT_b, ckv_key, kr_f, kr_key, sample_rope, dstKT, dst_key):
            rsm = ring[w.small]
            for h in range(6):
                ba = bank("A")
                mm(ps[ba][0:64, 0:N], rsm[:, 1152 + h * 64:1152 + h * 64 + 64], ckvT_b, True, True,
                   [("ring", w.small), ckv_key], [("ps", ba)])
                ring_touch(w.small)
                sq_, ksq = tmp(tsq, "tsq")
                act(sq_[0:64, :], ps[ba][0:64, 0:N], AF.Square, [("ps", ba)], [ksq])
                act(sq_[64:96, :], kr_f[64:96, :], AF.Square, [kr_key], [ksq])
                b2 = bank("B")
                mm(ps[b2][0:96, 0:N], BO96[0:96, 0:96], sq_[0:96, :], True, True, [ksq, "bones"], [("ps", b2)])
                t1, k1 = tmp(tf, "tf")
                act(t1[0:96, :], ps[b2][0:96, 0:N], AF.Ln, [("ps", b2), "epsb"], [k1], scale=1.0 / 96, bias=epsb[0:96, 0:1])
                r_, kr_ = tmp(tr, "tr")
                act(r_[0:96, :], t1[0:96, :], AF.Exp, [k1], [kr_], scale=-0.5)
                if not sample_rope:
                    stt(dstKT(h)[0:64, :], ps[ba][0:64, 0:N], gcol(l, G_MK)[0:64, :], r_[0:64, :], ALU.mult, ALU.mult,
                        [("ps", ba), kr_, "gains"], [dst_key(h)])
                    stt(dstKT(h)[64:96, :], kr_f[64:96, :], gcol(l, G_MK)[64:96, :], r_[64:96, :], ALU.mult, ALU.mult,
                        [kr_key, kr_, "gains"], [dst_key(h)])
                    nexttmp()
                else:
                    nexttmp()
                    kf, kfk = knf[:, 4, :], ("knf", 4)
                    stt(kf[0:64, :], ps[ba][0:64, 0:N], gcol(l, G_MK)[0:64, :], r_[0:64, :], ALU.mult, ALU.mult,
                        [("ps", ba), kr_, "gains"], [kfk])
                    stt(kf[64:96, :], kr_f[64:96, :], gcol(l, G_MK)[64:96, :], r_[64:96, :], ALU.mult, ALU.mult,
                        [kr_key, kr_, "gains"], [kfk])
                    rope(kf[0:96, :], 96, "mla", dstKT(h)[0:96, :], [kfk], [dst_key(h)])

        tsq2 = sb("tsqx", [128, N], BF16)
        P4 = (4, 5, 0, 1)
        pcnt = {"p": 0, "j": 0}

        def bank4():
            b_ = P4[pcnt["p"] % 4]
            pcnt["p"] += 1
            return b_

        def run_norm_pipeline(jobs):
            n = len(jobs)
            for i in range(min(2, n)):
                jobs[i]["P"]()
                jobs[i]["Q"]()
            for i in range(n):
                jobs[i]["R"]()
                jobs[i]["T"]()
                if i + 2 < n:
                    jobs[i + 2]["P"]()
                    jobs[i + 2]["Q"]()
                jobs[i]["U"]()

        def mk_job(P, M, d, bo, U, sq_extra=None, nop_norm=False, R_custom=None):
            st = {}
            j = pcnt["j"]
            pcnt["j"] += 1
            ti = j % NT
            sq_, ksq = tsq[ti], ("tsq", ti)
            t1, k1 = tf[ti], ("tf", ti)
            r_, kr_ = tr[ti], ("tr", ti)
            st.update(sq=sq_, ksq=ksq, r=r_, kr=kr_, ti=ti)

            def P_():
                P(st)

            def Q_():
                if nop_norm:
                    return
                pb = st["pb"]
                rows = st.get("rows", M)
                act(sq_[0:rows, :], ps[pb][0:rows, 0:N], AF.Square, [("ps", pb)], [ksq])
                if sq_extra is not None:
                    sq_extra(st)

            def R_():
                if nop_norm:
                    return
                if R_custom is not None:
                    R_custom(st)
                    return
                b2 = bank("B")
                st["b2"] = b2
                mm(ps[b2][0:M, 0:N], bo[0:M, 0:M], sq_[0:M, :], True, True, [ksq, "bones", "ones_b"], [("ps", b2)])

            def T_():
                if nop_norm:
                    return
                b2 = st["b2"]
                act(t1[0:M, :], ps[b2][0:M, 0:N], AF.Ln, [("ps", b2), "epsb"], [k1], scale=1.0 / d, bias=epsb[0:M, 0:1])
                act(r_[0:M, :], t1[0:M, :], AF.Exp, [k1], [kr_], scale=-0.5)

            def U_():
                U(st)
            return {"P": P_, "Q": Q_, "R": R_, "T": T_, "U": U_}

        def mk_mla_k_job(l, w, h, ckvT_b, ckv_key, kr_f, kr_key, sample_rope, dstKT, dst_key):
            rsm_ = ring[w.small]

            def P(st):
                pb = bank4()
                st["pb"] = pb
                st["rows"] = 64
                mm(ps[pb][0:64, 0:N], rsm_[:, 1152 + h * 64:1152 + h * 64 + 64], ckvT_b, True, True,
                   [("ring", w.small), ckv_key], [("ps", pb)])
                ring_touch(w.small)

            def sqx(st):
                act(st["sq"][64:96, :], kr_f[64:96, :], AF.Square, [kr_key], [st["ksq"]])

            def U(st):
                pb, r_, kr_ = st["pb"], st["r"], st["kr"]
                if not sample_rope:
                    d0, dk = dstKT(h), dst_key(h)
                else:
                    d0, dk = knf[:, 4, :], ("knf", 4)
                stt(d0[0:64, :], ps[pb][0:64, 0:N], gcol(l, G_MK)[0:64, :], r_[0:64, :], ALU.mult, ALU.mult,
                    [("ps", pb), kr_, "gains"], [dk])
                stt(d0[64:96, :], kr_f[64:96, :], gcol(l, G_MK)[64:96, :], r_[64:96, :], ALU.mult, ALU.mult,
                    [kr_key, kr_, "gains"], [dk])
                if sample_rope:
                    rope(d0[0:96, :], 96, "mla", dstKT(h)[0:96, :], [dk], [dst_key(h)])
            return mk_job(P, 96, 96, BO96, U, sq_extra=sqx)

        def front(l, g, w, part="all"):
            do_q = part in ("all", "q")
            do_kv = part in ("all", "kv")
            sample = (g == 4)
            norm_mod(l, g, K_G1, K_SH1, lambda k: hT[:, k, :], lambda k: ("hT", k))
            rsm = ring[w.small]

            def proj(col, M, pb, po=0):
                for k in range(8):
                    wap, wkey, s_ = win_ap(w, col, M, k)
                    mm(ps[pb][po:po + M, 0:N], wap, hT[:, k, :], k == 0, k == 7, [wkey, ("hT", k)], [("ps", pb)])
                    ring_touch(s_)

            def chunk_job(col, M, d, bo, gain, outs, post=None):
                def P(st):
                    st["pb"] = bank4()
                    proj(col, M, st["pb"])

                def U(st):
                    pb, r_, kr_ = st["pb"], st["r"], st["kr"]
                    for (oap, okey) in outs:
                        stt(oap, ps[pb][0:M, 0:N], gain, r_[0:M, :], ALU.mult, ALU.mult, [("ps", pb), kr_, "gains"], [okey])
                    if post is not None:
                        post()
                return mk_job(P, M, d, bo, U)

            jobs = []
            if do_q:
                def P_cq(st):
                    st["pb"] = bank4()
                    st["pb1"] = bank4()
                    proj(C_CQ0, 128, st["pb"])
                    proj(C_CQ1, 128, st["pb1"])

                def sq_cq(st):
                    act(tsq2[:], ps[st["pb1"]][:, 0:N], AF.Square, [("ps", st["pb1"])], ["tsq2"])

                def U_cq(st):
                    r_, kr_ = st["r"], st["kr"]
                    stt(cqn[:, 0, :], ps[st["pb"]][:, 0:N], gcol(l, G_QA), r_[:], ALU.mult, ALU.mult, [("ps", st["pb"]), kr_, "gains"], [("cqn", 0)])
                    stt(cqn[:, 1, :], ps[st["pb1"]][:, 0:N], gcol(l, G_QA + 1), r_[:], ALU.mult, ALU.mult, [("ps", st["pb1"]), kr_, "gains"], [("cqn", 1)])
                def R_cq(st):
                    b2 = bank("B")
                    st["b2"] = b2
                    mm(ps[b2][:, 0:N], ones_b[:], st["sq"][:], True, False, [st["ksq"], "ones_b"], [("ps", b2)])
                    mm(ps[b2][:, 0:N], ones_b[:], tsq2[:], False, True, ["tsq2", "ones_b"], [("ps", b2)])
                jobs.append(mk_job(P_cq, 128, 256, ones_b, U_cq, sq_extra=sq_cq, R_custom=R_cq))
            if do_kv:
                jobs.append(chunk_job(C_CKV, 128, 128, ones_b, gcol(l, G_KVA), [(ckvn_f[:], "ckvn_f"), (ckvn_b[:], "ckvn_b")]))

                def P_kr(st):
                    st["pb"] = bank4()
                    proj(C_KR, 32, st["pb"], po=64)

                def U_kr(st):
                    cp(krT[64:96, :], ps[st["pb"]][64:96, 0:N], [("ps", st["pb"])], ["krT"])
                jobs.append(mk_job(P_kr, 32, 32, ones_b, U_kr, nop_norm=True))
            if do_q:
                for c in range(3):
                    jobs.append(chunk_job(C_NAQ + c * 128, 128, 64, BO2, gcol(l, G_NQ), [(QT[:, 6 + c, :], ("QT", 6 + c))]))
                uq = rsm[:, 0:1152].rearrange("p (k c) -> p k c", k=2)
                for h in range(6):
                    def P_mq(st, h=h):
                        st["pb"] = bank4()
                        for k2 in range(2):
                            mm(ps[st["pb"]][0:96, 0:N], uq[:, k2, h * 96:(h + 1) * 96], cqn[:, k2, :], k2 == 0, k2 == 1,
                               [("ring", w.small), ("cqn", k2)], [("ps", st["pb"])])
                        ring_touch(w.small)

                    def U_mq(st, h=h):
                        pb, r_, kr_ = st["pb"], st["r"], st["kr"]
                        if not sample:
                            stt(QT[0:96, h, :], ps[pb][0:96, 0:N], gcol(l, G_MQ)[0:96, :], r_[0:96, :], ALU.mult, ALU.mult,
                                [("ps", pb), kr_, "gains"], [("QT", h)])
                        else:
                            stt(knf[0:96, 3, :], ps[pb][0:96, 0:N], gcol(l, G_MQ)[0:96, :], r_[0:96, :], ALU.mult, ALU.mult,
                                [("ps", pb), kr_, "gains"], [("knf", 3)])
                            rope(knf[0:96, 3, :], 96, "mla", QT[0:96, h, :], [("knf", 3)], [("QT", h)])
                    jobs.append(mk_job(P_mq, 96, 96, BO96, U_mq))
            if do_kv:
                for c in range(3):
                    outs = [(KT[:, 6 + c, :], ("KT", 6 + c))]
                    if not sample:
                        outs.append((knf[:, c, :], ("knf", c)))
                    jobs.append(chunk_job(C_NAK + c * 128, 128, 64, BO2, gcol(l, G_NK), outs))
                for h in range(6):
                    jobs.append(mk_mla_k_job(l, w, h, ckvn_b[:], "ckvn_b", krT, "krT", sample,
                                             lambda h_: KT[:, h_, :], lambda h_: ("KT", h_)))
            if do_q:
                for c in range(2):
                    def post_q(c=c):
                        if sample:
                            rope(knf[:, 3, :], 128, "df", QT[:, 9 + c, :], [("knf", 3)], [("QT", 9 + c)])
                        for m_ in range(2):
                            ts(QTm[:, 2 * c + m_, :], QT[:, 9 + c, :], gains[:, DEPTH * GCOLS + m_:DEPTH * GCOLS + m_ + 1], None, ALU.mult, None,
                               [("QT", 9 + c), "gains"], [("QTm", 2 * c + m_)])
                    outs = [(QT[:, 9 + c, :], ("QT", 9 + c))] if not sample else [(knf[:, 3, :], ("knf", 3))]
                    jobs.append(chunk_job(C_DFQ + c * 128, 128, 32, BO4, gcol(l, G_DQ), outs, post=post_q))
            if do_kv:
                for c in range(2):
                    if not sample:
                        jobs.append(chunk_job(C_DFK + c * 128, 128, 32, BO4, gcol(l, G_DK),
                                              [(KT[:, 9 + c, :], ("KT", 9 + c)), (knf[:, 3 + c, :], ("knf", 3 + c))]))
                    else:
                        def post_k(c=c):
                            rope(knf[:, 3, :], 128, "df", KT[:, 9 + c, :], [("knf", 3)], [("KT", 9 + c)])
                        jobs.append(chunk_job(C_DFK + c * 128, 128, 32, BO4, gcol(l, G_DK), [(knf[:, 3, :], ("knf", 3))], post=post_k))
            run_norm_pipeline(jobs)
            if do_kv:
                vsrc = rsm[:, 1536:1920]
                for t in range(2):
                    bv = bank("A")
                    mm(ps[bv][:, 0:384], ckvn_b[:, t * 128:(t + 1) * 128], vsrc, True, True,
                       [("ring", w.small), "ckvn_b"], [("ps", bv)])
                    ring_touch(w.small)
                    cp(VP4[:, t * 8:t * 8 + 3, 0:3:2, :], ps[bv][:, 0:384].rearrange("p (q a c) -> p q a c", q=3, a=2),
                       [("ps", bv)], [("VP", t, 0)])
            sub(8, g)
            for t in range(2 if do_kv else 0):
                b1_, b2_ = bank("A"), bank("A")
                for k in range(8):
                    wap, wkey, s = win_ap(w, C_NAV, 384, k)
                    mm(ps[b1_][:, 0:384], hT[:, k, t * 128:(t + 1) * 128], wap, k == 0, k == 7, [wkey, ("hT", k)], [("ps", b1_)])
                    ring_touch(s)
                for k in range(8):
                    wap, wkey, s = win_ap(w, C_DFV, 256, k)
                    mm(ps[b2_][:, 0:256], hT[:, k, t * 128:(t + 1) * 128], wap, k == 0, k == 7, [wkey, ("hT", k)], [("ps", b2_)])
                    ring_touch(s)
                if not sample:
                    act(vstage[:, 0:384], ps[b1_][:, 0:384], AF.Copy, [("ps", b1_)], [("vstage", 0)])
                    act(vstage[:, 384:640], ps[b2_][:, 0:256], AF.Copy, [("ps", b2_)], [("vstage", 1)])
                if sample:
                    cp(VP4[:, t * 8 + 3:t * 8 + 6, 0:3:2, :], ps[b1_][:, 0:384].rearrange("p (q a c) -> p q a c", q=3, a=2),
                       [("ps", b1_)], [("VP", t, 1)])
                    cp(VP4[:, t * 8 + 6:t * 8 + 8, 0:3:2, :], ps[b2_][:, 0:256].rearrange("p (q a c) -> p q a c", q=2, a=2),
                       [("ps", b2_)], [("VP", t, 2)])
                else:
                    cp(VP4[:, t * 8 + 3:t * 8 + 6, 0:3:2, :], vstage[:, 0:384].rearrange("p (q a c) -> p q a c", q=3, a=2),
                       [("vstage", 0)], [("VP", t, 1)])
                    cp(VP4[:, t * 8 + 6:t * 8 + 8, 0:3:2, :], vstage[:, 384:640].rearrange("p (q a c) -> p q a c", q=2, a=2),
                       [("vstage", 1)], [("VP", t, 2)])
                if not sample and "vo" not in KSKIP:
                    rv = [("vstage", 0), ("vstage", 1)]
                    kb.dma("sp", "o_nav", o_nav[g, l][:, t * 128:(t + 1) * 128, :].rearrange("h p c -> p h c"),
                           vstage[:, 0:384].rearrange("p (h c) -> p h c", h=6), reads=rv, writes=[("out", "nav", g, l, t)], final=True)
                    kb.dma("sp", "o_dfv", o_dfv[g, l][:, t * 128:(t + 1) * 128, :].rearrange("h p c -> p h c"),
                           vstage[:, 384:640].rearrange("p (h c) -> p h c", h=4), reads=rv, writes=[("out", "dfv", g, l, t)], final=True)

        def prompt_outputs(l, g):
            srcs = [(ckvn_f, None, "ckvn_f"), (krT, None, "krT")] + [(knf, c, ("knf", c)) for c in range(5)]
            for t in range(2):
                kst_ = kstage if t == 0 else kstage2
                for half in range(2):
                    b = bank("B")
                    lst = srcs[0:4] if half == 0 else srcs[4:7]
                    for j, (tile_, c, key) in enumerate(lst):
                        src = tile_[:, t * 128:(t + 1) * 128] if c is None else tile_[:, c, t * 128:(t + 1) * 128]
                        tp(ps[b][:, j * 128:(j + 1) * 128], src, ident[:], [key, "ident"], [("ps", b)])
                    n = len(lst)
                    j0 = 0 if half == 0 else 4
                    if half == 0:
                        act(kst_[:, j0:j0 + n, :], ps[b][:, 0:n * 128].rearrange("p (j c) -> p j c", j=n), AF.Copy,
                            [("ps", b)], [("kstage", t, half)])
                    else:
                        cp(kst_[:, j0:j0 + n, :], ps[b][:, 0:n * 128].rearrange("p (j c) -> p j c", j=n),
                           [("ps", b)], [("kstage", t, half)])
                rk = [("kstage", t, 0), ("kstage", t, 1)]
                kb.dma("sp", "o_ckv", o_ckv[g, l][t * 128:(t + 1) * 128, :], kst_[:, 0, :], reads=rk, writes=[("out", "ckv", g, l, t)], final=True)
                kb.dma("sp", "o_kr", o_kr[g, l][t * 128:(t + 1) * 128, :], kst_[:, 1, 64:96], reads=rk, writes=[("out", "kr", g, l, t)], final=True)
                kb.dma("sp", "o_nak", o_nak[g, l][:, t * 128:(t + 1) * 128, :].rearrange("h p c -> p h c"),
                       kst_[:, 2:5, :].rearrange("p j (a c) -> p (j a) c", a=2), reads=rk, writes=[("out", "nak", g, l, t)], final=True)
                kb.dma("sp", "o_dfk", o_dfk[g, l][:, t * 128:(t + 1) * 128, :].rearrange("h p c -> p h c"),
                       kst_[:, 5:7, :].rearrange("p j (a c) -> p (j a) c", a=2), reads=rk, writes=[("out", "dfk", g, l, t)], final=True)

        def head_maps(hh):
            if hh < 6:
                return [(hh, 0, 96, 96 ** -0.5, None)]
            if hh < 12:
                j = hh - 6
                return [(6 + j // 2, (j % 2) * 64, 64, 0.125, None)]
            j = hh - 12
            return [(9 + j // 2, (j % 2) * 64, 64, 32 ** -0.5, 2 * (j // 2) + m) for m in range(2)]

        def vblock(tile_, base, c, hh, nh):
            blk = c * 8 + hh // 2 if nh == 16 else c
            if hh % 2 == 0:
                return tile_[:, blk, 0:128]
            return tile_[:, blk, 64:192]

        def attn_norm(hh, ob, dst, dkey, rdkeys):
            i = cnt["rc"] % 2
            cnt["rc"] += 1
            lo, hi = (0, 64) if hh % 2 == 0 else (64, 128)
            dl, dh = (64, 128) if hh % 2 == 0 else (0, 64)
            act(rc[i][lo:hi, :], ps[ob][dl:dh, 0:N], AF.Ln, [("ps", ob)], [("rc", i)])
            act(rc[i][lo:hi, :], rc[i][lo:hi, :], AF.Exp, [("rc", i)], [("rc", i)], scale=-1.0)
            tt(dst[lo:hi, :], ps[ob][lo:hi, 0:N], rc[i][lo:hi, :], ALU.mult, [("ps", ob), ("rc", i)] + rdkeys, [dkey])

        def df_finish(l, pi):
            sq_, ksq = tmp(tsq, "tsq")
            act(sq_[:], odf[:], AF.Square, ["odf0", "odf1"], [ksq])
            b2 = bank("B")
            mm(ps[b2][:, 0:N], BO2, sq_[:], True, True, [ksq, "bones"], [("ps", b2)])
            t1, k1 = tmp(tf, "tf")
            act(t1[:], ps[b2][:, 0:N], AF.Ln, [("ps", b2), "epsb"], [k1], scale=1.0 / 64, bias=epsb[:, 0:1])
            r_, kr_ = tmp(tr, "tr")
            act(r_[:], t1[:], AF.Exp, [k1], [kr_], scale=-0.5)
            t2, k2 = tmp(tg, "tg")
            stt(t2[:], odf[:], gcol(l, G_DS), r_[:], ALU.mult, ALU.mult, ["odf0", "odf1", kr_, "gains"], [k2])
            ts(mixT[:, pi, :], t2[:], 1.0 - LAM_INIT[l], None, ALU.mult, None, [k2], [("hT", pi)])
            nexttmp()

        def run_pipeline(jobs):
            n = len(jobs)
            if not n:
                return
            jobs[0][0]()
            if n > 1:
                jobs[1][0]()
            jobs[0][1]()
            for i in range(n):
                jobs[i][2]()
                if i + 2 < n:
                    jobs[i + 2][0]()
                if i + 1 < n:
                    jobs[i + 1][1]()
                jobs[i][3]()

        def attention_prompt(l, g):
            jobs = []
            for hh in range(16):
                pi = hh // 2
                maps = head_maps(hh)
                obs = []
                for mi, (qc, pb, kr, scale, qm) in enumerate(maps):
                    st = {}

                    def S_(st=st, qc=qc, pb=pb, kr=kr, qm=qm):
                        sbk = bank("S")
                        st["sbk"] = sbk
                        qap = QT[pb:pb + kr, qc, :] if qm is None else QTm[pb:pb + kr, qm, :]
                        qkey = ("QT", qc) if qm is None else ("QTm", qm)
                        for kc in range(2):
                            mm(ps[sbk][:, kc * N:(kc + 1) * N], KT[pb:pb + kr, qc, kc * 128:(kc + 1) * 128], qap,
                               True, True, [("KT", qc), qkey], [("ps", sbk)])

                    def E_(st=st, scale=scale):
                        pi_ = cnt["pt"] % 3
                        cnt["pt"] += 1
                        st["pt"] = pi_
                        act(PT[pi_][:], ps[st["sbk"]][:, :], AF.Exp, [("ps", st["sbk"])], [("PT", pi_)], scale=scale)

                    def PV_(st=st, hh=hh, obs=obs):
                        ob = bank("O")
                        pi_ = st["pt"]
                        grp = 0 if hh < 6 else (1 if hh < 12 else 2)
                        for kc in range(2):
                            mm(ps[ob][:, 0:N], vblock(VP, 1, kc, hh, 16), PT[pi_][:, kc * N:(kc + 1) * N], kc == 0, kc == 1,
                               [("VP", kc, grp), "VPones", ("PT", pi_)], [("ps", ob)])
                        obs.append(ob)

                    def POST_(hh=hh, pi=pi, obs=obs, last=(mi == len(maps) - 1)):
                        if not last:
                            return
                        if hh < 12:
                            attn_norm(hh, obs[0], mixT[:, pi, :], ("hT", pi), [])
                        else:
                            df_combine(l, hh, obs)
                            if hh % 2 == 1:
                                df_finish(l, pi)
                    jobs.append((S_, E_, PV_, POST_))
            run_pipeline(jobs)

        def df_combine(l, hh, obs):
            lo = 0 if hh % 2 == 0 else 64
            t1, k1 = tmp(tf, "tf")
            attn_norm(hh, obs[0], t1, k1, [])
            t2, k2 = tmp(tg, "tg")
            attn_norm(hh, obs[1], t2, k2, [])
            stt(odf[lo:lo + 64, :], t2[lo:lo + 64, :], nlam[lo:lo + 64, l:l + 1], t1[lo:lo + 64, :], ALU.mult, ALU.add,
                [k1, k2, ("nlam", l)], ["odf%d" % (hh % 2)])
            nexttmp()

        def out_proj(l, g, w):
            cnd = 1 if g == 4 else 0
            T = slice(g * N, (g + 1) * N)
            for f in range(8):
                ba = bank("A")
                s = w.wout[f // 4]
                view = ring[s][:, :].rearrange("p (k c) -> p k c", k=8)
                for k in range(8):
                    mm(ps[ba][:, 0:N], view[:, k, (f % 4) * 128:(f % 4 + 1) * 128], mixT[:, k, :], k == 0, k == 7,
                       [("ring", s), ("hT", k)], [("ps", ba)])
                ring_touch(s)
                stt(xT[:, f, T], ps[ba][:, 0:N], MV(l, cnd, K_GATE1, f), xT[:, f, T], ALU.mult, ALU.add,
                    [("ps", ba), ("mvraw", l, cnd), ("xT", g, f)], [("xT", g, f)])

        def sample_ctx(l, w):
            rsm = ring[w.small]
            kb.dma("sp", "cst", cst[:], c_ckv_d[l].rearrange("(t p) c -> p t c", p=128), writes=["cst"])
            kb.dma("sp", "cst2", cst2[:, :, 64:96], c_kr_d[l].rearrange("(t p) c -> p t c", p=128), writes=["cst2"])
            for t in range(2):
                kb.dma("sp", "cstk", cstk[:, t], c_nak_d[l][:, t * 128:(t + 1) * 128, :].rearrange("h p c -> p h c"), writes=["cstk"])
                kb.dma("sp", "cstd", cstd[:, t], c_dfk_d[l][:, t * 128:(t + 1) * 128, :].rearrange("h p c -> p h c"), writes=["cstd"])
            for t in range(2):
                for a_ in range(2):
                    kb.dma("pool", f"vpc_na{t}{a_}", VPc4[:, t * 8 + 3:t * 8 + 6, 2 * a_, :],
                           c_nav_d[l][a_:6:2, t * 128:(t + 1) * 128, :].rearrange("h p c -> p h c"), writes=[("VPc", t, 1, a_)])
                    kb.dma("pool", f"vpc_df{t}{a_}", VPc4[:, t * 8 + 6:t * 8 + 8, 2 * a_, :],
                           c_dfv_d[l][a_:4:2, t * 128:(t + 1) * 128, :].rearrange("h p c -> p h c"), writes=[("VPc", t, 2, a_)])
            b = bank("B")
            for t in range(2):
                tp(ps[b][:, t * 128:(t + 1) * 128], cst[:, t, :], ident[:], ["cst", "ident"], [("ps", b)])
            cp(Ebuf[0][:, 0, :], ps[b][:, 0:N], [("ps", b)], [("Ebuf", 0, 0)])
            ckvc_b = Ebuf[0][:, 0, :]
            b = bank("B")
            for t in range(2):
                tp(ps[b][:, t * 128:(t + 1) * 128], cst2[:, t, :], ident[:], ["cst2", "ident"], [("ps", b)])
            krc = Est[0][:, 0, :]
            cp(krc, ps[b][:, 0:N], [("ps", b)], [("Est", 0)])
            run_norm_pipeline([mk_mla_k_job(l, w, h, ckvc_b, ("Ebuf", 0, 0), krc, ("Est", 0), False,
                                            lambda h_: KTc[:, h_, :], lambda h_: ("KTc", h_)) for h in range(6)])
            vsrc = rsm[:, 1536:1920]
            for t in range(2):
                bv = bank("A")
                mm(ps[bv][:, 0:384], ckvc_b[:, t * 128:(t + 1) * 128], vsrc, True, True,
                   [("ring", w.small), ("Ebuf", 0, 0)], [("ps", bv)])
                ring_touch(w.small)
                cp(VPc4[:, t * 8:t * 8 + 3, 0:3:2, :], ps[bv][:, 0:384].rearrange("p (q a c) -> p q a c", q=3, a=2),
                   [("ps", bv)], [("VPc", t, 0)])
            for c in range(3):
                b = bank("B")
                for t in range(2):
                    tp(ps[b][:, t * 128:(t + 1) * 128], cstk[:, t, 2 * c:2 * c + 2, :].rearrange("p h c -> p (h c)"), ident[:],
                       ["cstk", "ident"], [("ps", b)])
                cp(KTc[:, 6 + c, :], ps[b][:, 0:N], [("ps", b)], [("KTc", 6 + c)])
            for c in range(2):
                b = bank("B")
                for t in range(2):
                    tp(ps[b][:, t * 128:(t + 1) * 128], cstd[:, t, 2 * c:2 * c + 2, :].rearrange("p h c -> p (h c)"), ident[:],
                       ["cstd", "ident"], [("ps", b)])
                cp(KTc[:, 9 + c, :], ps[b][:, 0:N], [("ps", b)], [("KTc", 9 + c)])

        def sample_exchange(l):
            kb.dma("sp", "exk", exk_in[l].ap().rearrange("(c p) t -> p c t", p=128), KT[:],
                   reads=[("KT", c) for c in range(11)], writes=[("exin", l, "k")])
            kb.coll(f"exk{l}", lambda e, l=l: e.collective_compute("AllGather", ALU.bypass, replica_groups=RG,
                                                                  ins=[exk_in[l].ap().opt()], outs=[exk_out[l].ap().opt()]),
                    reads=[("exin", l, "k")], writes=[("exoutk", l)])
            vreg = exv_in[l].ap().rearrange("(tok a) c -> tok (a c)", a=4).rearrange("(t p) f -> p t f", p=128)
            for t in range(2):
                for a_ in range(2):
                    kb.dma("sp", f"exv{t}{a_}", vreg[:, t, :].rearrange("p (q a c) -> p q a c", q=8, a=2)[:, :, a_, :],
                           VP4[:, t * 8:(t + 1) * 8, 2 * a_, :],
                           reads=[("VP", t, j) for j in range(3)], writes=[("exin", l, "v", t, a_)])
            kb.coll(f"exv{l}", lambda e, l=l: e.collective_compute("AllGather", ALU.bypass, replica_groups=RG,
                                                                  ins=[exv_in[l].ap().opt()], outs=[exv_out[l].ap().opt()]),
                    reads=[("exin", l, "v", t, a_) for t in range(2) for a_ in range(2)], writes=[("exoutv", l)])

        def attention_sample(l):
            exo = exk_out[l].ap().rearrange("(r x) t -> r x t", r=4)
            vall = exv_out[l].ap().rearrange("(r x) t -> r (x t)", r=4)
            kst = {"i": 0}
            kslot = {}
            vdone = set()

            def prestage(hh):
                if hh >= 16:
                    return
                pi = hh // 2
                st_ = pi % 2
                kc_ = head_maps(hh)[0][0]
                if kc_ not in kslot:
                    si = kst["i"] % 2
                    kst["i"] += 1
                    kslot[kc_] = si
                    kb.dma("sp", f"ktst{si}", KTst[si][:, :].rearrange("p (r t) -> p r t", r=4),
                           exo[:, kc_ * 128:(kc_ + 1) * 128, :].rearrange("r p t -> p r t"),
                           reads=[("exoutk", l)], writes=[("KTst", si)])
                for pv in (pi,):
                    if pv in vdone or pv >= 8:
                        continue
                    vdone.add(pv)
                    sv = pv % 2
                    for r in range(4):
                        for a_ in range(2):
                            src = vall[r, :].rearrange("(t p f) -> p t f", t=2, p=128)[:, :, pv * 128 + a_ * 64:pv * 128 + a_ * 64 + 64]
                            kb.dma("pool", f"vpst{sv}_{r}{a_}", VPst4[sv][:, r * 2:r * 2 + 2, 2 * a_, :],
                                   src, reads=[("exoutv", l)], writes=[("VPst", sv, r, a_)])
                if 6 <= hh < 12:
                    hn = hh - 6
                    eb = hn % 2
                    for qd in range(4):
                        es_ = qd % 2
                        kb.dma("sp", f"est{es_}", Est[es_][:], biasm_d[l, hn, qd * 256:(qd + 1) * 256, :].rearrange("(c p) q -> p c q", p=128),
                               writes=[("Est", es_)])
                        act(Ebuf[eb][:, qd * 2:(qd + 1) * 2, :], Est[es_][:], AF.Exp, [("Est", es_)], [("Ebuf", eb, qd)])

            jobs = []
            for hh in range(16):
                pi = hh // 2
                st_ = pi % 2
                maps = head_maps(hh)
                obs = []
                isna = 6 <= hh < 12
                eb = (hh - 6) % 2
                for mi, (qc, pb, kr, scale, qm) in enumerate(maps):
                    hst = {}
                    for cp_ in range(5):
                        st = {}

                        def S_(st=st, hst=hst, qc=qc, pb=pb, kr=kr, qm=qm, cp_=cp_, hh=hh, mi=mi):
                            if cp_ == 0 and mi == 0:
                                prestage(hh + 1)
                            si = kslot[qc]
                            qap = QT[pb:pb + kr, qc, :] if qm is None else QTm[pb:pb + kr, qm, :]
                            qkey = ("QT", qc) if qm is None else ("QTm", qm)
                            sbk = bank("S")
                            st["sbk"] = sbk
                            for kk in range(2):
                                c = cp_ * 2 + kk
                                if c < 2:
                                    lhs = KTc[pb:pb + kr, qc, c * 128:(c + 1) * 128]
                                    rk = [("KTc", qc)]
                                else:
                                    lhs = KTst[si][pb:pb + kr, (c - 2) * 128:(c - 1) * 128]
                                    rk = [("KTst", si)]
                                mm(ps[sbk][:, kk * N:(kk + 1) * N], lhs, qap, True, True, rk + [qkey], [("ps", sbk)])

                        def E_(st=st, scale=scale, cp_=cp_, isna=isna, eb=eb):
                            pi_ = cnt["pt"] % 3
                            cnt["pt"] += 1
                            st["pt"] = pi_
                            act(PT[pi_][:], ps[st["sbk"]][:, :], AF.Exp, [("ps", st["sbk"])], [("PT", pi_)], scale=scale)
                            if isna and cp_ >= 1:
                                e0 = (cp_ - 1) * 2
                                tt(PT[pi_][:], PT[pi_][:], Ebuf[eb][:, e0:e0 + 2, :].rearrange("p c q -> p (c q)"), ALU.mult,
                                   [("PT", pi_), ("Ebuf", eb, cp_ - 1)], [("PT", pi_)])

                        def PV_(st=st, hst=hst, hh=hh, cp_=cp_, st_=st_, obs=obs):
                            if cp_ == 0:
                                hst["ob"] = bank("O")
                                obs.append(hst["ob"])
                            ob = hst["ob"]
                            pi_ = st["pt"]
                            grp = 0 if hh < 6 else (1 if hh < 12 else 2)
                            for kk in range(2):
                                c = cp_ * 2 + kk
                                if c < 2:
                                    lhsv = vblock(VPc, 1, c, hh, 16)
                                    rk = ([("VPc", c, 0)] if grp == 0 else [("VPc", c, grp, hh % 2)]) + ["VPcones"]
                                else:
                                    lhsv = vblock(VPst[st_], 1, c - 2, hh, 2)
                                    rk = [("VPst", st_, (c - 2) // 2, hh % 2), ("VPstones", st_)]
                                mm(ps[ob][:, 0:N], lhsv, PT[pi_][:, kk * N:(kk + 1) * N],
                                   c == 0, c == 9, rk + [("PT", pi_)], [("ps", ob)])

                        def POST_(hh=hh, pi=pi, obs=obs, last=(mi == len(maps) - 1 and cp_ == 4)):
                            if not last:
                                return
                            if hh < 12:
                                attn_norm(hh, obs[0], mixT[:, pi, :], ("hT", pi), [])
                            else:
                                df_combine(l, hh, obs)
                                if hh % 2 == 1:
                                    df_finish(l, pi)
                        jobs.append((S_, E_, PV_, POST_))
            prestage(0)
            run_pipeline(jobs)

        def ffn(l, nxt_steps):
            blocks = [None] * len(FFN_BLOCKS)
            blocks[0] = load_ffn_block(l, 0)
            blocks[1] = load_ffn_block(l, 1)
            barrier()
            for g in (0, 1, 2, 3, 4):
                norm_mod(l, g, K_G2, K_SH2, lambda k, g=g: h2T[:, k, g * N:(g + 1) * N], lambda k, g=g: ("h2T", g, k))

            def pump():
                while nxt_steps and ring_free() > 0:
                    nxt_steps.pop(0)()
            pump()
            TB = [(0, 512, 0, (0, 1)), (512, 512, 0, (2, 3)), (1024, 256, 1, (4,))]
            fcnt = {"a": 0, "o": 0}
            OB = (2, 3, 6, 7)
            for bi in range(len(FFN_BLOCKS)):
                w = blocks[bi]
                for (t0_, W, cnd, grps) in TB:
                    T = slice(t0_, t0_ + W)
                    ai = fcnt["a"] % 2
                    fcnt["a"] += 1
                    a_ = aT[ai]
                    akey = ("aT", ai)
                    for c in range(w.n):
                        bg, bu = bank4(), bank4()
                        hk = [("h2T", g_, k) for g_ in grps for k in range(8)]
                        for k in range(8):
                            mm(ps[bg][:, 0:W], w.vg[:, k, c * 128:(c + 1) * 128], h2T[:, k, T], k == 0, k == 7,
                               [("ring", w.g)] + [("h2T", g_, k) for g_ in grps], [("ps", bg)])
                        for k in range(8):
                            mm(ps[bu][:, 0:W], w.vu[:, k, c * 128:(c + 1) * 128], h2T[:, k, T], k == 0, k == 7,
                               [("ring", w.u)] + [("h2T", g_, k) for g_ in grps], [("ps", bu)])
                        si_ = (fcnt["a"] + c) % 2
                        act(silt[si_][:, 0:W], ps[bg][:, 0:W], AF.Silu, [("ps", bg)], [("silt", si_)])
                        tt(a_[:, c, 0:W], ps[bu][:, 0:W], silt[si_][:, 0:W], ALU.mult, [("ps", bu), ("silt", si_)], [akey + (c,)])
                    for f in range(8):
                        b = OB[fcnt["o"] % 4]
                        fcnt["o"] += 1
                        for c in range(w.n):
                            mm(ps[b][:, 0:W], w.vd[:, c, f * 128:(f + 1) * 128], a_[:, c, 0:W], c == 0, c == w.n - 1,
                               [("ring", w.d), akey + (c,)], [("ps", b)])
                        xk = [("xT", g_, f) for g_ in grps]
                        stt(xT[:, f, T], ps[b][:, 0:W], MV(l, cnd, K_GATE2, f), xT[:, f, T], ALU.mult, ALU.add,
                            [("ps", b), ("mvraw", l, cnd)] + xk, xk)
                ring_touch(w.g)
                ring_touch(w.u)
                ring_touch(w.d)
                if bi + 2 < len(FFN_BLOCKS):
                    blocks[bi + 2] = load_ffn_block(l, bi + 2)
                pump()
            assert not nxt_steps

        for l in range(DEPTH):
            w = mixw
            stage("layer")
            barrier()
            mixer_init()
            if "s4" not in KSKIP:
                sample_ctx(l, w)
            stage("front4")
            if "s4" not in KSKIP:
                front(l, 4, w, "kv")
            stage("exch")
            if "s4" not in KSKIP and "ex" not in KSKIP:
                sample_exchange(l)
            for g in range(4):
                stage("frontg")
                front(l, g, w)
                if "po" not in KSKIP:
                    prompt_outputs(l, g)
                stage("attng")
                attention_prompt(l, g)
                out_proj(l, g, w)
            stage("attns")
            front(l, 4, w, "q")
            attention_sample(l)
            if l == 0:
                dbg("mix0", hT[:], [128, 8, N], [("hT", k) for k in range(8)], dt=BF16)
                dbg("qt0", QT[:], [128, 11, N], [("QT", k) for k in range(11)], dt=BF16)
                dbg("ktc0", KTc[:], [128, 11, N], [("KTc", k) for k in range(11)], dt=BF16)
            out_proj(l, 4, w)
            stage("ffn")
            if l + 1 < DEPTH:
                nxt_w, nxt_steps = mixer_loader(l + 1)
            else:
                nxt_w, nxt_steps = None, []
            ffn(l, nxt_steps)
            mixw = nxt_w

        kb.enabled = True
        barrier()
        for g in range(NGRP):
            st_ = 0
            for t in range(2):
                for q4 in range(2):
                    b = bank(("A", "B", "S", "O")[(t * 2 + q4) % 4])
                    for kk in range(4):
                        k = q4 * 4 + kk
                        tp(ps[b][:, kk * 128:(kk + 1) * 128], xT[:, k, g * N + t * 128:g * N + (t + 1) * 128], ident[:],
                           [("xT", g, k), "ident"], [("ps", b)])
                    if q4 == 0:
                        act(xstage[st_][:, t, 0:512], ps[b][:, :], AF.Copy, [("ps", b)], [("xstage", st_)])
                    else:
                        cp(xstage[st_][:, t, 512:1024], ps[b][:, :], [("ps", b)], [("xstage", st_)])
            kb.dma("sp", f"y{st_}", y_d[g].rearrange("(t p) d -> p t d", p=128), xstage[st_][:], reads=[("xstage", st_)],
                   writes=[("out", "y", g)], final=True)

        names = kb.all_sems()
        semd = {}
        for nm in names:
            semd[nm] = es.enter_context(nc.semaphore(nm.replace(":", "_")))
        block = es.enter_context(nc.Block())

        def replay(engname, e, drain=False):
            for (waits, fn, s, amt) in kb.ops[engname]:
                for (ws, v) in waits:
                    e.wait_ge(semd[ws], v)
                fn(e).then_inc(semd[s], amt)
            if drain:
                for s, v in kb.final.items():
                    e.wait_ge(semd[s], v)

        @block.tensor
        def _(e):
            replay("pe", e)

        @block.scalar
        def _(e):
            replay("act", e)

        @block.vector
        def _(e):
            replay("dve", e)

        @block.gpsimd
        def _(e):
            replay("pool", e)

        @block.sync
        def _(e):
            replay("sp", e, drain=True)

    build_program.marks = marks
    return nc, dbg_outs


def _rope_tables(pos0):
    t = np.arange(pos0, pos0 + N)
    row = (t // GRID_W).astype(np.float32)
    col = (t % GRID_W).astype(np.float32)

    def tab(rot_dim):
        n = rot_dim // 4
        inv = (1.0 / (np.float32(10000.0) ** (np.arange(n, dtype=np.float32) * np.float32(2.0) / np.float32(rot_dim // 2)))).astype(np.float32)
        ar = row[:, None] * inv
        ac = col[:, None] * inv
        ang = np.concatenate([ar, ar, ac, ac], axis=-1).astype(np.float32)
        return np.cos(ang).astype(np.float32).T, np.sin(ang).astype(np.float32).T
    cm, sm = tab(32)
    cosm = np.ones((128, N), np.float32)
    sinm = np.zeros((128, N), np.float32)
    cosm[64:96] = cm
    sinm[64:96] = sm
    cd, sd = tab(32)
    cosd = np.tile(cd, (4, 1))
    sind = np.tile(sd, (4, 1))
    return np.concatenate([cosm, sinm, cosd, sind], axis=1)


def _rot_block():
    R = np.zeros((32, 32), np.float32)
    for m in range(8):
        R[8 + m, m] = -1.0
        R[m, 8 + m] = 1.0
        R[24 + m, 16 + m] = -1.0
        R[16 + m, 24 + m] = 1.0
    return R


def _consts():
    ident = np.eye(128, dtype=np.float32)
    Rb = _rot_block()
    Rm = np.zeros((128, 128), np.float32)
    Rm[64:96, 64:96] = Rb
    Rd = np.zeros((128, 128), np.float32)
    for j in range(4):
        Rd[32 * j:32 * j + 32, 32 * j:32 * j + 32] = Rb
    bo2 = np.kron(np.eye(2, dtype=np.float32), np.ones((64, 64), np.float32))
    bo4 = np.kron(np.eye(4, dtype=np.float32), np.ones((32, 32), np.float32))
    bo96 = np.zeros((128, 128), np.float32)
    bo96[0:96, 0:96] = 1.0
    return ident, np.concatenate([Rm, Rd], 1), np.concatenate([bo2, bo4, bo96], 1)


def _bias_index(qrank):
    rows, kr, kw = 16, 8, 16
    keys = np.arange(1024)
    kr_ = keys // GRID_W
    kc_ = keys % GRID_W
    q = np.arange(N) + qrank * N
    qr = q // GRID_W
    qc = q % GRID_W
    rstart = np.clip(qr - kr // 2, 0, rows - kr)
    cstart = np.clip(qc - kw // 2, 0, GRID_W - kw)
    inr = (kr_[:, None] >= rstart[None, :]) & (kr_[:, None] < rstart[None, :] + kr)
    inc = (kc_[:, None] >= cstart[None, :]) & (kc_[:, None] < cstart[None, :] + kw)
    mask = inr & inc
    rel_r = np.clip(kr_[:, None] - qr[None, :] + (kr - 1), 0, 2 * kr - 2)
    rel_c = np.clip(kc_[:, None] - qc[None, :], -(kw - 1), kw - 1) + (kw - 1)
    return mask, rel_r, rel_c


_CACHE = {}


def kernel(**inp):
    f32 = lambda a: np.ascontiguousarray(np.asarray(a, dtype=np.float32))
    if "nc" not in _CACHE:
        taps = os.environ.get("KTAPS")
        _CACHE["nc"] = build_program(debug_taps=taps.split(",") if taps else None)
    nc, dbg_outs = _CACHE["nc"]
    x_prompt = f32(inp["x_prompt"])
    x_sample = f32(inp["x_sample"])
    ident, rmat, bones = _consts()
    g = lambda k: f32(inp[k])
    gains = np.zeros((128, DEPTH * GCOLS), np.float32)
    for l in range(DEPTH):
        o = l * GCOLS
        gains[:, o + 0:o + 8] = g("g_mix")[l].reshape(8, 128).T
        gains[:, o + 8:o + 16] = g("g_ffn")[l].reshape(8, 128).T
        gains[:, o + 16:o + 18] = g("g_qa")[l].reshape(2, 128).T
        gains[:, o + 18] = g("g_kva")[l]
        gains[0:96, o + 19] = g("g_mla_q")[l]
        gains[0:96, o + 20] = g("g_mla_k")[l]
        gains[:, o + 21] = np.tile(g("g_na_q")[l], 2)
        gains[:, o + 22] = np.tile(g("g_na_k")[l], 2)
        gains[:, o + 23] = np.tile(g("g_df_q")[l], 4)
        gains[:, o + 24] = np.tile(g("g_df_k")[l], 4)
        gains[:, o + 25] = np.tile(g("g_df_sub")[l], 2)
    gains = np.concatenate([gains, np.zeros((128, 2), np.float32)], axis=1)
    gains[:, DEPTH * GCOLS] = np.tile(np.concatenate([np.ones(32), np.zeros(32)]), 2)
    gains[:, DEPTH * GCOLS + 1] = np.tile(np.concatenate([np.zeros(32), np.ones(32)]), 2)
    lamp = np.stack([np.stack([g("df_lq1")[l], g("df_lk1")[l], g("df_lq2")[l], g("df_lk2")[l]]) for l in range(DEPTH)]).reshape(1, -1)
    rpb = g("na_rpb")
    rpb_ext = np.concatenate([rpb.reshape(DEPTH, 6, -1), np.full((DEPTH, 6, 1), NEGB, np.float32)], axis=-1)
    w_mod = g("w_mod")
    b_mod = g("b_mod")
    shared = {
        "w_in": g("w_in"), "w_uq": g("w_uq"), "w_ukv": g("w_ukv"), "w_out": g("w_out"),
        "w_gate": g("w_gate"), "w_up": g("w_up"), "w_down": g("w_down"),
        "gains": gains, "lamp": np.ascontiguousarray(lamp), "ident": ident, "rmat": rmat, "bones": bones,
    }
    in_maps = []
    for r in range(8):
        b = r // 4
        qr = r % 4
        m = dict(shared)
        m["xin"] = np.ascontiguousarray(np.concatenate([x_prompt[4 * r:4 * r + 4], x_sample[b:b + 1, qr * N:(qr + 1) * N]], axis=0))
        cvec = np.stack([g("c_ctx"), g("c")[b]], axis=-1)
        m["cT"] = np.ascontiguousarray(cvec.reshape(8, 128, 2).transpose(1, 0, 2).reshape(128, 16))
        m["wmod"] = np.ascontiguousarray(w_mod[:, :, qr * 1536:(qr + 1) * 1536])
        m["bmod"] = np.ascontiguousarray(b_mod[:, None, qr * 1536:(qr + 1) * 1536])
        m["rope"] = _rope_tables(qr * N)
        mask, rel_r, rel_c = _bias_index(qr)
        flat = np.where(mask, rel_r * 31 + rel_c, 15 * 31)
        m["biasm"] = np.ascontiguousarray(rpb_ext[:, :, flat])
        m["c_ckv"] = np.ascontiguousarray(g("cache_mla_ckv")[b])
        m["c_kr"] = np.ascontiguousarray(g("cache_mla_krope")[b])
        m["c_nak"] = np.ascontiguousarray(g("cache_na_k")[b])
        m["c_nav"] = np.ascontiguousarray(g("cache_na_v")[b])
        m["c_dfk"] = np.ascontiguousarray(g("cache_df_k")[b])
        m["c_dfv"] = np.ascontiguousarray(g("cache_df_v")[b])
        in_maps.append(m)
    res = run_bass_kernel_spmd(nc, in_maps, core_ids=list(range(8)))
    R = res.results
    _CACHE["last"] = R
    y_prompt = np.concatenate([np.asarray(R[r]["y"])[0:4] for r in range(8)], axis=0)
    y_sample = np.stack([np.concatenate([np.asarray(R[4 * b + q]["y"])[4] for q in range(4)], axis=0) for b in range(2)], axis=0)
    cat = lambda k: np.concatenate([np.asarray(R[r][k]) for r in range(8)], axis=0)
    outs = (y_prompt, y_sample, cat("o_ckv"), cat("o_kr"), cat("o_nak"), cat("o_nav"), cat("o_dfk"), cat("o_dfv"))
    return tuple(np.ascontiguousarray(o, dtype=np.float32) for o in outs)
```
